# Optimizing a Trainium2 kernel written in Bass

```python
import math
import jax
import jax.numpy as jnp
from jax import lax
import numpy as np

D_MODEL = 1024
BATCH = 4
SEQ = 8192
DEPTH = 2

GDN_HEADS = 4
GDN_DK = 128
GDN_DV = 128
GDN_CONV = 5
GDN_CHUNK = 64
MLA_HEADS = 8
MLA_NOPE = 64
MLA_ROPE = 32
MLA_V = 64
MLA_Q_LORA = 384
MLA_KV_LORA = 256
MLA_QBLOCK = 128
ROPE_THETA = 10000.0
D_FF = 2816
RES_HALF = 0.5
N_BRANCH = 2
EPS = 1e-6

GDN_QK = GDN_HEADS * GDN_DK
GDN_VW = GDN_HEADS * GDN_DV
MLA_QK = MLA_NOPE + MLA_ROPE
MLA_OUT = MLA_HEADS * MLA_V
IN_SPLITS = (GDN_QK, GDN_QK, GDN_VW, GDN_VW, 2 * GDN_HEADS, 2 * GDN_HEADS,
             MLA_Q_LORA, MLA_KV_LORA, MLA_ROPE, N_BRANCH * D_MODEL)
D_IN = sum(IN_SPLITS)

kernel_name = "hybrid_gdn_mla_macaron_encoder"


def rmsnorm(x, w):
    xf = x.astype(jnp.float32)
    y = xf * lax.rsqrt(jnp.mean(xf * xf, axis=-1, keepdims=True) + EPS)
    return (y * w.astype(jnp.float32)).astype(x.dtype)


def l2norm(x):
    xf = x.astype(jnp.float32)
    return (xf * lax.rsqrt(jnp.sum(xf * xf, axis=-1, keepdims=True) + EPS)).astype(x.dtype)


def swiglu(x, w_gate, w_up, w_down):
    return (jax.nn.silu(x @ w_gate) * (x @ w_up)) @ w_down


def split_cols(t):
    parts, start = [], 0
    for width in IN_SPLITS:
        parts.append(t[..., start:start + width])
        start += width
    return parts


def centred_short_conv(x, w):
    pad = GDN_CONV // 2
    y = lax.conv_general_dilated(x, w[:, None, :].astype(x.dtype), (1,), [(pad, pad)],
                                 dimension_numbers=("NWC", "WIO", "NWC"),
                                 feature_group_count=x.shape[-1])
    return jax.nn.silu(y)


def rope(x, cos, sin):
    half = x.shape[-1] // 2
    x1, x2 = x[..., :half], x[..., half:]
    return jnp.concatenate([x1 * cos - x2 * sin, x2 * cos + x1 * sin], axis=-1).astype(x.dtype)


def gated_delta_chunked(q, k, v, g, beta):
    out_dtype = v.dtype
    f32 = jnp.float32
    q, k, v, g, beta = (t.astype(f32) for t in (q, k, v, g, beta))
    B, S, H, Dk = q.shape
    Dv = v.shape[-1]
    C = GDN_CHUNK
    N = S // C

    def to_chunks(t):
        return jnp.moveaxis(t.reshape((B, N, C, H) + t.shape[3:]), (1, 3), (0, 2))

    qc, kc, vc, bc = to_chunks(q), to_chunks(k), to_chunks(v), to_chunks(beta)
    gc = jnp.cumsum(to_chunks(g), axis=-1)
    incl = jnp.tril(jnp.ones((C, C), bool))
    strict = jnp.tril(jnp.ones((C, C), bool), -1)
    decay = jnp.exp(jnp.where(incl, gc[..., :, None] - gc[..., None, :], -jnp.inf))
    kb = kc * bc[..., None]
    lower = jnp.where(strict, jnp.einsum("nbhik,nbhjk->nbhij", kb, kc) * decay, 0.0)
    unit = jnp.eye(C, dtype=f32) + lower
    rhs = jnp.concatenate([vc * bc[..., None], kb * jnp.exp(gc)[..., None]], axis=-1)
    uw = lax.linalg.triangular_solve(unit, rhs, left_side=True, lower=True, unit_diagonal=True)
    u, w = uw[..., :Dv], uw[..., Dv:]
    intra = jnp.where(incl, jnp.einsum("nbhik,nbhjk->nbhij", qc, kc) * decay, 0.0)

    def step(state, xs):
        q_i, k_i, u_i, w_i, g_i, a_i = xs
        v_new = u_i - jnp.einsum("bhck,bhkv->bhcv", w_i, state)
        o_i = (jnp.einsum("bhck,bhkv->bhcv", q_i * jnp.exp(g_i)[..., None], state)
               + jnp.einsum("bhij,bhjv->bhiv", a_i, v_new))
        g_last = g_i[..., -1:]
        state = (state * jnp.exp(g_last)[..., None]
                 + jnp.einsum("bhck,bhcv->bhkv", k_i * jnp.exp(g_last - g_i)[..., None], v_new))
        return state, o_i

    state0 = jnp.zeros((B, H, Dk, Dv), f32)
    _, o = lax.scan(step, state0, (qc, kc, u, w, gc, intra))
    return jnp.moveaxis(o, (0, 2), (1, 3)).reshape(B, S, H, Dv).astype(out_dtype)


def gdn_branch(q, k, v, z, b, a, conv_w, A_log, dt_bias, norm_w, w_proj):
    B, S, _ = q.shape
    f32 = jnp.float32
    qkv = centred_short_conv(jnp.concatenate([q, k, v], axis=-1), conv_w)
    qh = l2norm(qkv[..., :GDN_QK].reshape(B, S, GDN_HEADS, GDN_DK)) * GDN_DK ** -0.5
    kh = l2norm(qkv[..., GDN_QK:2 * GDN_QK].reshape(B, S, GDN_HEADS, GDN_DK))
    vh = qkv[..., 2 * GDN_QK:].reshape(B, S, GDN_HEADS, GDN_DV)
    beta = jax.nn.sigmoid(b.astype(f32)).reshape(B, S, 2, GDN_HEADS)
    g = -jnp.exp(A_log.astype(f32)) * jax.nn.softplus(
        a.astype(f32).reshape(B, S, 2, GDN_HEADS) + dt_bias.astype(f32))
    o_fwd = gated_delta_chunked(qh, kh, vh, g[:, :, 0], beta[:, :, 0])
    flip = lambda t: jnp.flip(t, axis=1)
    o_bwd = flip(gated_delta_chunked(flip(qh), flip(kh), flip(vh), flip(g[:, :, 1]), flip(beta[:, :, 1])))
    o = rmsnorm(o_fwd + o_bwd, norm_w) * jax.nn.silu(z.reshape(B, S, GDN_HEADS, GDN_DV))
    return o.reshape(B, S, GDN_VW) @ w_proj


def mla_branch(c_q, c_kv, k_rope, cos, sin, q_norm, w_uq, kv_norm, w_ukv, w_proj):
    B, S, _ = c_q.shape
    scale = MLA_QK ** -0.5
    q = (rmsnorm(c_q, q_norm) @ w_uq).reshape(B, S, MLA_HEADS, MLA_QK)
    q_nope = q[..., :MLA_NOPE] * scale
    q_rope = rope(q[..., MLA_NOPE:], cos[:, :, None, :], sin[:, :, None, :]) * scale
    kv = (rmsnorm(c_kv, kv_norm) @ w_ukv).reshape(B, S, MLA_HEADS, MLA_NOPE + MLA_V)
    k_nope, v = kv[..., :MLA_NOPE], kv[..., MLA_NOPE:]
    k_r = rope(k_rope, cos, sin)
    nb = S // MLA_QBLOCK

    def blocks(t):
        return jnp.moveaxis(t.reshape((B, nb, MLA_QBLOCK) + t.shape[2:]), 1, 0)

    def attend(qb):
        qn, qr = qb
        s = (jnp.einsum("bqhd,bkhd->bhqk", qn, k_nope)
             + jnp.einsum("bqhr,bkr->bhqk", qr, k_r))
        p = jax.nn.softmax(s.astype(jnp.float32), axis=-1).astype(v.dtype)
        return jnp.einsum("bhqk,bkhd->bqhd", p, v)

    o = lax.map(attend, (blocks(q_nope), blocks(q_rope)))
    o = jnp.moveaxis(o, 0, 1).reshape(B, S, MLA_OUT)
    return o @ w_proj


def setup_inputs(seed: int = 0) -> dict:
    key = jax.random.key(seed)
    ks = jax.random.split(key, 32)
    f32 = jnp.float32

    def dense(k, fan_in, *shape):
        return jax.random.normal(k, (DEPTH,) + shape, f32) * fan_in ** -0.5

    def gain(k, *shape):
        return 1.0 + 0.02 * jax.random.normal(k, shape, f32)

    x = jax.random.normal(ks[0], (BATCH, SEQ, D_MODEL), f32)
    positions = (jax.random.randint(ks[1], (BATCH, 1), 0, 1024, jnp.int32)
                 + jnp.arange(SEQ, dtype=jnp.int32)[None, :])
    gdn_A_log = jnp.log(jax.random.uniform(ks[2], (DEPTH, 2, GDN_HEADS), f32, 1.0, 16.0))
    dt = jnp.exp(jax.random.uniform(ks[3], (DEPTH, 2, GDN_HEADS), f32, math.log(1e-3), math.log(1e-1)))
    gdn_dt_bias = dt + jnp.log(-jnp.expm1(-dt))
    return {
        "x": x,
        "positions": positions,
        "norm_ffn1": gain(ks[4], DEPTH, D_MODEL),
        "ffn1_w_gate": dense(ks[5], D_MODEL, D_MODEL, D_FF),
        "ffn1_w_up": dense(ks[6], D_MODEL, D_MODEL, D_FF),
        "ffn1_w_down": dense(ks[7], D_FF, D_FF, D_MODEL),
        "norm_mix": gain(ks[8], DEPTH, D_MODEL),
        "w_in": dense(ks[9], D_MODEL, D_MODEL, D_IN),
        "gdn_conv": dense(ks[10], GDN_CONV, GDN_CONV, 2 * GDN_QK + GDN_VW),
        "gdn_A_log": gdn_A_log,
        "gdn_dt_bias": gdn_dt_bias,
        "gdn_norm": gain(ks[11], DEPTH, GDN_DV),
        "gdn_proj": dense(ks[12], GDN_VW, GDN_VW, D_MODEL),
        "mla_q_norm": gain(ks[13], DEPTH, MLA_Q_LORA),
        "mla_w_uq": dense(ks[14], MLA_Q_LORA, MLA_Q_LORA, MLA_HEADS * MLA_QK),
        "mla_kv_norm": gain(ks[15], DEPTH, MLA_KV_LORA),
        "mla_w_ukv": dense(ks[16], MLA_KV_LORA, MLA_KV_LORA, MLA_HEADS * (MLA_NOPE + MLA_V)),
        "mla_proj": dense(ks[17], MLA_OUT, MLA_OUT, D_MODEL),
        "w_out": dense(ks[18], D_MODEL, D_MODEL, D_MODEL),
        "norm_ffn2": gain(ks[19], DEPTH, D_MODEL),
        "ffn2_w_gate": dense(ks[20], D_MODEL, D_MODEL, D_FF),
        "ffn2_w_up": dense(ks[21], D_MODEL, D_MODEL, D_FF),
        "ffn2_w_down": dense(ks[22], D_FF, D_FF, D_MODEL),
        "final_norm": gain(ks[23], D_MODEL),
    }


def reference(x, positions, norm_ffn1, ffn1_w_gate, ffn1_w_up, ffn1_w_down, norm_mix, w_in,
              gdn_conv, gdn_A_log, gdn_dt_bias, gdn_norm, gdn_proj, mla_q_norm, mla_w_uq,
              mla_kv_norm, mla_w_ukv, mla_proj, w_out, norm_ffn2, ffn2_w_gate, ffn2_w_up,
              ffn2_w_down, final_norm):
    B, S, D = x.shape
    inv_freq = jnp.power(ROPE_THETA, -jnp.arange(0, MLA_ROPE, 2, dtype=jnp.float32) / MLA_ROPE)
    ang = positions.astype(jnp.float32)[..., None] * inv_freq
    cos, sin = jnp.cos(ang), jnp.sin(ang)
    for l in range(DEPTH):
        h = rmsnorm(x, norm_ffn1[l])
        x = x + RES_HALF * swiglu(h, ffn1_w_gate[l], ffn1_w_up[l], ffn1_w_down[l])
        h = rmsnorm(x, norm_mix[l])
        gq, gk, gv, gz, gb, ga, c_q, c_kv, k_rope, gate_logits = split_cols(h @ w_in[l])
        y_a = gdn_branch(gq, gk, gv, gz, gb, ga, gdn_conv[l], gdn_A_log[l], gdn_dt_bias[l],
                         gdn_norm[l], gdn_proj[l])
        y_b = mla_branch(c_q, c_kv, k_rope, cos, sin, mla_q_norm[l], mla_w_uq[l],
                         mla_kv_norm[l], mla_w_ukv[l], mla_proj[l])
        gates = jax.nn.sigmoid(gate_logits.astype(jnp.float32)).astype(x.dtype).reshape(B, S, N_BRANCH, D)
        x = x + (gates[:, :, 0] * y_a + gates[:, :, 1] * y_b) @ w_out[l]
        h = rmsnorm(x, norm_ffn2[l])
        x = x + RES_HALF * swiglu(h, ffn2_w_gate[l], ffn2_w_up[l], ffn2_w_down[l])
    return rmsnorm(x, final_norm)
```

```python
import numpy as np
import concourse.bass as bass
import concourse.mybir as mybir
from concourse.bass_utils import run_bass_kernel_spmd

F32 = mybir.dt.float32
BF16 = mybir.dt.bfloat16
I32 = mybir.dt.int32
AF = mybir.ActivationFunctionType
ALU = mybir.AluOpType
AX = mybir.AxisListType

D_MODEL = 1024
BATCH = 4
SEQ = 8192
DEPTH = 2
T = SEQ // 2
NBLK = T // 128
D_FF = 2816
NFF = D_FF // 128
D_IN = 4784
EPS = 1e-6
NDS = 40


class Buf:
    __slots__ = ("name", "w", "r", "x")

    def __init__(self, name):
        self.name = name
        self.w = None
        self.r = {}
        self.x = False


class V:
    __slots__ = ("ap", "bufs")

    def __init__(self, ap, bufs):
        self.ap = ap
        self.bufs = bufs


class TT:
    def __init__(self, handle, name):
        self.h = handle
        self.buf = Buf(name)

    def __getitem__(self, idx):
        return V(self.h[idx], (self.buf,))


def dv(ap):
    return V(ap, ())


class KB:
    def __init__(self, nc):
        self.nc = nc
        self.engs = {"pe": nc.tensor, "dve": nc.vector, "act": nc.scalar, "pool": nc.gpsimd, "sp": nc.sync}
        self.esem = {k: nc.alloc_semaphore("e_" + k) for k in self.engs}
        self.ecnt = {k: 0 for k in self.engs}
        self.dsem = [nc.alloc_semaphore("d%d" % i) for i in range(NDS)]
        self.dcnt = [0] * NDS
        self.dnext = 0
        self.seen = {k: {} for k in self.engs}
        self.ninst = 0

    def _sem(self, key):
        return self.esem[key[1]] if key[0] == "e" else self.dsem[key[1]]

    def _wait(self, e, key, val):
        if self.seen[e].get(key, 0) >= val:
            return
        self.seen[e][key] = val
        self.engs[e].wait_ge(self._sem(key), val)
        self.ninst += 1

    def _deps(self, e, reads, writes):
        need = {}
        for v in reads:
            for b in v.bufs:
                if b.w is not None:
                    k, val = b.w
                    if need.get(k, 0) < val:
                        need[k] = val
                if b.x:
                    for k, val in b.r.items():
                        if k != ("e", e) and need.get(k, 0) < val:
                            need[k] = val
        for v in writes:
            for b in v.bufs:
                if b.w is not None:
                    k, val = b.w
                    if need.get(k, 0) < val:
                        need[k] = val
                for k, val in b.r.items():
                    if need.get(k, 0) < val:
                        need[k] = val
        for k, val in need.items():
            if k == ("e", "pe") and e == "pe":
                continue
            self._wait(e, k, val)

    def _mark(self, tok, reads, writes):
        k, val = tok
        for v in reads:
            for b in v.bufs:
                if b.r.get(k, 0) < val:
                    b.r[k] = val
        for v in writes:
            for b in v.bufs:
                b.w = tok
                b.r = {}

    def op(self, e, fn, reads=(), writes=()):
        self._deps(e, reads, writes)
        ins = fn(self.engs[e])
        self.ecnt[e] += 1
        ins.then_inc(self.esem[e], 1)
        self.ninst += 1
        self._mark((("e", e), self.ecnt[e]), reads, writes)
        return ins

    def dma(self, q, out, in_, **kw):
        i = self.dnext
        self.dnext = (i + 1) % NDS
        if self.dcnt[i] > 0:
            self._wait(q, ("d", i), self.dcnt[i])
        self._deps(q, (in_,), (out,))
        ins = self.engs[q].dma_start(out=out.ap, in_=in_.ap, **kw)
        self.dcnt[i] += 16
        ins.then_inc(self.dsem[i], 16)
        self.ninst += 1
        self._mark((("d", i), self.dcnt[i]), (in_,), (out,))
        return ins

    def barrier(self, engines=None):
        for e in (engines or self.engs):
            for k2 in self.engs:
                if k2 != e and self.ecnt[k2] > 0:
                    self._wait(e, ("e", k2), self.ecnt[k2])
            for i in range(NDS):
                if self.dcnt[i] > 0:
                    self._wait(e, ("d", i), self.dcnt[i])

    def mm(self, out, lhsT, rhs, start, stop):
        return self.op("pe", lambda t: t.matmul(out.ap, lhsT=lhsT.ap, rhs=rhs.ap, start=start, stop=stop),
                       reads=(lhsT, rhs), writes=(out,))

    def tr(self, out, in_, ident):
        return self.op("pe", lambda t: t.transpose(out.ap, in_.ap, ident.ap), reads=(in_, ident), writes=(out,))

    def act(self, out, in_, func, bias=None, scale=None, accum=None, e="act"):
        kw = {}
        reads = [in_]
        writes = [out]
        if bias is not None:
            if isinstance(bias, V):
                kw["bias"] = bias.ap
                reads.append(bias)
            else:
                kw["bias"] = bias
        if scale is not None:
            if isinstance(scale, V):
                kw["scale"] = scale.ap
                reads.append(scale)
            else:
                kw["scale"] = scale
        if accum is not None:
            kw["accum_out"] = accum.ap
            writes.append(accum)
        return self.op(e, lambda a: a.activation(out=out.ap, in_=in_.ap, func=func, **kw), reads=reads, writes=writes)

    def tt(self, e, out, in0, in1, op):
        return self.op(e, lambda g: g.tensor_tensor(out=out.ap, in0=in0.ap, in1=in1.ap, op=op),
                       reads=(in0, in1), writes=(out,))

    def ts(self, e, out, in0, s1, op0, s2=None, op1=None):
        reads = [in0]
        a1 = s1
        a2 = s2
        if isinstance(s1, V):
            reads.append(s1)
            a1 = s1.ap
        if isinstance(s2, V):
            reads.append(s2)
            a2 = s2.ap
        kw = {}
        if op1 is not None:
            kw["op1"] = op1
        return self.op(e, lambda g: g.tensor_scalar(out=out.ap, in0=in0.ap, scalar1=a1, scalar2=a2, op0=op0, **kw),
                       reads=reads, writes=(out,))

    def stt(self, e, out, in0, scalar, in1, op0, op1):
        reads = [in0, in1]
        a = scalar
        if isinstance(scalar, V):
            reads.append(scalar)
            a = scalar.ap
        return self.op(e, lambda g: g.scalar_tensor_tensor(out=out.ap, in0=in0.ap, scalar=a, in1=in1.ap, op0=op0, op1=op1),
                       reads=reads, writes=(out,))

    def copy(self, e, out, in_):
        if e == "act":
            return self.op(e, lambda g: g.copy(out=out.ap, in_=in_.ap), reads=(in_,), writes=(out,))
        return self.op(e, lambda g: g.tensor_copy(out=out.ap, in_=in_.ap), reads=(in_,), writes=(out,))

    def memset(self, e, out, val):
        return self.op(e, lambda g: g.memset(out.ap, val), reads=(), writes=(out,))

    def recip(self, out, in_):
        return self.op("dve", lambda g: g.reciprocal(out=out.ap, in_=in_.ap), reads=(in_,), writes=(out,))


class Ctx:
    pass


_uid = [0]


def sb(nc, stack, name, shape, dtype):
    _uid[0] += 1
    name = "%s_%d" % (name, _uid[0])
    h = stack.enter_context(nc.sbuf_tensor(name, shape, dtype))
    return TT(h, name)


def load_cast(K, c, stg, dst_views, src_aps, cast_engs=("dve", "pool")):
    qs = ("sp", "act", "pool")
    for i, (d, s) in enumerate(zip(dst_views, src_aps)):
        st = stg[i % len(stg)]
        shp = list(s.shape)
        sv = st[:shp[0], :shp[1]] if len(shp) == 2 else st[:shp[0], :shp[1], :shp[2]]
        K.dma(qs[i % 3], sv, dv(s))
        K.copy(cast_engs[i % len(cast_engs)], d, sv)


def rms_rstd(K, c, x_v, ss, rstd, junk, n, e_sq="act"):
    K.act(junk, x_v, AF.Square, accum=ss[:, 0:1])
    K.act(rstd[:, 0:1], ss[:, 0:1], AF.Sqrt, bias=c.eps_t[:, 0:1], scale=1.0 / n)
    K.recip(rstd[:, 0:1], rstd[:, 0:1])


def phase_ffn(K, c, stack_outer, x_src, x_dst, w_gate, w_up, w_down, gain, final_gain=None, y_out=None):
    import contextlib
    nc = c.nc
    TTK = 256
    NB = TTK // 128
    with contextlib.ExitStack() as stack:
        wg = sb(nc, stack, "wg", [128, 8, D_FF], BF16)
        wu = sb(nc, stack, "wu", [128, 8, D_FF], BF16)
        wd = sb(nc, stack, "wd", [128, NFF, D_MODEL], BF16)
        grow = sb(nc, stack, "grow", [128, D_MODEL], F32)
        K.dma("sp", grow[:, :], dv(gain.partition_broadcast(128)))
        if final_gain is not None:
            fgrow = sb(nc, stack, "fgrow", [128, D_MODEL], F32)
            K.dma("sp", fgrow[:, :], dv(final_gain.partition_broadcast(128)))
        with contextlib.ExitStack() as st2:
            stg = [sb(nc, st2, "stg%d" % i, [128, D_FF], F32) for i in range(3)]
            dsts, srcs = [], []
            for kc in range(8):
                dsts.append(wg[:, kc, :]); srcs.append(w_gate[kc * 128:(kc + 1) * 128, :])
                dsts.append(wu[:, kc, :]); srcs.append(w_up[kc * 128:(kc + 1) * 128, :])
            for j in range(NFF):
                dsts.append(wd[:, j, :]); srcs.append(w_down[j * 128:(j + 1) * 128, :])
            load_cast(K, c, stg, dsts, srcs)
            K.barrier()
        xt = [sb(nc, stack, "xt%d" % i, [128, NB, D_MODEL], F32) for i in range(2)]
        xn = [sb(nc, stack, "xn%d" % i, [128, D_MODEL], BF16) for i in range(2)]
        junk = sb(nc, stack, "junk", [128, D_MODEL], F32)
        hT = sb(nc, stack, "hT", [128, 8, TTK], BF16)
        actT = sb(nc, stack, "actT", [128, NFF, TTK], BF16)
        sg = [sb(nc, stack, "sg%d" % i, [128, TTK], F32) for i in range(2)]
        ss = [sb(nc, stack, "ss%d" % i, [128, 1], F32) for i in range(2)]
        rstd = [sb(nc, stack, "rstd%d" % i, [128, 1], F32) for i in range(2)]
        ntile = T // TTK
        xs_t = x_src.rearrange("(n b p) d -> n p b d", p=128, b=NB)
        xd_t = x_dst.rearrange("(n b p) d -> n p b d", p=128, b=NB)
        if y_out is not None:
            yo_t = y_out.rearrange("(n b p) d -> n p b d", p=128, b=NB)
        K.dma("sp", xt[0][:, :, :], dv(xs_t[0]))
        for i in range(ntile):
            x = xt[i % 2]
            if i + 1 < ntile:
                K.dma("sp", xt[(i + 1) % 2][:, :, :], dv(xs_t[i + 1]))
            for b in range(NB):
                rms_rstd(K, c, x[:, b, :], ss[b], rstd[b], junk[:, :], D_MODEL)
                K.stt("dve", xn[b][:, :], x[:, b, :], rstd[b][:, 0:1], grow[:, :], ALU.mult, ALU.mult)
                pt = c.psb[b]
                for kc in range(8):
                    K.tr(pt[:, kc * 128:(kc + 1) * 128], xn[b][:, kc * 128:(kc + 1) * 128], c.ident_bf[:, :])
                K.copy("act" if b == 0 else "dve", hT[:, :, b * 128:(b + 1) * 128],
                       V(pt.h[:, :].rearrange("p (k t) -> p k t", k=8), (pt.buf,)))
            for j in range(NFF):
                pg = c.ps[2 + (j % 3)]
                for kc in range(8):
                    K.mm(pg[:, 0:TTK], wg[:, kc, j * 128:(j + 1) * 128], hT[:, kc, :], kc == 0, kc == 7)
                for kc in range(8):
                    K.mm(pg[:, TTK:2 * TTK], wu[:, kc, j * 128:(j + 1) * 128], hT[:, kc, :], kc == 0, kc == 7)
                s = sg[j % 2]
                K.act(s[:, :], pg[:, 0:TTK], AF.Silu)
                K.tt("dve", actT[:, j, :], s[:, :], pg[:, TTK:2 * TTK], ALU.mult)
            for b in range(NB):
                for h in range(2):
                    po = c.ps[5 + ((b * 2 + h) % 3)]
                    for j in range(NFF):
                        K.mm(po[:, :], actT[:, j, b * 128:(b + 1) * 128], wd[:, j, h * 512:(h + 1) * 512], j == 0, j == NFF - 1)
                    K.stt("dve", x[:, b, h * 512:(h + 1) * 512], po[:, :], 0.5, x[:, b, h * 512:(h + 1) * 512], ALU.mult, ALU.add)
            if final_gain is None:
                K.dma("pool", dv(xd_t[i]), x[:, :, :])
            else:
                for b in range(NB):
                    rms_rstd(K, c, x[:, b, :], ss[b], rstd[b], junk[:, :], D_MODEL)
                    K.stt("dve", x[:, b, :], x[:, b, :], rstd[b][:, 0:1], fgrow[:, :], ALU.mult, ALU.mult)
                    K.dma("pool", dv(yo_t[i][:, b, :]), x[:, b, :])
        K.barrier()


MLA_SCALE = 96.0 ** -0.5
TWO_PI = 6.283185307179586
CW1 = 6.28125
CW2 = TWO_PI - CW1
MAGIC = 12582912.0
PI_LIM = 3.1415925


def prologue_rope(K, c, stack, pos_pm, invf):
    import contextlib
    nc = c.nc
    nb = T // 128
    c.cos = sb(nc, stack, "cos", [128, nb, 16], F32)
    c.sin = sb(nc, stack, "sin", [128, nb, 16], F32)
    c.cos_s = sb(nc, stack, "cos_s", [128, nb, 16], F32)
    c.sin_s = sb(nc, stack, "sin_s", [128, nb, 16], F32)
    with contextlib.ExitStack() as st:
        pos_i = sb(nc, st, "pos_i", [128, nb], I32)
        posf = sb(nc, st, "posf", [128, nb], F32)
        ivf = sb(nc, st, "ivf", [128, 16], F32)
        ang = sb(nc, st, "ang", [128, nb, 16], F32)
        u = sb(nc, st, "u", [128, nb, 16], F32)
        n = sb(nc, st, "n", [128, nb, 16], F32)
        r = sb(nc, st, "r", [128, nb, 16], F32)
        K.dma("sp", pos_i[:, :], dv(pos_pm))
        K.dma("sp", ivf[:, :], dv(invf.partition_broadcast(128)))
        K.copy("dve", posf[:, :], pos_i[:, :])
        pf_b = V(posf.h[:, :].unsqueeze(2).to_broadcast([128, nb, 16]), (posf.buf,))
        iv_b = V(ivf.h[:, :].unsqueeze(1).to_broadcast([128, nb, 16]), (ivf.buf,))
        K.tt("dve", ang[:, :, :], pf_b, iv_b, ALU.mult)
        for which, dst, dst_s in (("sin", c.sin, c.sin_s), ("cos", c.cos, c.cos_s)):
            off = 0.0 if which == "sin" else 0.25
            K.ts("dve", u[:, :, :], ang[:, :, :], 1.0 / TWO_PI, ALU.mult, off, ALU.add)
            K.ts("dve", n[:, :, :], u[:, :, :], MAGIC, ALU.add)
            K.ts("dve", n[:, :, :], n[:, :, :], MAGIC, ALU.subtract)
            K.stt("dve", r[:, :, :], n[:, :, :], -CW1, ang[:, :, :], ALU.mult, ALU.add)
            K.stt("dve", r[:, :, :], n[:, :, :], -CW2, r[:, :, :], ALU.mult, ALU.add)
            if which == "cos":
                K.ts("dve", r[:, :, :], r[:, :, :], TWO_PI / 4, ALU.add)
            K.ts("dve", r[:, :, :], r[:, :, :], PI_LIM, ALU.min, -PI_LIM, ALU.max)
            K.act(dst[:, :, :], r[:, :, :], AF.Sin)
            K.ts("dve", dst_s[:, :, :], dst[:, :, :], MLA_SCALE, ALU.mult)
        K.barrier()


def rope_tm(K, c, e, out, xin, cs, sn, nh, tmp):
    def bc(v):
        return V(v.ap.unsqueeze(1).to_broadcast([128, nh, 16]), v.bufs)
    x1 = V(xin.ap[:, :, 0:16], xin.bufs)
    x2 = V(xin.ap[:, :, 16:32], xin.bufs)
    o1 = V(out.ap[:, :, 0:16], out.bufs)
    o2 = V(out.ap[:, :, 16:32], out.bufs)
    t = [V(tmp.h[:, i, 0:nh, :], (tmp.buf,)) for i in range(4)]
    K.tt(e, t[0], x1, bc(cs), ALU.mult)
    K.tt(e, t[1], x2, bc(sn), ALU.mult)
    K.tt(e, t[2], x2, bc(cs), ALU.mult)
    K.tt(e, t[3], x1, bc(sn), ALU.mult)
    K.tt(e, o1, t[0], t[1], ALU.subtract)
    K.tt(e, o2, t[2], t[3], ALU.add)


def phase_w(K, c, l, d):
    import contextlib
    nc = c.nc
    TW = 256
    NB = TW // 128
    with contextlib.ExitStack() as stack:
        win = sb(nc, stack, "win", [128, 8, D_IN], BF16)
        wuq = sb(nc, stack, "wuq", [128, 3, 768], BF16)
        grow = sb(nc, stack, "grow_w", [128, D_MODEL], F32)
        qnrow = sb(nc, stack, "qnrow", [128, 384], F32)
        kvnrow = sb(nc, stack, "kvnrow", [128, 256], F32)
        dtbrow = sb(nc, stack, "dtbrow", [128, 8], F32)
        negA = sb(nc, stack, "negA", [128, 8], F32)
        K.dma("sp", grow[:, :], dv(d["norm_mix"][l].partition_broadcast(128)))
        K.dma("sp", qnrow[:, :], dv(d["mla_q_norm"][l].partition_broadcast(128)))
        K.dma("sp", kvnrow[:, :], dv(d["mla_kv_norm"][l].partition_broadcast(128)))
        K.dma("sp", dtbrow[:, :], dv(d["gdn_dt_bias"][l].partition_broadcast(128)))
        K.dma("sp", negA[:, :], dv(d["gdn_A_log"][l].partition_broadcast(128)))
        K.act(negA[:, :], negA[:, :], AF.Exp)
        K.ts("dve", negA[:, :], negA[:, :], -1.0, ALU.mult)
        with contextlib.ExitStack() as st2:
            stg = [sb(nc, st2, "stgw%d" % i, [128, D_IN], F32) for i in range(2)]
            dsts = [win[:, kc, :] for kc in range(8)] + [wuq[:, kc, :] for kc in range(3)]
            srcs = [d["w_in"][l][kc * 128:(kc + 1) * 128, :] for kc in range(8)] + \
                   [d["mla_w_uq"][l][kc * 128:(kc + 1) * 128, :] for kc in range(3)]
            load_cast(K, c, stg, dsts, srcs)
            K.barrier()
        xt = [sb(nc, stack, "xw%d" % i, [128, NB, D_MODEL], F32) for i in range(2)]
        xn = [sb(nc, stack, "xnw%d" % i, [128, D_MODEL], BF16) for i in range(2)]
        junk = sb(nc, stack, "junkw", [128, D_MODEL], F32)
        hT = sb(nc, stack, "hTw", [128, 8, TW], BF16)
        ss = [sb(nc, stack, "ssw%d" % i, [128, 1], F32) for i in range(2)]
        rstd = [sb(nc, stack, "rstdw%d" % i, [128, 1], F32) for i in range(2)]
        qkvst = sb(nc, stack, "qkvst", [128, 12, TW], F32)
        zs_t = [sb(nc, stack, "zs_t%d" % i, [128, 512], F32) for i in range(2)]
        sg_t = [sb(nc, stack, "sg_t%d" % i, [128, 2048], F32) for i in range(2)]
        bg_t = [sb(nc, stack, "bg_t%d" % i, [128, 16], F32) for i in range(2)]
        sp_t = sb(nc, stack, "sp_t", [128, 4, 8], F32)
        cqn = sb(nc, stack, "cqn", [128, 384], BF16)
        cqnT = sb(nc, stack, "cqnT", [128, 3, 128], BF16)
        qs = sb(nc, stack, "qs", [128, 8, 128], BF16)
        K.memset("pool", qs[:, :, :], 0.0)
        qT_t = [sb(nc, stack, "qT_t%d" % i, [128, 8, 128], BF16) for i in range(2)]
        lat_t = [sb(nc, stack, "lat_t%d" % i, [128, 288], F32) for i in range(2)]
        rtmp = sb(nc, stack, "rtmp", [128, 4, 8, 16], F32)
        ss2 = sb(nc, stack, "ss2", [128, 2], F32)
        rs2 = sb(nc, stack, "rs2", [128, 2], F32)
        ntile = T // TW
        xs_t = d["xres"].rearrange("(n b p) d -> n p b d", p=128, b=NB)
        qkvT_v = d["qkvT"].rearrange("(c p) t -> p c t", p=128)
        QT_v = d["QT"].rearrange("h d t -> d h t")
        K.dma("sp", xt[0][:, :, :], dv(xs_t[0]))
        for i in range(ntile):
            x = xt[i % 2]
            if i + 1 < ntile:
                K.dma("sp", xt[(i + 1) % 2][:, :, :], dv(xs_t[i + 1]))
            for b in range(NB):
                rms_rstd(K, c, x[:, b, :], ss[b % 2], rstd[b % 2], junk[:, :], D_MODEL)
                K.stt("dve", xn[b % 2][:, :], x[:, b, :], rstd[b % 2][:, 0:1], grow[:, :], ALU.mult, ALU.mult)
                pt = c.psb[b % 2]
                for kc in range(8):
                    K.tr(pt[:, kc * 128:(kc + 1) * 128], xn[b % 2][:, kc * 128:(kc + 1) * 128], c.ident_bf[:, :])
                K.copy("act" if b % 2 == 0 else "dve", hT[:, :, b * 128:(b + 1) * 128],
                       V(pt.h[:, :].rearrange("p (k t) -> p k t", k=8), (pt.buf,)))
            for j in range(12):
                pf = c.ps[2 + (j % 2)]
                for kc in range(8):
                    K.mm(pf[:, 0:TW], win[:, kc, j * 128:(j + 1) * 128], hT[:, kc, :], kc == 0, kc == 7)
                K.copy("act" if j % 2 == 0 else "dve", qkvst[:, j, :], pf[:, 0:TW])
            K.dma("pool", dv(qkvT_v[:, :, 2 + i * TW:2 + (i + 1) * TW]), qkvst[:, :, :])
            for b in range(NB):
                blk = i * NB + b
                hb = lambda kc: hT[:, kc, b * 128:(b + 1) * 128]
                pz = c.ps[4]
                for kc in range(8):
                    K.mm(pz[:, :], hb(kc), win[:, kc, 1536:2048], kc == 0, kc == 7)
                z = zs_t[blk % 2]
                K.act(z[:, :], pz[:, :], AF.Silu)
                K.dma("pool", dv(d["zs"][blk * 128:(blk + 1) * 128, :]), z[:, :])
                sgt = sg_t[blk % 2]
                for g4 in range(4):
                    pgt = c.ps[5 + (g4 % 2)]
                    for kc in range(8):
                        K.mm(pgt[:, :], hb(kc), win[:, kc, 2736 + g4 * 512:2736 + (g4 + 1) * 512], kc == 0, kc == 7)
                    K.act(sgt[:, g4 * 512:(g4 + 1) * 512], pgt[:, :], AF.Sigmoid)
                K.dma("pool", dv(d["sgd"][blk * 128:(blk + 1) * 128, :]), sgt[:, :])
                p2 = c.ps[7]
                for kc in range(8):
                    K.mm(p2[:, 0:400], hb(kc), win[:, kc, 2048:2448], kc == 0, kc == 7)
                p3 = c.ps[4]
                bgt = bg_t[blk % 2]
                K.act(bgt[:, 0:8], p2[:, 0:8], AF.Sigmoid)
                tt_ = sp_t[:, 0, :]
                K.tt("dve", tt_, p2[:, 8:16], dtbrow[:, :], ALU.add)
                ab = sp_t[:, 1, :]
                K.act(ab, tt_, AF.Abs)
                ee = sp_t[:, 2, :]
                K.act(ee, ab, AF.Exp, scale=-1.0)
                K.act(ee, ee, AF.Ln, bias=c.one_t[:, 0:1])
                sp_ = sp_t[:, 3, :]
                K.stt("dve", sp_, tt_, 0.0, ee, ALU.max, ALU.add)
                K.tt("dve", bgt[:, 8:16], sp_, negA[:, :], ALU.mult)
                K.dma("pool", dv(d["bg"][blk * 128:(blk + 1) * 128, :]), bgt[:, :])
                K.act(junk[:, 0:384], p2[:, 16:400], AF.Square, accum=ss2[:, 0:1])
                K.act(rs2[:, 0:1], ss2[:, 0:1], AF.Sqrt, bias=c.eps_t[:, 0:1], scale=1.0 / 384)
                K.recip(rs2[:, 0:1], rs2[:, 0:1])
                K.stt("dve", cqn[:, :], p2[:, 16:400], rs2[:, 0:1], qnrow[:, :], ALU.mult, ALU.mult)
                ptq = c.psb[0]
                for kc in range(3):
                    K.tr(ptq[:, kc * 128:(kc + 1) * 128], cqn[:, kc * 128:(kc + 1) * 128], c.ident_bf[:, :])
                K.copy("act", cqnT[:, :, :], V(ptq.h[:, 0:384].rearrange("p (k t) -> p k t", k=3), (ptq.buf,)))
                for kc in range(8):
                    K.mm(p3[:, 0:288], hb(kc), win[:, kc, 2448:2736], kc == 0, kc == 7)
                for hh in range(2):
                    pq = c.ps[5 + hh]
                    for kc in range(3):
                        K.mm(pq[:, 0:384], cqnT[:, kc, :], wuq[:, kc, hh * 384:(hh + 1) * 384], kc == 0, kc == 2)
                    pqv = V(pq.h[:, 0:384].rearrange("p (h e) -> p h e", h=4), (pq.buf,))
                    K.ts("dve", qs[:, hh * 4:(hh + 1) * 4, 0:64], V(pqv.ap[:, :, 0:64], pqv.bufs), MLA_SCALE, ALU.mult)
                    rope_tm(K, c, "dve", qs[:, hh * 4:(hh + 1) * 4, 64:96], V(pqv.ap[:, :, 64:96], pqv.bufs),
                            c.cos_s[:, blk, :], c.sin_s[:, blk, :], 4, rtmp)
                pqt = c.psb[1]
                for h in range(8):
                    K.tr(pqt[:, h * 128:(h + 1) * 128], qs[:, h, :], c.ident_bf[:, :])
                qTt = qT_t[blk % 2]
                K.copy("act", qTt[:, :, :], V(pqt.h[:, :].rearrange("p (h t) -> p h t", h=8), (pqt.buf,)))
                K.dma("pool", dv(QT_v[:, :, blk * 128:(blk + 1) * 128]), qTt[:, :, :])
                lt = lat_t[blk % 2]
                K.act(junk[:, 0:256], p3[:, 0:256], AF.Square, accum=ss2[:, 1:2])
                K.act(rs2[:, 1:2], ss2[:, 1:2], AF.Sqrt, bias=c.eps_t[:, 0:1], scale=1.0 / 256)
                K.recip(rs2[:, 1:2], rs2[:, 1:2])
                K.stt("dve", lt[:, 0:256], p3[:, 0:256], rs2[:, 1:2], kvnrow[:, :], ALU.mult, ALU.mult)
                rope_tm(K, c, "dve", V(lt.h[:, 256:288].rearrange("p (h e) -> p h e", h=1), (lt.buf,)),
                        V(p3.h[:, 256:288].rearrange("p (h e) -> p h e", h=1), (p3.buf,)),
                        c.cos[:, blk, :], c.sin[:, blk, :], 1, rtmp)
                CR = min(1024, T)
                K.dma("pool", dv(d["lat_src%d" % (blk * 128 // CR)][(blk * 128) % CR:(blk * 128) % CR + 128, :]), lt[:, :])
        K.barrier()
        hl = sb(nc, stack, "hl", [128, 12, 2], F32)
        K.dma("sp", hl[:, :, :], dv(qkvT_v[:, :, T:T + 2]))
        K.dma("sp", dv(d["halo_src"].rearrange("(c p) t -> p c t", p=128)), hl[:, :, :])
        K.barrier()


def collective_gather(K, c, src_h, dst_h):
    nc = c.nc
    K.barrier()
    groups = [[2 * i, 2 * i + 1] for i in range(c.ncores // 2)]
    ins = nc.gpsimd.collective_compute("AllGather", ALU.bypass, replica_groups=groups,
                                       ins=[src_h.ap().opt()], outs=[dst_h.ap().opt()])
    c.cc_cnt += 1
    ins.then_inc(c.cc_sem)
    K.ninst += 1
    for e in K.engs:
        K.engs[e].wait_ge(c.cc_sem, c.cc_cnt)


def sel_other(K, c, e, out, slot0, slot1):
    K.ts(e, out, slot0, c.sel[:, 0:1], ALU.mult)
    K.stt(e, out, slot1, c.sel[:, 1:2], out, ALU.mult, ALU.add)


def phase_x1(K, c, l, d):
    import contextlib
    nc = c.nc
    for ch in range(T // min(1024, T)):
        collective_gather(K, c, d["lat_src%d_h" % ch], d["lat_all%d_h" % ch])
    collective_gather(K, c, d["halo_src_h"], d["halo_all_h"])
    with contextlib.ExitStack() as stack:
        ha = sb(nc, stack, "ha", [128, 2, 12, 2], F32)
        ho = sb(nc, stack, "ho", [128, 12, 2], F32)
        hz = sb(nc, stack, "hz", [128, 12, 2], F32)
        K.dma("sp", ha[:, :, :, :], dv(d["halo_all"].rearrange("(s c p) t -> p s c t", p=128, s=2)))
        sel_other(K, c, "dve", ho[:, :, :], ha[:, 0, :, :], ha[:, 1, :, :])
        qkvT_v = d["qkvT"].rearrange("(c p) t -> p c t", p=128)
        K.dma("sp", dv(qkvT_v[:, :, T + 2:T + 3]), ho[:, :, 1:2], allow_slow_non_contiguous=True)
        K.dma("sp", dv(qkvT_v[:, :, T + 3:T + 4]), ho[:, :, 0:1], allow_slow_non_contiguous=True)
        K.memset("dve", hz[:, :, :], 0.0)
        K.dma("sp", dv(qkvT_v[:, :, 0:2]), hz[:, :, :])
        K.barrier()


def phase_a(K, c, l, d, aoT):
    import contextlib
    nc = c.nc
    NK = 2 * T
    NKB = NK // 128
    NKT = NK // 512
    QW = 512 if T >= 512 else T
    NQT = T // QW
    with contextlib.ExitStack() as stack:
        wukv = sb(nc, stack, "wukv", [128, 2, 1024], BF16)
        ckvnT = sb(nc, stack, "ckvnT", [128, 2, NK], BF16)
        KT = [sb(nc, stack, "KT%d" % i, [128, NK], BF16) for i in range(2)]
        Vh = [sb(nc, stack, "Vh%d" % i, [128, NKB, 128], BF16) for i in range(2)]
        QTh = [sb(nc, stack, "QTh%d" % i, [128, T], BF16) for i in range(2)]
        pT = [sb(nc, stack, "pT%d" % i, [128, QW], BF16) for i in range(3)]
        rs = sb(nc, stack, "rs_a", [128, QW], F32)
        bc = sb(nc, stack, "bc_a", [128, QW], F32)
        latf = [sb(nc, stack, "latf%d" % i, [128, 288], F32) for i in range(3)]
        latb = [sb(nc, stack, "latb%d" % i, [128, 320], BF16) for i in range(2)]
        for i in range(2):
            K.memset("pool", latb[i][:, 288:320], 0.0)
        K.memset("pool", rs[:, :], 0.0)
        with contextlib.ExitStack() as st2:
            stg = [sb(nc, st2, "stga%d" % i, [128, 1024], F32) for i in range(2)]
            load_cast(K, c, stg, [wukv[:, kc, :] for kc in range(2)],
                      [d["mla_w_ukv"][l][kc * 128:(kc + 1) * 128, :] for kc in range(2)])
            K.barrier()
        import os
        amode = int(os.environ.get("AMODE", "0"))
        if amode == 4:
            K.barrier()
            return
        for p in range(2):
            K.memset("pool", Vh[p][:, :, :], 0.0)
        K.memset("pool", Vh[0][:, :, 64:65], 1.0)
        K.memset("pool", Vh[1][:, :, 0:1], 1.0)
        if amode == 5:
            K.barrier()
            return
        for kb in range(NKB):
            lf = latf[kb % 3]
            lb = latb[kb % 2]
            CR2 = 2 * min(1024, T)
            K.dma("sp" if kb % 2 == 0 else "act", lf[:, :],
                  dv(d["lat_all%d" % (kb * 128 // CR2)][(kb * 128) % CR2:(kb * 128) % CR2 + 128, :]))
            K.copy("pool", lb[:, 0:288], lf[:, :])
            pt = c.psb[5 + (kb % 2)]
            for kc in range(2):
                K.tr(pt[:, kc * 128:(kc + 1) * 128], lb[:, kc * 128:(kc + 1) * 128], c.ident_bf[:, :])
            pk_ = c.psb[3 + (kb % 2)]
            K.tr(pk_[:, 0:128], lb[:, 192:320], c.ident_bf[:, :])
            K.copy("dve", ckvnT[:, :, kb * 128:(kb + 1) * 128],
                   V(pt.h[:, 0:256].rearrange("p (k t) -> p k t", k=2), (pt.buf,)))
            K.copy("act", KT[0][64:128, kb * 128:(kb + 1) * 128], pk_[64:128, 0:128])
            K.copy("dve", KT[1][64:128, kb * 128:(kb + 1) * 128], pk_[64:128, 0:128])
        QT_d = d["QT"]
        if amode == 1:
            K.barrier()
            return
        K.dma("sp", QTh[0][:, :], dv(QT_d[0]))
        cnt = 0
        for h in range(8):
            par = h % 2
            kt_ = KT[par]
            vh = Vh[par]
            if h + 1 < 8:
                K.dma("sp", QTh[(h + 1) % 2][:, :], dv(QT_d[h + 1]))
            qh = QTh[h % 2]
            for kt in range(NKT):
                pk = c.ps[5 + (kt % 2)]
                for kc in range(2):
                    K.mm(pk[:, :], wukv[:, kc, h * 128:h * 128 + 128], ckvnT[:, kc, kt * 512:(kt + 1) * 512], kc == 0, kc == 1)
                K.copy("dve", kt_[0:64, kt * 512:(kt + 1) * 512], pk[0:64, :])
            voff = 0 if par == 0 else 64
            for k8 in range(NKB // 8):
                pv = c.ps[5 + (k8 % 2)]
                for j in range(8):
                    kb = k8 * 8 + j
                    for kc in range(2):
                        K.mm(pv[:, j * 64:(j + 1) * 64], ckvnT[:, kc, kb * 128:(kb + 1) * 128],
                             wukv[:, kc, h * 128 + 64:h * 128 + 128], kc == 0, kc == 1)
                K.copy("dve", vh[:, k8 * 8:(k8 + 1) * 8, voff:voff + 64],
                       V(pv.h[:, :].rearrange("p (j e) -> p j e", j=8), (pv.buf,)))
            srow = 64 if par == 0 else 0
            if amode == 2:
                continue
            for qt in range(NQT):
                po = c.ps[3 + (cnt % 2)]
                cnt += 1
                qv = qh[:, qt * QW:(qt + 1) * QW]
                K.mm(c.ps[0][:, 0:QW], kt_[:, 0:128], qv, True, True)
                for kb in range(NKB):
                    if kb + 1 < NKB:
                        K.mm(c.ps[(kb + 1) % 3][:, 0:QW], kt_[:, (kb + 1) * 128:(kb + 2) * 128], qv, True, True)
                    K.act(pT[kb % 3][:, :], c.ps[kb % 3][:, 0:QW], AF.Exp)
                    K.mm(po[:, 0:QW], vh[:, kb, :], pT[kb % 3][:, :], kb == 0, kb == NKB - 1)
                if amode == 3:
                    continue
                K.recip(rs[srow:srow + 1, :], po[srow:srow + 1, 0:QW])
                pb = c.ps[7]
                K.mm(pb[:, 0:QW], c.rowsel[par][:, :], rs[:, :], True, True)
                K.copy("act", bc[:, :], pb[:, 0:QW])
                K.tt("dve", aoT[voff:voff + 64, h // 2, qt * QW:(qt + 1) * QW], po[voff:voff + 64, 0:QW],
                     bc[voff:voff + 64, :], ALU.mult)
        K.barrier()


def phase_m(K, c, l, d, aoT, goT):
    import contextlib
    nc = c.nc
    with contextlib.ExitStack() as stack:
        wpa = sb(nc, stack, "wpa", [128, 4, D_MODEL], BF16)
        wpb = sb(nc, stack, "wpb", [128, 4, D_MODEL], BF16)
        wo = sb(nc, stack, "wo", [128, 8, D_MODEL], BF16)
        with contextlib.ExitStack() as st2:
            stg = [sb(nc, st2, "stgm%d" % i, [128, D_MODEL], F32) for i in range(3)]
            dsts = [wpa[:, kc, :] for kc in range(4)] + [wpb[:, kc, :] for kc in range(4)] + [wo[:, kc, :] for kc in range(8)]
            srcs = [d["gdn_proj"][l][kc * 128:(kc + 1) * 128, :] for kc in range(4)] + \
                   [d["mla_proj"][l][kc * 128:(kc + 1) * 128, :] for kc in range(4)] + \
                   [d["w_out"][l][kc * 128:(kc + 1) * 128, :] for kc in range(8)]
            load_cast(K, c, stg, dsts, srcs)
            K.barrier()
        xt = [sb(nc, stack, "xm%d" % i, [128, D_MODEL], F32) for i in range(2)]
        sgt = [sb(nc, stack, "sgm%d" % i, [128, 2048], F32) for i in range(2)]
        ya = sb(nc, stack, "ya", [128, D_MODEL], F32)
        yb = sb(nc, stack, "yb", [128, D_MODEL], BF16)
        yT = sb(nc, stack, "yT", [128, 8, 128], BF16)
        nb = T // 128
        K.dma("sp", xt[0][:, :], dv(d["xres"][0:128, :]))
        K.dma("act", sgt[0][:, :], dv(d["sgd"][0:128, :]))
        for blk in range(nb):
            x = xt[blk % 2]
            sg = sgt[blk % 2]
            if blk + 1 < nb:
                K.dma("sp", xt[(blk + 1) % 2][:, :], dv(d["xres"][(blk + 1) * 128:(blk + 2) * 128, :]))
                K.dma("act", sgt[(blk + 1) % 2][:, :], dv(d["sgd"][(blk + 1) * 128:(blk + 2) * 128, :]))
            for hf in range(2):
                pa = c.ps[hf]
                pb = c.ps[2 + hf]
                for kc in range(4):
                    K.mm(pa[:, :], goT[:, kc, blk * 128:(blk + 1) * 128], wpa[:, kc, hf * 512:(hf + 1) * 512], kc == 0, kc == 3)
                for kc in range(4):
                    K.mm(pb[:, :], aoT[:, kc, blk * 128:(blk + 1) * 128], wpb[:, kc, hf * 512:(hf + 1) * 512], kc == 0, kc == 3)
                K.tt("dve", ya[:, hf * 512:(hf + 1) * 512], pa[:, :], sg[:, hf * 512:(hf + 1) * 512], ALU.mult)
                K.tt("dve", sg[:, 1024 + hf * 512:1024 + (hf + 1) * 512], pb[:, :], sg[:, 1024 + hf * 512:1024 + (hf + 1) * 512], ALU.mult)
                K.tt("pool", yb[:, hf * 512:(hf + 1) * 512], ya[:, hf * 512:(hf + 1) * 512],
                     sg[:, 1024 + hf * 512:1024 + (hf + 1) * 512], ALU.add)
            pt = c.psb[4]
            for kc in range(8):
                K.tr(pt[:, kc * 128:(kc + 1) * 128], yb[:, kc * 128:(kc + 1) * 128], c.ident_bf[:, :])
            K.copy("act", yT[:, :, :], V(pt.h[:, :].rearrange("p (k t) -> p k t", k=8), (pt.buf,)))
            for hf in range(2):
                po = c.ps[5 + hf]
                for kc in range(8):
                    K.mm(po[:, :], yT[:, kc, :], wo[:, kc, hf * 512:(hf + 1) * 512], kc == 0, kc == 7)
                K.tt("dve", x[:, hf * 512:(hf + 1) * 512], po[:, :], x[:, hf * 512:(hf + 1) * 512], ALU.add)
            K.dma("pool", dv(d["xres"][blk * 128:(blk + 1) * 128, :]), x[:, :])
        K.barrier()


def phase_g0(K, c, l, d):
    import contextlib
    nc = c.nc
    TG = 512 if T >= 512 else T
    NB = TG // 128
    with contextlib.ExitStack() as stack:
        cw = sb(nc, stack, "cw", [128, 12, 5], F32)
        K.dma("sp", cw[:, :, :], dv(d["gdn_conv"][l].rearrange("(c p) k -> p c k", p=128)))
        xin = [sb(nc, stack, "xin%d" % i, [128, 12, TG + 4], F32) for i in range(2)]
        y = [sb(nc, stack, "ycv%d" % i, [128, TG], F32) for i in range(2)]
        ysil = sb(nc, stack, "ysil", [128, 12, TG], F32)
        ytmp = sb(nc, stack, "ytmp", [128, TG], F32)
        tm = [sb(nc, stack, "tm%d" % i, [128, 1536], F32) for i in range(2)]
        ss8 = sb(nc, stack, "ss8", [128, 8], F32)
        rs8 = sb(nc, stack, "rs8", [128, 8], F32)
        junk = sb(nc, stack, "junkg", [128, 128], F32)
        qkvT_v = d["qkvT"].rearrange("(c p) t -> p c t", p=128)
        ntile = T // TG
        K.dma("sp", xin[0][:, :, :], dv(qkvT_v[:, :, 0:TG + 4]))
        for i in range(ntile):
            xi = xin[i % 2]
            if i + 1 < ntile:
                K.dma("sp", xin[(i + 1) % 2][:, :, :], dv(qkvT_v[:, :, (i + 1) * TG:(i + 1) * TG + TG + 4]))
            for cc in range(12):
                e = "dve" if cc % 2 == 0 else "pool"
                yy = y[cc % 2]
                K.ts(e, yy[:, :], xi[:, cc, 0:TG], cw[:, cc, 0:1], ALU.mult)
                for k in range(1, 5):
                    if e == "dve":
                        K.stt(e, yy[:, :], xi[:, cc, k:k + TG], cw[:, cc, k:k + 1], yy[:, :], ALU.mult, ALU.add)
                    else:
                        K.ts(e, ytmp[:, :], xi[:, cc, k:k + TG], cw[:, cc, k:k + 1], ALU.mult)
                        K.tt(e, yy[:, :], yy[:, :], ytmp[:, :], ALU.add)
                K.act(ysil[:, cc, :], yy[:, :], AF.Silu)
            for b in range(NB):
                blk = i * NB + b
                t_ = tm[blk % 2]
                for cc in range(12):
                    K.tr(c.ps[cc // 4][:, (cc % 4) * 128:(cc % 4 + 1) * 128], ysil[:, cc, b * 128:(b + 1) * 128], c.ident_f[:, :])
                for g in range(8):
                    K.act(junk[:, :], c.ps[g // 4][:, (g % 4) * 128:(g % 4 + 1) * 128], AF.Square, accum=ss8[:, g:g + 1])
                K.act(rs8[:, :], ss8[:, :], AF.Sqrt, bias=c.eps_t[:, 0:1], scale=1.0)
                K.recip(rs8[:, :], rs8[:, :])
                K.ts("dve", rs8[:, 0:4], rs8[:, 0:4], 128.0 ** -0.5, ALU.mult)
                for g in range(8):
                    K.ts("dve", t_[:, g * 128:(g + 1) * 128], c.ps[g // 4][:, (g % 4) * 128:(g % 4 + 1) * 128], rs8[:, g:g + 1], ALU.mult)
                K.copy("act", t_[:, 1024:1536], c.ps[2][:, :])
                K.dma("pool", dv(d["qkvn"][blk * 128:(blk + 1) * 128, :]), t_[:, :])
        K.barrier()


def phase_gscan(K, c, l, d, dirn, S, goT=None):
    import contextlib
    nc = c.nc
    nb = T // 128
    tri = c.tri[dirn]
    negm4 = c.negm4[dirn]
    nstr4 = c.nstr4[dirn]
    with contextlib.ExitStack() as stack:
        qkv = [sb(nc, stack, "qkvg%d" % i, [128, 1536], F32) for i in range(2)]
        bgt = [sb(nc, stack, "bgg%d" % i, [128, 16], F32) for i in range(2)]
        kqT = sb(nc, stack, "kqT", [128, 4, 256], F32)
        GT = sb(nc, stack, "GT", [128, 4, 128], F32)
        sm = sb(nc, stack, "smg", [128, 6, 4], F32)
        DT = sb(nc, stack, "DT", [128, 4, 128], F32)
        Xt = sb(nc, stack, "Xt", [128, 4, 128], F32)
        Xs = [sb(nc, stack, "Xs%d" % i, [128, 4, 128], F32) for i in range(2)]
        Ys = [sb(nc, stack, "Ys%d" % i, [128, 4, 128], F32) for i in range(2)]
        Ns = [sb(nc, stack, "Ns%d" % i, [128, 4, 128], F32) for i in range(2)]
        A2 = sb(nc, stack, "A2", [128, 4, 128], F32)
        rhs = sb(nc, stack, "rhsg", [128, 4, 256], F32)
        UW = sb(nc, stack, "UW", [128, 4, 256], F32)
        WT = sb(nc, stack, "WT", [128, 4, 128], F32)
        qg = sb(nc, stack, "qg", [128, 4, 128], F32)
        qgT = sb(nc, stack, "qgT", [128, 4, 128], F32)
        kd = sb(nc, stack, "kd", [128, 4, 128], F32)
        VNs = [sb(nc, stack, "VN%d" % i, [128, 4, 128], F32) for i in range(2)]
        for i in range(2):
            K.memset("pool", VNs[i][:, :, :], 0.0)
        O = [sb(nc, stack, "Og%d" % i, [128, 4, 128], F32) for i in range(2)]
        if dirn == 1:
            o1t = [sb(nc, stack, "o1t%d" % i, [128, 4, 128], F32) for i in range(2)]
            zt = [sb(nc, stack, "ztg%d" % i, [128, 4, 128], F32) for i in range(2)]
            ob = sb(nc, stack, "obg", [128, 4, 128], BF16)
            nrow = sb(nc, stack, "nrowg", [128, 128], F32)
            ss4 = sb(nc, stack, "ss4", [128, 4], F32)
            rs4 = sb(nc, stack, "rs4", [128, 4], F32)
            junk = sb(nc, stack, "junkgs", [128, 128], F32)
            K.dma("sp", nrow[:, :], dv(d["gdn_norm"][l].partition_broadcast(128)))
        ident4 = V(c.ident_f.h[:, :].unsqueeze(1).to_broadcast([128, 4, 128]), (c.ident_f.buf,))
        order = list(range(nb)) if dirn == 0 else list(range(nb - 1, -1, -1))

        def loads(j, blk):
            K.dma("sp", qkv[j % 2][:, :], dv(d["qkvn"][blk * 128:(blk + 1) * 128, :]))
            K.dma("act", bgt[j % 2][:, :], dv(d["bg"][blk * 128:(blk + 1) * 128, :]))
            if dirn == 1:
                K.dma("sp", o1t[j % 2][:, :, :], dv(d["o1"][blk * 128:(blk + 1) * 128, :].rearrange("p (h e) -> p h e", h=4)))
                K.dma("act", zt[j % 2][:, :, :], dv(d["zs"][blk * 128:(blk + 1) * 128, :].rearrange("p (h e) -> p h e", h=4)))

        loads(0, order[0])
        for j, blk in enumerate(order):
            if j + 1 < nb:
                loads(j + 1, order[j + 1])
            qk = qkv[j % 2]
            bg_ = bgt[j % 2]
            beta = lambda h: bg_[:, dirn * 4 + h:dirn * 4 + h + 1]
            g4 = bg_[:, 8 + dirn * 4:8 + dirn * 4 + 4]
            gcol = lambda h: bg_[:, 8 + dirn * 4 + h:8 + dirn * 4 + h + 1]
            qv = lambda h: qk[:, h * 128:(h + 1) * 128]
            kv = lambda h: qk[:, 512 + h * 128:512 + (h + 1) * 128]
            for hh in range(2):
                pt = c.ps[hh]
                for h2 in range(2):
                    h = hh * 2 + h2
                    K.tr(pt[:, h2 * 256:h2 * 256 + 128], kv(h), c.ident_f[:, :])
                    K.tr(pt[:, h2 * 256 + 128:h2 * 256 + 256], qv(h), c.ident_f[:, :])
                K.copy("act" if hh == 0 else "dve", kqT[:, hh * 2:hh * 2 + 2, :],
                       V(pt.h[:, :].rearrange("p (h e) -> p h e", h=2), (pt.buf,)))
            pg = c.ps[2]
            K.mm(pg[:, 0:4], tri[:, :], g4, True, True)
            K.mm(pg[:, 4:8], c.same[:, :], g4, True, True)
            K.mm(pg[:, 8:12], c.ch[0][:, :], g4, True, True)
            K.mm(pg[:, 12:16], c.ch[1][:, :], g4, True, True)
            gc = sm[:, 0, :]
            ngc = sm[:, 1, :]
            egc = sm[:, 2, :]
            ekd = sm[:, 3, :]
            K.copy("dve", gc, pg[:, 0:4])
            K.ts("dve", ngc, pg[:, 0:4], -1.0, ALU.mult)
            K.act(egc, pg[:, 0:4], AF.Exp)
            K.tt("dve", ekd, pg[:, 4:8], gc, ALU.subtract)
            K.act(ekd, ekd, AF.Exp)
            K.act(V(sm.h[:, 4:6, :], (sm.buf,)), V(pg.h[:, 8:16].rearrange("p (a b) -> p a b", a=2), (pg.buf,)), AF.Exp)
            for h in range(4):
                K.ts("pool" if h % 2 else "dve", GT[:, h, :], tri[:, :], gcol(h), ALU.mult)
            pR = c.ps[3]
            GTf = V(GT.h[:, :, :].rearrange("p h e -> p (h e)"), (GT.buf,))
            K.mm(pR[:, :], c.ones_f[:, :], GTf, True, False)
            K.mm(pR[:, :], c.ident_f[:, :], negm4[:, :], False, True)
            for h in range(4):
                K.act(DT[:, h, :], pR[:, h * 128:(h + 1) * 128], AF.Exp, bias=sm[:, 1, h:h + 1])
            pK = [c.ps[4], c.ps[5]]
            for h in range(4):
                K.mm(pK[h // 2][:, (h % 2) * 256:(h % 2) * 256 + 256], kqT[:, h, 0:128], kqT[:, h, :], True, True)
            for h in range(4):
                K.stt("dve", Xt[:, h, :], pK[h // 2][:, (h % 2) * 256:(h % 2) * 256 + 128], beta(h), DT[:, h, :], ALU.mult, ALU.mult)
                K.tt("dve", A2[:, h, :], pK[h // 2][:, (h % 2) * 256 + 128:(h % 2) * 256 + 256], DT[:, h, :], ALU.mult)
            X, Y, N = Xs[0], Ys[0], Ns[0]
            K.tt("pool", X[:, :, :], Xt[:, :, :], nstr4[:, :, :], ALU.mult)
            pY = c.ps[6]
            for h in range(4):
                K.tr(pY[:, h * 128:(h + 1) * 128], X[:, h, :], c.ident_f[:, :])
            K.copy("act", Y[:, :, :], V(pY.h[:, :].rearrange("p (h e) -> p h e", h=4), (pY.buf,)))
            K.tt("pool", N[:, :, :], X[:, :, :], ident4, ALU.add)
            for lvl in range(5):
                last = lvl == 4
                Xn, Yn, Nn = Xs[(lvl + 1) % 2], Ys[(lvl + 1) % 2], Ns[(lvl + 1) % 2]
                pY2 = c.ps[6 + (lvl % 2)]
                for h in range(4):
                    K.mm(pY2[:, h * 128:(h + 1) * 128], X[:, h, :], Y[:, h, :], True, True)
                if not last:
                    pX2 = c.ps[0 + (lvl % 2)]
                    for h in range(4):
                        K.mm(pX2[:, h * 128:(h + 1) * 128], Y[:, h, :], X[:, h, :], True, True)
                K.copy("act", Yn[:, :, :], V(pY2.h[:, :].rearrange("p (h e) -> p h e", h=4), (pY2.buf,)))
                if not last:
                    K.copy("dve", Xn[:, :, :], V(pX2.h[:, :].rearrange("p (h e) -> p h e", h=4), (pX2.buf,)))
                pN = c.ps[2 + (lvl % 2)]
                for h in range(4):
                    K.mm(pN[:, h * 128:(h + 1) * 128], Yn[:, h, :], N[:, h, :], True, True)
                K.tt("dve", Nn[:, :, :], V(pN.h[:, :].rearrange("p (h e) -> p h e", h=4), (pN.buf,)), N[:, :, :], ALU.add)
                X, Y, N = Xn, Yn, Nn
            K.copy("pool", rhs[:, :, 0:128], V(qk.h[:, 1024:1536].rearrange("p (h e) -> p h e", h=4), (qk.buf,)))
            for h in range(4):
                K.ts("pool", rhs[:, h, 128:256], kv(h), sm[:, 2, h:h + 1], ALU.mult)
            pU = [c.ps[4], c.ps[5]]
            for h in range(4):
                K.mm(pU[h // 2][:, (h % 2) * 256:(h % 2) * 256 + 256], N[:, h, :], rhs[:, h, :], True, True)
            for h in range(4):
                K.ts("dve", UW[:, h, :], pU[h // 2][:, (h % 2) * 256:(h % 2) * 256 + 256], beta(h), ALU.mult)
            pW = c.ps[0]
            for h in range(4):
                K.tr(pW[:, h * 128:(h + 1) * 128], UW[:, h, 128:256], c.ident_f[:, :])
            K.copy("act", WT[:, :, :], V(pW.h[:, :].rearrange("p (h e) -> p h e", h=4), (pW.buf,)))
            for h in range(4):
                K.ts("pool", qg[:, h, :], qv(h), sm[:, 2, h:h + 1], ALU.mult)
                K.ts("pool", kd[:, h, :], kv(h), sm[:, 3, h:h + 1], ALU.mult)
            pQ = c.ps[1]
            for h in range(4):
                K.tr(pQ[:, h * 128:(h + 1) * 128], qg[:, h, :], c.ident_f[:, :])
            K.copy("dve", qgT[:, :, :], V(pQ.h[:, :].rearrange("p (h e) -> p h e", h=4), (pQ.buf,)))
            Ot = O[j % 2]
            for cch in ((0, 1) if dirn == 0 else (1, 0)):
                r0, r1 = cch * 64, cch * 64 + 64
                VN = VNs[cch]
                pV = c.ps[2]
                for h in range(4):
                    K.mm(pV[:, h * 128:(h + 1) * 128], WT[:, h, :], S[:, h, :], True, True)
                K.tt("dve", VN[r0:r1, :, :], UW[r0:r1, :, 0:128],
                     V(pV.h[r0:r1, :].rearrange("p (h e) -> p h e", h=4), (pV.buf,)), ALU.subtract)
                pO = c.ps[3]
                for h in range(4):
                    K.mm(pO[:, h * 128:(h + 1) * 128], qgT[:, h, :], S[:, h, :], True, False)
                    K.mm(pO[:, h * 128:(h + 1) * 128], A2[:, h, :], VN[:, h, :], False, True)
                K.copy("act", Ot[r0:r1, :, :], V(pO.h[r0:r1, :].rearrange("p (h e) -> p h e", h=4), (pO.buf,)))
                pS = c.ps[6]
                for h in range(4):
                    K.mm(pS[:, h * 128:(h + 1) * 128], kd[:, h, :], VN[:, h, :], True, True)
                for h in range(4):
                    K.stt("dve", S[:, h, :], S[:, h, :], sm[:, 4 + cch, h:h + 1], pS[:, h * 128:(h + 1) * 128], ALU.mult, ALU.add)
            if dirn == 0:
                K.dma("pool", dv(d["o1"][blk * 128:(blk + 1) * 128, :].rearrange("p (h e) -> p h e", h=4)), Ot[:, :, :])
            else:
                K.tt("pool", Ot[:, :, :], Ot[:, :, :], o1t[j % 2][:, :, :], ALU.add)
                for h in range(4):
                    K.act(junk[:, :], Ot[:, h, :], AF.Square, accum=ss4[:, h:h + 1])
                K.act(rs4[:, :], ss4[:, :], AF.Sqrt, bias=c.eps_t[:, 0:1], scale=1.0 / 128)
                K.recip(rs4[:, :], rs4[:, :])
                for h in range(4):
                    K.stt("dve", Ot[:, h, :], Ot[:, h, :], rs4[:, h:h + 1], nrow[:, :], ALU.mult, ALU.mult)
                K.tt("pool", ob[:, :, :], Ot[:, :, :], zt[j % 2][:, :, :], ALU.mult)
                pG = c.psb[7]
                for h in range(4):
                    K.tr(pG[:, h * 128:(h + 1) * 128], ob[:, h, :], c.ident_bf[:, :])
                K.copy("act", goT[:, :, blk * 128:(blk + 1) * 128], V(pG.h[:, 0:512].rearrange("p (h e) -> p h e", h=4), (pG.buf,)))
        K.barrier()


def phase_x2(K, c, l, d, S):
    import contextlib
    nc = c.nc
    K.dma("sp", dv(d["st_src"].rearrange("(h p) v -> p h v", p=128)), S[:, :, :])
    collective_gather(K, c, d["st_src_h"], d["st_all_h"])
    with contextlib.ExitStack() as stack:
        sa = sb(nc, stack, "sa", [128, 2, 4, 128], F32)
        K.dma("sp", sa[:, :, :, :], dv(d["st_all"].rearrange("(s h p) v -> p s h v", p=128, s=2)))
        sel_other(K, c, "dve", S[:, :, :], sa[:, 0, :, :], sa[:, 1, :, :])
        K.barrier()


CONST_COLS = 128 * 9 + 512 * 4


def make_consts():
    idx = np.arange(128)
    same = (idx[:, None] // 64) == (idx[None, :] // 64)
    cs = np.zeros((128, CONST_COLS), np.float32)
    cs[:, 0:128] = np.eye(128)
    cs[:, 128:256] = same
    cs[:, 256:384] = 1.0
    for dirn in range(2):
        if dirn == 0:
            tri = same & (idx[:, None] <= idx[None, :])
            allow = same & (idx[None, :] >= idx[:, None])
            strict = same & (idx[None, :] > idx[:, None])
        else:
            tri = same & (idx[:, None] >= idx[None, :])
            allow = same & (idx[None, :] <= idx[:, None])
            strict = same & (idx[None, :] < idx[:, None])
        cs[:, 384 + dirn * 128:384 + (dirn + 1) * 128] = tri
        cs[:, 640 + dirn * 512:640 + (dirn + 1) * 512] = np.tile(np.where(allow, 0.0, -30000.0), (1, 4))
        cs[:, 1664 + dirn * 512:1664 + (dirn + 1) * 512] = np.tile(np.where(strict, -1.0, 0.0), (1, 4))
    cs[0:64, 2688:2816] = 1.0
    cs[64:128, 2816:2944] = 1.0
    cs[64, 2944:3072] = 1.0
    cs[0, 3072:3200] = 1.0
    return cs


WEIGHT_SPECS = [
    ("norm_ffn1", [DEPTH, D_MODEL]), ("ffn1_w_gate", [DEPTH, D_MODEL, D_FF]), ("ffn1_w_up", [DEPTH, D_MODEL, D_FF]),
    ("ffn1_w_down", [DEPTH, D_FF, D_MODEL]), ("norm_mix", [DEPTH, D_MODEL]), ("w_in", [DEPTH, D_MODEL, D_IN]),
    ("gdn_conv", [DEPTH, 1536, 5]), ("gdn_A_log", [DEPTH, 8]), ("gdn_dt_bias", [DEPTH, 8]), ("gdn_norm", [DEPTH, 128]),
    ("gdn_proj", [DEPTH, 512, D_MODEL]), ("mla_q_norm", [DEPTH, 384]), ("mla_w_uq", [DEPTH, 384, 768]),
    ("mla_kv_norm", [DEPTH, 256]), ("mla_w_ukv", [DEPTH, 256, 1024]), ("mla_proj", [DEPTH, 512, D_MODEL]),
    ("w_out", [DEPTH, D_MODEL, D_MODEL]), ("norm_ffn2", [DEPTH, D_MODEL]), ("ffn2_w_gate", [DEPTH, D_MODEL, D_FF]),
    ("ffn2_w_up", [DEPTH, D_MODEL, D_FF]), ("ffn2_w_down", [DEPTH, D_FF, D_MODEL]), ("final_norm", [D_MODEL]),
]


def build(cfg):
    import contextlib
    nc = bass.Bass("TRN2", target_bir_lowering=False)
    K = KB(nc)
    c = Ctx()
    c.nc = nc
    c.K = K
    c.ncores = cfg.get("ncores", 8)
    c.cc_sem = nc.alloc_semaphore("cc_sem")
    c.cc_cnt = 0
    d = {}

    def inp(name, shape, dtype=F32):
        d[name] = nc.dram_tensor(name, shape, dtype, kind="ExternalInput").ap()
        return d[name]

    inp("x", [T, D_MODEL])
    inp("pos_pm", [128, T // 128], I32)
    inp("inv_freq", [16])
    inp("consts", [128, CONST_COLS])
    inp("sel", [2])
    for name, shape in WEIGHT_SPECS:
        inp(name, shape)
    y_out = nc.dram_tensor("y", [T, D_MODEL], F32, kind="ExternalOutput").ap()

    def scratch(name, shape, dtype=F32):
        h = nc.dram_tensor(name, shape, dtype)
        d[name + "_h"] = h
        d[name] = h.ap()

    scratch("xres", [T, D_MODEL])
    scratch("qkvT", [1536, T + 4])
    scratch("zs", [T, 512])
    scratch("sgd", [T, 2048])
    scratch("bg", [T, 16])
    scratch("QT", [8, 128, T], BF16)
    for ch in range(T // min(1024, T)):
        scratch("lat_src%d" % ch, [min(1024, T), 288])
        scratch("lat_all%d" % ch, [2 * min(1024, T), 288])
    scratch("halo_src", [1536, 2])
    scratch("halo_all", [2 * 1536, 2])
    scratch("qkvn", [T, 1536])
    scratch("o1", [T, 512])
    scratch("st_src", [512, 128])
    scratch("st_all", [1024, 128])

    dump = cfg.get("dump", ())
    stop = cfg.get("stop", None)
    dbg = {}

    def dbg_out(name, shape, dtype=F32):
        dbg[name] = nc.dram_tensor("dbg_" + name, shape, dtype, kind="ExternalOutput").ap()
        return dbg[name]

    with contextlib.ExitStack() as stack:
        c.eps_t = sb(nc, stack, "eps_t", [128, 1], F32)
        c.one_t = sb(nc, stack, "one_t", [128, 1], F32)
        K.memset("dve", c.eps_t[:, :], EPS)
        K.memset("dve", c.one_t[:, :], 1.0)
        cst = sb(nc, stack, "cst", [128, CONST_COLS], F32)
        K.dma("sp", cst[:, :], dv(d["consts"]))

        def cview(lo, hi, shape3=None):
            t = TT(cst.h[:, lo:hi] if shape3 is None else cst.h[:, lo:hi].rearrange("p (h e) -> p h e", h=4), "cst")
            t.buf = cst.buf
            return t
        c.ident_f = cview(0, 128)
        c.same = cview(128, 256)
        c.ones_f = cview(256, 384)
        c.tri = [cview(384, 512), cview(512, 640)]
        c.negm4 = [cview(640, 1152), cview(1152, 1664)]
        c.nstr4 = [cview(1664, 2176, True), cview(2176, 2688, True)]
        c.ch = [cview(2688, 2816), cview(2816, 2944)]
        c.rowsel = [cview(2944, 3072), cview(3072, 3200)]
        c.ident_bf = sb(nc, stack, "ident_bf", [128, 128], BF16)
        K.copy("dve", c.ident_bf[:, :], c.ident_f[:, :])
        c.sel = sb(nc, stack, "sel", [128, 2], F32)
        K.dma("sp", c.sel[:, :], dv(d["sel"].partition_broadcast(128)))
        c.ps = []
        c.psb = []
        for i in range(8):
            h = stack.enter_context(nc.psum_tensor("ps%d" % i, [128, 512], F32))
            t = TT(h, "ps%d" % i)
            t.buf.x = True
            c.ps.append(t)
            tb = TT(h[:, :].bitcast(BF16), "psb%d" % i)
            tb.buf = t.buf
            c.psb.append(tb)
        prologue_rope(K, c, stack, d["pos_pm"], d["inv_freq"])

        def run():
            src = d["x"]
            for l in range(DEPTH):
                phase_ffn(K, c, stack, src, d["xres"], d["ffn1_w_gate"][l], d["ffn1_w_up"][l], d["ffn1_w_down"][l],
                          d["norm_ffn1"][l])
                src = d["xres"]
                if stop == ("f1", l):
                    return
                phase_w(K, c, l, d)
                if stop == ("w", l):
                    return
                phase_x1(K, c, l, d)
                if stop == ("x1", l):
                    return
                with contextlib.ExitStack() as lst:
                    aoT = sb(nc, lst, "aoT", [128, 4, T], BF16)
                    if cfg.get("skip_a"):
                        K.memset("dve", aoT[:, :, :], 0.0)
                    else:
                        phase_a(K, c, l, d, aoT)
                        if cfg.get("a_twice"):
                            phase_a(K, c, l, d, aoT)
                    if "aoT" in dump and l == 0:
                        K.dma("sp", dv(dbg_out("aoT", [128, 4, T], BF16)), aoT[:, :, :])
                    if stop == ("a", l):
                        K.barrier()
                        return
                    goT = sb(nc, lst, "goT", [128, 4, T], BF16)
                    S = sb(nc, lst, "Sst", [128, 4, 128], F32)
                    phase_g0(K, c, l, d)
                    if stop == ("g0", l):
                        return
                    K.memset("dve", S[:, :, :], 0.0)
                    phase_gscan(K, c, l, d, 0, S)
                    if "S1" in dump and l == 0:
                        K.dma("sp", dv(dbg_out("S1", [128, 4, 128])), S[:, :, :])
                    if stop == ("g1", l):
                        K.barrier()
                        return
                    phase_x2(K, c, l, d, S)
                    phase_gscan(K, c, l, d, 1, S, goT)
                    if "goT" in dump and l == 0:
                        K.dma("sp", dv(dbg_out("goT", [128, 4, T], BF16)), goT[:, :, :])
                    if stop == ("g2", l):
                        K.barrier()
                        return
                    phase_m(K, c, l, d, aoT, goT)
                if stop == ("m", l):
                    return
                last = (l == DEPTH - 1)
                phase_ffn(K, c, stack, d["xres"], d["xres"], d["ffn2_w_gate"][l], d["ffn2_w_up"][l], d["ffn2_w_down"][l],
                          d["norm_ffn2"][l], final_gain=d["final_norm"] if last else None, y_out=y_out if last else None)
                if stop == ("f2", l):
                    return

        run()
        K.barrier()
        for name in dump:
            if name in ("aoT", "goT", "S1"):
                continue
            src_ap = d[name]
            K.dma("sp", dv(dbg_out(name, list(src_ap.shape), src_ap.dtype)), dv(src_ap))
        K.barrier()
    print("instructions:", K.ninst, dict(K.ecnt))
    return nc


def shard_inputs(inputs, ncores=8):
    maps = []
    consts = make_consts()
    inv_freq = np.power(np.float32(10000.0), -np.arange(0, 32, 2, dtype=np.float32) / np.float32(32)).astype(np.float32)
    w = {k: np.asarray(inputs[k], dtype=np.float32) for k, _ in WEIGHT_SPECS}
    w["gdn_conv"] = np.ascontiguousarray(np.transpose(w["gdn_conv"], (0, 2, 1)))
    w["gdn_A_log"] = w["gdn_A_log"].reshape(DEPTH, 8)
    w["gdn_dt_bias"] = w["gdn_dt_bias"].reshape(DEPTH, 8)
    wr = dict(w)
    wi = w["w_in"].copy()
    for base in (2048, 2056):
        wi[:, :, base:base + 4] = w["w_in"][:, :, base + 4:base + 8]
        wi[:, :, base + 4:base + 8] = w["w_in"][:, :, base:base + 4]
    wr["w_in"] = wi
    wr["gdn_A_log"] = np.ascontiguousarray(w["gdn_A_log"].reshape(DEPTH, 2, 4)[:, ::-1].reshape(DEPTH, 8))
    wr["gdn_dt_bias"] = np.ascontiguousarray(w["gdn_dt_bias"].reshape(DEPTH, 2, 4)[:, ::-1].reshape(DEPTH, 8))
    wr["gdn_conv"] = np.ascontiguousarray(w["gdn_conv"][:, :, ::-1])
    S_ = inputs["x"].shape[1]
    Tl = S_ // 2
    for core in range(ncores):
        b, p = core // 2, core % 2
        xs = inputs["x"][b, p * Tl:(p + 1) * Tl]
        ps = inputs["positions"][b, p * Tl:(p + 1) * Tl]
        if p == 1:
            xs = xs[::-1]
            ps = ps[::-1]
        m = dict(w if p == 0 else wr)
        m["x"] = np.ascontiguousarray(xs, dtype=np.float32)
        m["pos_pm"] = np.ascontiguousarray(np.asarray(ps, dtype=np.int32).reshape(Tl // 128, 128).T)
        m["inv_freq"] = inv_freq
        m["consts"] = consts
        m["sel"] = np.array([0.0, 1.0] if p == 0 else [1.0, 0.0], np.float32)
        maps.append(m)
    return maps


def kernel(**inputs):
    inputs = {k: np.asarray(v) for k, v in inputs.items()}
    nc = build({})
    maps = shard_inputs(inputs)
    res = run_bass_kernel_spmd(nc, maps, core_ids=list(range(8)))
    out = np.empty((BATCH, SEQ, D_MODEL), np.float32)
    for core in range(8):
        b, p = core // 2, core % 2
        y = res.results[core]["y"]
        if p == 1:
            y = y[::-1]
        out[b, p * T:(p + 1) * T] = y
    return out
```

```python
import numpy as np
import concourse.bass as bass
import concourse.mybir as mybir
from concourse.bass_utils import run_bass_kernel_spmd

F32 = mybir.dt.float32
BF16 = mybir.dt.bfloat16
I32 = mybir.dt.int32
AF = mybir.ActivationFunctionType
ALU = mybir.AluOpType
AX = mybir.AxisListType

D_MODEL = 1024
BATCH = 4
SEQ = 8192
DEPTH = 2
T = SEQ // 2
NBLK = T // 128
D_FF = 2816
NFF = D_FF // 128
D_IN = 4784
EPS = 1e-6
NDS = 40


class Buf:
    __slots__ = ("name", "w", "r", "x")

    def __init__(self, name):
        self.name = name
        self.w = None
        self.r = {}
        self.x = False


class V:
    __slots__ = ("ap", "bufs")

    def __init__(self, ap, bufs):
        self.ap = ap
        self.bufs = bufs


class TT:
    def __init__(self, handle, name):
        self.h = handle
        self.buf = Buf(name)

    def __getitem__(self, idx):
        return V(self.h[idx], (self.buf,))


def dv(ap):
    return V(ap, ())


class KB:
    def __init__(self, nc):
        self.nc = nc
        self.engs = {"pe": nc.tensor, "dve": nc.vector, "act": nc.scalar, "pool": nc.gpsimd, "sp": nc.sync}
        self.esem = {k: nc.alloc_semaphore("e_" + k) for k in self.engs}
        self.ecnt = {k: 0 for k in self.engs}
        self.dsem = [nc.alloc_semaphore("d%d" % i) for i in range(NDS)]
        self.dcnt = [0] * NDS
        self.dnext = 0
        self.seen = {k: {} for k in self.engs}
        self.ninst = 0

    def _sem(self, key):
        return self.esem[key[1]] if key[0] == "e" else self.dsem[key[1]]

    def _wait(self, e, key, val):
        if self.seen[e].get(key, 0) >= val:
            return
        self.seen[e][key] = val
        self.engs[e].wait_ge(self._sem(key), val)
        self.ninst += 1

    def _deps(self, e, reads, writes):
        need = {}
        for v in reads:
            for b in v.bufs:
                if b.w is not None:
                    k, val = b.w
                    if need.get(k, 0) < val:
                        need[k] = val
                if b.x:
                    for k, val in b.r.items():
                        if k != ("e", e) and need.get(k, 0) < val:
                            need[k] = val
        for v in writes:
            for b in v.bufs:
                if b.w is not None:
                    k, val = b.w
                    if need.get(k, 0) < val:
                        need[k] = val
                for k, val in b.r.items():
                    if need.get(k, 0) < val:
                        need[k] = val
        for k, val in need.items():
            if k == ("e", "pe") and e == "pe":
                continue
            self._wait(e, k, val)

    def _mark(self, tok, reads, writes):
        k, val = tok
        for v in reads:
            for b in v.bufs:
                if b.r.get(k, 0) < val:
                    b.r[k] = val
        for v in writes:
            for b in v.bufs:
                b.w = tok
                b.r = {}

    def op(self, e, fn, reads=(), writes=()):
        self._deps(e, reads, writes)
        ins = fn(self.engs[e])
        self.ecnt[e] += 1
        ins.then_inc(self.esem[e], 1)
        self.ninst += 1
        self._mark((("e", e), self.ecnt[e]), reads, writes)
        return ins

    def dma(self, q, out, in_, **kw):
        i = self.dnext
        self.dnext = (i + 1) % NDS
        if self.dcnt[i] > 0:
            self._wait(q, ("d", i), self.dcnt[i])
        self._deps(q, (in_,), (out,))
        ins = self.engs[q].dma_start(out=out.ap, in_=in_.ap, **kw)
        self.dcnt[i] += 16
        ins.then_inc(self.dsem[i], 16)
        self.ninst += 1
        self._mark((("d", i), self.dcnt[i]), (in_,), (out,))
        return ins

    def barrier(self, engines=None):
        for e in (engines or self.engs):
            for k2 in self.engs:
                if k2 != e and self.ecnt[k2] > 0:
                    self._wait(e, ("e", k2), self.ecnt[k2])
            for i in range(NDS):
                if self.dcnt[i] > 0:
                    self._wait(e, ("d", i), self.dcnt[i])

    def mm(self, out, lhsT, rhs, start, stop):
        return self.op("pe", lambda t: t.matmul(out.ap, lhsT=lhsT.ap, rhs=rhs.ap, start=start, stop=stop),
                       reads=(lhsT, rhs), writes=(out,))

    def tr(self, out, in_, ident):
        return self.op("pe", lambda t: t.transpose(out.ap, in_.ap, ident.ap), reads=(in_, ident), writes=(out,))

    def act(self, out, in_, func, bias=None, scale=None, accum=None, e="act"):
        kw = {}
        reads = [in_]
        writes = [out]
        if bias is not None:
            if isinstance(bias, V):
                kw["bias"] = bias.ap
                reads.append(bias)
            else:
                kw["bias"] = bias
        if scale is not None:
            if isinstance(scale, V):
                kw["scale"] = scale.ap
                reads.append(scale)
            else:
                kw["scale"] = scale
        if accum is not None:
            kw["accum_out"] = accum.ap
            writes.append(accum)
        return self.op(e, lambda a: a.activation(out=out.ap, in_=in_.ap, func=func, **kw), reads=reads, writes=writes)

    def tt(self, e, out, in0, in1, op):
        return self.op(e, lambda g: g.tensor_tensor(out=out.ap, in0=in0.ap, in1=in1.ap, op=op),
                       reads=(in0, in1), writes=(out,))

    def ts(self, e, out, in0, s1, op0, s2=None, op1=None):
        reads = [in0]
        a1 = s1
        a2 = s2
        if isinstance(s1, V):
            reads.append(s1)
            a1 = s1.ap
        if isinstance(s2, V):
            reads.append(s2)
            a2 = s2.ap
        kw = {}
        if op1 is not None:
            kw["op1"] = op1
        return self.op(e, lambda g: g.tensor_scalar(out=out.ap, in0=in0.ap, scalar1=a1, scalar2=a2, op0=op0, **kw),
                       reads=reads, writes=(out,))

    def stt(self, e, out, in0, scalar, in1, op0, op1):
        reads = [in0, in1]
        a = scalar
        if isinstance(scalar, V):
            reads.append(scalar)
            a = scalar.ap
        return self.op(e, lambda g: g.scalar_tensor_tensor(out=out.ap, in0=in0.ap, scalar=a, in1=in1.ap, op0=op0, op1=op1),
                       reads=reads, writes=(out,))

    def copy(self, e, out, in_):
        if e == "act":
            return self.op(e, lambda g: g.copy(out=out.ap, in_=in_.ap), reads=(in_,), writes=(out,))
        return self.op(e, lambda g: g.tensor_copy(out=out.ap, in_=in_.ap), reads=(in_,), writes=(out,))

    def memset(self, e, out, val):
        return self.op(e, lambda g: g.memset(out.ap, val), reads=(), writes=(out,))

    def recip(self, out, in_):
        return self.op("dve", lambda g: g.reciprocal(out=out.ap, in_=in_.ap), reads=(in_,), writes=(out,))


class Ctx:
    pass


_uid = [0]


def sb(nc, stack, name, shape, dtype):
    _uid[0] += 1
    name = "%s_%d" % (name, _uid[0])
    h = stack.enter_context(nc.sbuf_tensor(name, shape, dtype))
    return TT(h, name)


def load_cast(K, c, stg, dst_views, src_aps, cast_engs=("dve", "act")):
    qs = ("sp", "act", "pool")
    for i, (d, s) in enumerate(zip(dst_views, src_aps)):
        st = stg[i % len(stg)]
        shp = list(s.shape)
        sv = st[:shp[0], :shp[1]] if len(shp) == 2 else st[:shp[0], :shp[1], :shp[2]]
        K.dma(qs[i % 3], sv, dv(s))
        K.copy(cast_engs[i % len(cast_engs)], d, sv)


def rms_rstd(K, c, x_v, ss, rstd, junk, n, e_sq="act"):
    K.act(junk, x_v, AF.Square, accum=ss[:, 0:1])
    K.act(rstd[:, 0:1], ss[:, 0:1], AF.Sqrt, bias=c.eps_t[:, 0:1], scale=1.0 / n)
    K.recip(rstd[:, 0:1], rstd[:, 0:1])


def phase_ffn(K, c, stack_outer, x_src, x_dst, w_gate, w_up, w_down, gain, final_gain=None, y_out=None):
    import contextlib
    nc = c.nc
    TTK = 256
    NB = TTK // 128
    with contextlib.ExitStack() as stack:
        wg = sb(nc, stack, "wg", [128, 8, D_FF], BF16)
        wu = sb(nc, stack, "wu", [128, 8, D_FF], BF16)
        wd = sb(nc, stack, "wd", [128, NFF, D_MODEL], BF16)
        grow = sb(nc, stack, "grow", [128, D_MODEL], F32)
        K.dma("sp", grow[:, :], dv(gain.partition_broadcast(128)))
        if final_gain is not None:
            fgrow = sb(nc, stack, "fgrow", [128, D_MODEL], F32)
            K.dma("sp", fgrow[:, :], dv(final_gain.partition_broadcast(128)))
        with contextlib.ExitStack() as st2:
            stg = [sb(nc, st2, "stg%d" % i, [128, D_FF], F32) for i in range(3)]
            dsts, srcs = [], []
            for kc in range(8):
                dsts.append(wg[:, kc, :]); srcs.append(w_gate[kc * 128:(kc + 1) * 128, :])
                dsts.append(wu[:, kc, :]); srcs.append(w_up[kc * 128:(kc + 1) * 128, :])
            for j in range(NFF):
                dsts.append(wd[:, j, :]); srcs.append(w_down[j * 128:(j + 1) * 128, :])
            load_cast(K, c, stg, dsts, srcs)
            K.barrier()
        xt = [sb(nc, stack, "xt%d" % i, [128, NB, D_MODEL], F32) for i in range(2)]
        xn = [sb(nc, stack, "xn%d" % i, [128, D_MODEL], BF16) for i in range(2)]
        junk = sb(nc, stack, "junk", [128, D_MODEL], F32)
        hT = sb(nc, stack, "hT", [128, 8, TTK], BF16)
        actT = sb(nc, stack, "actT", [128, NFF, TTK], BF16)
        sg = [sb(nc, stack, "sg%d" % i, [128, TTK], F32) for i in range(2)]
        ss = [sb(nc, stack, "ss%d" % i, [128, 1], F32) for i in range(2)]
        rstd = [sb(nc, stack, "rstd%d" % i, [128, 1], F32) for i in range(2)]
        ntile = T // TTK
        xs_t = x_src.rearrange("(n b p) d -> n p b d", p=128, b=NB)
        xd_t = x_dst.rearrange("(n b p) d -> n p b d", p=128, b=NB)
        if y_out is not None:
            yo_t = y_out.rearrange("(n b p) d -> n p b d", p=128, b=NB)
        K.dma("sp", xt[0][:, :, :], dv(xs_t[0]))
        for i in range(ntile):
            x = xt[i % 2]
            if i + 1 < ntile:
                K.dma("sp", xt[(i + 1) % 2][:, :, :], dv(xs_t[i + 1]))
            for b in range(NB):
                rms_rstd(K, c, x[:, b, :], ss[b], rstd[b], junk[:, :], D_MODEL)
                K.stt("dve", xn[b][:, :], x[:, b, :], rstd[b][:, 0:1], grow[:, :], ALU.mult, ALU.mult)
                pt = c.psb[b]
                for kc in range(8):
                    K.tr(pt[:, kc * 128:(kc + 1) * 128], xn[b][:, kc * 128:(kc + 1) * 128], c.ident_bf[:, :])
                K.copy("act" if b == 0 else "dve", hT[:, :, b * 128:(b + 1) * 128],
                       V(pt.h[:, :].rearrange("p (k t) -> p k t", k=8), (pt.buf,)))
            for j in range(NFF):
                pg = c.ps[2 + (j % 3)]
                for kc in range(8):
                    K.mm(pg[:, 0:TTK], wg[:, kc, j * 128:(j + 1) * 128], hT[:, kc, :], kc == 0, kc == 7)
                for kc in range(8):
                    K.mm(pg[:, TTK:2 * TTK], wu[:, kc, j * 128:(j + 1) * 128], hT[:, kc, :], kc == 0, kc == 7)
                s = sg[j % 2]
                K.act(s[:, :], pg[:, 0:TTK], AF.Silu)
                K.tt("dve", actT[:, j, :], s[:, :], pg[:, TTK:2 * TTK], ALU.mult)
            for b in range(NB):
                for h in range(2):
                    po = c.ps[5 + ((b * 2 + h) % 3)]
                    for j in range(NFF):
                        K.mm(po[:, :], actT[:, j, b * 128:(b + 1) * 128], wd[:, j, h * 512:(h + 1) * 512], j == 0, j == NFF - 1)
                    K.stt("dve", x[:, b, h * 512:(h + 1) * 512], po[:, :], 0.5, x[:, b, h * 512:(h + 1) * 512], ALU.mult, ALU.add)
            if final_gain is None:
                K.dma("pool", dv(xd_t[i]), x[:, :, :])
            else:
                for b in range(NB):
                    rms_rstd(K, c, x[:, b, :], ss[b], rstd[b], junk[:, :], D_MODEL)
                    K.stt("dve", x[:, b, :], x[:, b, :], rstd[b][:, 0:1], fgrow[:, :], ALU.mult, ALU.mult)
                    K.dma("pool", dv(yo_t[i][:, b, :]), x[:, b, :])
        K.barrier()


MLA_SCALE = 96.0 ** -0.5
TWO_PI = 6.283185307179586
CW1 = 6.28125
CW2 = TWO_PI - CW1
MAGIC = 12582912.0
PI_LIM = 3.1415925


def prologue_rope(K, c, stack, pos_pm, invf):
    import contextlib
    nc = c.nc
    nb = T // 128
    c.cos = sb(nc, stack, "cos", [128, nb, 16], F32)
    c.sin = sb(nc, stack, "sin", [128, nb, 16], F32)
    c.cos_s = sb(nc, stack, "cos_s", [128, nb, 16], F32)
    c.sin_s = sb(nc, stack, "sin_s", [128, nb, 16], F32)
    with contextlib.ExitStack() as st:
        pos_i = sb(nc, st, "pos_i", [128, nb], I32)
        posf = sb(nc, st, "posf", [128, nb], F32)
        ivf = sb(nc, st, "ivf", [128, 16], F32)
        ang = sb(nc, st, "ang", [128, nb, 16], F32)
        u = sb(nc, st, "u", [128, nb, 16], F32)
        n = sb(nc, st, "n", [128, nb, 16], F32)
        r = sb(nc, st, "r", [128, nb, 16], F32)
        K.dma("sp", pos_i[:, :], dv(pos_pm))
        K.dma("sp", ivf[:, :], dv(invf.partition_broadcast(128)))
        K.copy("dve", posf[:, :], pos_i[:, :])
        pf_b = V(posf.h[:, :].unsqueeze(2).to_broadcast([128, nb, 16]), (posf.buf,))
        iv_b = V(ivf.h[:, :].unsqueeze(1).to_broadcast([128, nb, 16]), (ivf.buf,))
        K.tt("dve", ang[:, :, :], pf_b, iv_b, ALU.mult)
        for which, dst, dst_s in (("sin", c.sin, c.sin_s), ("cos", c.cos, c.cos_s)):
            off = 0.0 if which == "sin" else 0.25
            K.ts("dve", u[:, :, :], ang[:, :, :], 1.0 / TWO_PI, ALU.mult, off, ALU.add)
            K.ts("dve", n[:, :, :], u[:, :, :], MAGIC, ALU.add)
            K.ts("dve", n[:, :, :], n[:, :, :], MAGIC, ALU.subtract)
            K.stt("dve", r[:, :, :], n[:, :, :], -CW1, ang[:, :, :], ALU.mult, ALU.add)
            K.stt("dve", r[:, :, :], n[:, :, :], -CW2, r[:, :, :], ALU.mult, ALU.add)
            if which == "cos":
                K.ts("dve", r[:, :, :], r[:, :, :], TWO_PI / 4, ALU.add)
            K.ts("dve", r[:, :, :], r[:, :, :], PI_LIM, ALU.min, -PI_LIM, ALU.max)
            K.act(dst[:, :, :], r[:, :, :], AF.Sin)
            K.ts("dve", dst_s[:, :, :], dst[:, :, :], MLA_SCALE, ALU.mult)
        K.barrier()


def rope_tm(K, c, e, out, xin, cs, sn, nh, tmp):
    def bc(v):
        return V(v.ap.unsqueeze(1).to_broadcast([128, nh, 16]), v.bufs)
    x1 = V(xin.ap[:, :, 0:16], xin.bufs)
    x2 = V(xin.ap[:, :, 16:32], xin.bufs)
    o1 = V(out.ap[:, :, 0:16], out.bufs)
    o2 = V(out.ap[:, :, 16:32], out.bufs)
    t = [V(tmp.h[:, i, 0:nh, :], (tmp.buf,)) for i in range(4)]
    K.tt(e, t[0], x1, bc(cs), ALU.mult)
    K.tt(e, t[1], x2, bc(sn), ALU.mult)
    K.tt(e, t[2], x2, bc(cs), ALU.mult)
    K.tt(e, t[3], x1, bc(sn), ALU.mult)
    K.tt(e, o1, t[0], t[1], ALU.subtract)
    K.tt(e, o2, t[2], t[3], ALU.add)


def phase_w(K, c, l, d):
    import contextlib
    nc = c.nc
    TW = 256
    NB = TW // 128
    with contextlib.ExitStack() as stack:
        win = sb(nc, stack, "win", [128, 8, D_IN], BF16)
        wuq = sb(nc, stack, "wuq", [128, 3, 768], BF16)
        grow = sb(nc, stack, "grow_w", [128, D_MODEL], F32)
        qnrow = sb(nc, stack, "qnrow", [128, 384], F32)
        kvnrow = sb(nc, stack, "kvnrow", [128, 256], F32)
        dtbrow = sb(nc, stack, "dtbrow", [128, 8], F32)
        negA = sb(nc, stack, "negA", [128, 8], F32)
        K.dma("sp", grow[:, :], dv(d["norm_mix"][l].partition_broadcast(128)))
        K.dma("sp", qnrow[:, :], dv(d["mla_q_norm"][l].partition_broadcast(128)))
        K.dma("sp", kvnrow[:, :], dv(d["mla_kv_norm"][l].partition_broadcast(128)))
        K.dma("sp", dtbrow[:, :], dv(d["gdn_dt_bias"][l].partition_broadcast(128)))
        K.dma("sp", negA[:, :], dv(d["gdn_A_log"][l].partition_broadcast(128)))
        K.act(negA[:, :], negA[:, :], AF.Exp)
        K.ts("dve", negA[:, :], negA[:, :], -1.0, ALU.mult)
        with contextlib.ExitStack() as st2:
            stg = [sb(nc, st2, "stgw%d" % i, [128, D_IN], F32) for i in range(2)]
            dsts = [win[:, kc, :] for kc in range(8)] + [wuq[:, kc, :] for kc in range(3)]
            srcs = [d["w_in"][l][kc * 128:(kc + 1) * 128, :] for kc in range(8)] + \
                   [d["mla_w_uq"][l][kc * 128:(kc + 1) * 128, :] for kc in range(3)]
            load_cast(K, c, stg, dsts, srcs)
            K.barrier()
        xt = [sb(nc, stack, "xw%d" % i, [128, NB, D_MODEL], F32) for i in range(2)]
        xn = [sb(nc, stack, "xnw%d" % i, [128, D_MODEL], BF16) for i in range(2)]
        junk = sb(nc, stack, "junkw", [128, D_MODEL], F32)
        hT = sb(nc, stack, "hTw", [128, 8, TW], BF16)
        ss = [sb(nc, stack, "ssw%d" % i, [128, 1], F32) for i in range(2)]
        rstd = [sb(nc, stack, "rstdw%d" % i, [128, 1], F32) for i in range(2)]
        qkvst = sb(nc, stack, "qkvst", [128, 12, TW], F32)
        zs_t = [sb(nc, stack, "zs_t%d" % i, [128, 512], F32) for i in range(2)]
        sg_t = [sb(nc, stack, "sg_t%d" % i, [128, 2048], F32) for i in range(2)]
        bg_t = [sb(nc, stack, "bg_t%d" % i, [128, 16], F32) for i in range(2)]
        sp_t = sb(nc, stack, "sp_t", [128, 4, 8], F32)
        cqn = sb(nc, stack, "cqn", [128, 384], BF16)
        cqnT = sb(nc, stack, "cqnT", [128, 3, 128], BF16)
        qs = sb(nc, stack, "qs", [128, 8, 128], BF16)
        K.memset("pool", qs[:, :, :], 0.0)
        qT_t = [sb(nc, stack, "qT_t%d" % i, [128, 8, 128], BF16) for i in range(2)]
        lat_t = [sb(nc, stack, "lat_t%d" % i, [128, 288], F32) for i in range(2)]
        rtmp = sb(nc, stack, "rtmp", [128, 4, 8, 16], F32)
        ss2 = sb(nc, stack, "ss2", [128, 2], F32)
        rs2 = sb(nc, stack, "rs2", [128, 2], F32)
        ntile = T // TW
        xs_t = d["xres"].rearrange("(n b p) d -> n p b d", p=128, b=NB)
        qkvT_v = d["qkvT"].rearrange("(c p) t -> p c t", p=128)
        QT_v = d["QT"].rearrange("h d t -> d h t")
        K.dma("sp", xt[0][:, :, :], dv(xs_t[0]))
        for i in range(ntile):
            x = xt[i % 2]
            if i + 1 < ntile:
                K.dma("sp", xt[(i + 1) % 2][:, :, :], dv(xs_t[i + 1]))
            for b in range(NB):
                rms_rstd(K, c, x[:, b, :], ss[b % 2], rstd[b % 2], junk[:, :], D_MODEL)
                K.stt("dve", xn[b % 2][:, :], x[:, b, :], rstd[b % 2][:, 0:1], grow[:, :], ALU.mult, ALU.mult)
                pt = c.psb[b % 2]
                for kc in range(8):
                    K.tr(pt[:, kc * 128:(kc + 1) * 128], xn[b % 2][:, kc * 128:(kc + 1) * 128], c.ident_bf[:, :])
                K.copy("act" if b % 2 == 0 else "dve", hT[:, :, b * 128:(b + 1) * 128],
                       V(pt.h[:, :].rearrange("p (k t) -> p k t", k=8), (pt.buf,)))
            for j in range(12):
                pf = c.ps[2 + (j % 2)]
                for kc in range(8):
                    K.mm(pf[:, 0:TW], win[:, kc, j * 128:(j + 1) * 128], hT[:, kc, :], kc == 0, kc == 7)
                K.copy("act" if j % 2 == 0 else "dve", qkvst[:, j, :], pf[:, 0:TW])
            K.dma("pool", dv(qkvT_v[:, :, 2 + i * TW:2 + (i + 1) * TW]), qkvst[:, :, :])
            for b in range(NB):
                blk = i * NB + b
                hb = lambda kc: hT[:, kc, b * 128:(b + 1) * 128]
                pz = c.ps[4]
                for kc in range(8):
                    K.mm(pz[:, :], hb(kc), win[:, kc, 1536:2048], kc == 0, kc == 7)
                z = zs_t[blk % 2]
                K.act(z[:, :], pz[:, :], AF.Silu)
                K.dma("pool", dv(d["zs"][blk * 128:(blk + 1) * 128, :]), z[:, :])
                sgt = sg_t[blk % 2]
                for g4 in range(4):
                    pgt = c.ps[5 + (g4 % 2)]
                    for kc in range(8):
                        K.mm(pgt[:, :], hb(kc), win[:, kc, 2736 + g4 * 512:2736 + (g4 + 1) * 512], kc == 0, kc == 7)
                    K.act(sgt[:, g4 * 512:(g4 + 1) * 512], pgt[:, :], AF.Sigmoid)
                K.dma("pool", dv(d["sgd"][blk * 128:(blk + 1) * 128, :]), sgt[:, :])
                p2 = c.ps[7]
                for kc in range(8):
                    K.mm(p2[:, 0:400], hb(kc), win[:, kc, 2048:2448], kc == 0, kc == 7)
                p3 = c.ps[4]
                bgt = bg_t[blk % 2]
                K.act(bgt[:, 0:8], p2[:, 0:8], AF.Sigmoid)
                tt_ = sp_t[:, 0, :]
                K.tt("dve", tt_, p2[:, 8:16], dtbrow[:, :], ALU.add)
                ab = sp_t[:, 1, :]
                K.act(ab, tt_, AF.Abs)
                ee = sp_t[:, 2, :]
                K.act(ee, ab, AF.Exp, scale=-1.0)
                K.act(ee, ee, AF.Ln, bias=c.one_t[:, 0:1])
                sp_ = sp_t[:, 3, :]
                K.stt("dve", sp_, tt_, 0.0, ee, ALU.max, ALU.add)
                K.tt("dve", bgt[:, 8:16], sp_, negA[:, :], ALU.mult)
                K.dma("pool", dv(d["bg"][blk * 128:(blk + 1) * 128, :]), bgt[:, :])
                K.act(junk[:, 0:384], p2[:, 16:400], AF.Square, accum=ss2[:, 0:1])
                K.act(rs2[:, 0:1], ss2[:, 0:1], AF.Sqrt, bias=c.eps_t[:, 0:1], scale=1.0 / 384)
                K.recip(rs2[:, 0:1], rs2[:, 0:1])
                K.stt("dve", cqn[:, :], p2[:, 16:400], rs2[:, 0:1], qnrow[:, :], ALU.mult, ALU.mult)
                ptq = c.psb[0]
                for kc in range(3):
                    K.tr(ptq[:, kc * 128:(kc + 1) * 128], cqn[:, kc * 128:(kc + 1) * 128], c.ident_bf[:, :])
                K.copy("act", cqnT[:, :, :], V(ptq.h[:, 0:384].rearrange("p (k t) -> p k t", k=3), (ptq.buf,)))
                for kc in range(8):
                    K.mm(p3[:, 0:288], hb(kc), win[:, kc, 2448:2736], kc == 0, kc == 7)
                for hh in range(2):
                    pq = c.ps[5 + hh]
                    for kc in range(3):
                        K.mm(pq[:, 0:384], cqnT[:, kc, :], wuq[:, kc, hh * 384:(hh + 1) * 384], kc == 0, kc == 2)
                    pqv = V(pq.h[:, 0:384].rearrange("p (h e) -> p h e", h=4), (pq.buf,))
                    K.ts("dve", qs[:, hh * 4:(hh + 1) * 4, 0:64], V(pqv.ap[:, :, 0:64], pqv.bufs), MLA_SCALE, ALU.mult)
                    rope_tm(K, c, "dve", qs[:, hh * 4:(hh + 1) * 4, 64:96], V(pqv.ap[:, :, 64:96], pqv.bufs),
                            c.cos_s[:, blk, :], c.sin_s[:, blk, :], 4, rtmp)
                pqt = c.psb[1]
                for h in range(8):
                    K.tr(pqt[:, h * 128:(h + 1) * 128], qs[:, h, :], c.ident_bf[:, :])
                qTt = qT_t[blk % 2]
                K.copy("act", qTt[:, :, :], V(pqt.h[:, :].rearrange("p (h t) -> p h t", h=8), (pqt.buf,)))
                K.dma("pool", dv(QT_v[:, :, blk * 128:(blk + 1) * 128]), qTt[:, :, :])
                lt = lat_t[blk % 2]
                K.act(junk[:, 0:256], p3[:, 0:256], AF.Square, accum=ss2[:, 1:2])
                K.act(rs2[:, 1:2], ss2[:, 1:2], AF.Sqrt, bias=c.eps_t[:, 0:1], scale=1.0 / 256)
                K.recip(rs2[:, 1:2], rs2[:, 1:2])
                K.stt("dve", lt[:, 0:256], p3[:, 0:256], rs2[:, 1:2], kvnrow[:, :], ALU.mult, ALU.mult)
                rope_tm(K, c, "dve", V(lt.h[:, 256:288].rearrange("p (h e) -> p h e", h=1), (lt.buf,)),
                        V(p3.h[:, 256:288].rearrange("p (h e) -> p h e", h=1), (p3.buf,)),
                        c.cos[:, blk, :], c.sin[:, blk, :], 1, rtmp)
                CR = min(1024, T)
                K.dma("pool", dv(d["lat_src%d" % (blk * 128 // CR)][(blk * 128) % CR:(blk * 128) % CR + 128, :]), lt[:, :])
        K.barrier()
        hl = sb(nc, stack, "hl", [128, 12, 2], F32)
        K.dma("sp", hl[:, :, :], dv(qkvT_v[:, :, T:T + 2]))
        K.dma("sp", dv(d["halo_src"].rearrange("(c p) t -> p c t", p=128)), hl[:, :, :])
        K.barrier()


def collective_gather(K, c, src_h, dst_h):
    nc = c.nc
    K.barrier()
    groups = [[2 * i, 2 * i + 1] for i in range(c.ncores // 2)]
    ins = nc.gpsimd.collective_compute("AllGather", ALU.bypass, replica_groups=groups,
                                       ins=[src_h.ap().opt()], outs=[dst_h.ap().opt()])
    c.cc_cnt += 1
    ins.then_inc(c.cc_sem)
    K.ninst += 1
    for e in K.engs:
        K.engs[e].wait_ge(c.cc_sem, c.cc_cnt)


def sel_other(K, c, e, out, slot0, slot1):
    K.ts(e, out, slot0, c.sel[:, 0:1], ALU.mult)
    K.stt(e, out, slot1, c.sel[:, 1:2], out, ALU.mult, ALU.add)


def phase_x1(K, c, l, d):
    import contextlib
    nc = c.nc
    for ch in range(T // min(1024, T)):
        collective_gather(K, c, d["lat_src%d_h" % ch], d["lat_all%d_h" % ch])
    collective_gather(K, c, d["halo_src_h"], d["halo_all_h"])
    with contextlib.ExitStack() as stack:
        ha = sb(nc, stack, "ha", [128, 2, 12, 2], F32)
        ho = sb(nc, stack, "ho", [128, 12, 2], F32)
        hz = sb(nc, stack, "hz", [128, 12, 2], F32)
        K.dma("sp", ha[:, :, :, :], dv(d["halo_all"].rearrange("(s c p) t -> p s c t", p=128, s=2)))
        sel_other(K, c, "dve", ho[:, :, :], ha[:, 0, :, :], ha[:, 1, :, :])
        qkvT_v = d["qkvT"].rearrange("(c p) t -> p c t", p=128)
        K.dma("sp", dv(qkvT_v[:, :, T + 2:T + 3]), ho[:, :, 1:2], allow_slow_non_contiguous=True)
        K.dma("sp", dv(qkvT_v[:, :, T + 3:T + 4]), ho[:, :, 0:1], allow_slow_non_contiguous=True)
        K.memset("dve", hz[:, :, :], 0.0)
        K.dma("sp", dv(qkvT_v[:, :, 0:2]), hz[:, :, :])
        K.barrier()


def phase_a(K, c, l, d, aoT):
    import contextlib
    nc = c.nc
    NK = 2 * T
    NKB = NK // 128
    NKT = NK // 512
    QW = 512 if T >= 512 else T
    NQT = T // QW
    with contextlib.ExitStack() as stack:
        wukv = sb(nc, stack, "wukv", [128, 2, 1024], BF16)
        ckvnT = sb(nc, stack, "ckvnT", [128, 2, NK], BF16)
        KT = [sb(nc, stack, "KT%d" % i, [128, NK], BF16) for i in range(2)]
        Vh = [sb(nc, stack, "Vh%d" % i, [128, NKB, 128], BF16) for i in range(2)]
        QTh = [sb(nc, stack, "QTh%d" % i, [128, T], BF16) for i in range(2)]
        pT = [sb(nc, stack, "pT%d" % i, [128, QW], BF16) for i in range(3)]
        rs = sb(nc, stack, "rs_a", [128, QW], F32)
        bc = sb(nc, stack, "bc_a", [128, QW], F32)
        latf = [sb(nc, stack, "latf%d" % i, [128, 288], F32) for i in range(3)]
        latb = [sb(nc, stack, "latb%d" % i, [128, 320], BF16) for i in range(2)]
        for i in range(2):
            K.memset("pool", latb[i][:, 288:320], 0.0)
        K.memset("pool", rs[:, :], 0.0)
        with contextlib.ExitStack() as st2:
            stg = [sb(nc, st2, "stga%d" % i, [128, 1024], F32) for i in range(2)]
            load_cast(K, c, stg, [wukv[:, kc, :] for kc in range(2)],
                      [d["mla_w_ukv"][l][kc * 128:(kc + 1) * 128, :] for kc in range(2)])
            K.barrier()
        import os
        amode = int(os.environ.get("AMODE", "0"))
        if amode == 4:
            K.barrier()
            return
        for p in range(2):
            K.memset("pool", Vh[p][:, :, :], 0.0)
        K.memset("pool", Vh[0][:, :, 64:65], 1.0)
        K.memset("pool", Vh[1][:, :, 0:1], 1.0)
        if amode == 5:
            K.barrier()
            return
        for kb in range(NKB):
            lf = latf[kb % 3]
            lb = latb[kb % 2]
            CR2 = 2 * min(1024, T)
            K.dma("sp" if kb % 2 == 0 else "act", lf[:, :],
                  dv(d["lat_all%d" % (kb * 128 // CR2)][(kb * 128) % CR2:(kb * 128) % CR2 + 128, :]))
            K.copy("pool", lb[:, 0:288], lf[:, :])
            pt = c.psb[5 + (kb % 2)]
            for kc in range(2):
                K.tr(pt[:, kc * 128:(kc + 1) * 128], lb[:, kc * 128:(kc + 1) * 128], c.ident_bf[:, :])
            pk_ = c.psb[3 + (kb % 2)]
            K.tr(pk_[:, 0:128], lb[:, 192:320], c.ident_bf[:, :])
            K.copy("dve", ckvnT[:, :, kb * 128:(kb + 1) * 128],
                   V(pt.h[:, 0:256].rearrange("p (k t) -> p k t", k=2), (pt.buf,)))
            K.copy("act", KT[0][64:128, kb * 128:(kb + 1) * 128], pk_[64:128, 0:128])
            K.copy("dve", KT[1][64:128, kb * 128:(kb + 1) * 128], pk_[64:128, 0:128])
        QT_d = d["QT"]
        if amode == 1:
            K.barrier()
            return
        K.dma("sp", QTh[0][:, :], dv(QT_d[0]))
        cnt = 0
        for h in range(8):
            par = h % 2
            kt_ = KT[par]
            vh = Vh[par]
            if h + 1 < 8:
                K.dma("sp", QTh[(h + 1) % 2][:, :], dv(QT_d[h + 1]))
            qh = QTh[h % 2]
            for kt in range(NKT):
                pk = c.ps[5 + (kt % 2)]
                for kc in range(2):
                    K.mm(pk[:, :], wukv[:, kc, h * 128:h * 128 + 128], ckvnT[:, kc, kt * 512:(kt + 1) * 512], kc == 0, kc == 1)
                K.copy("dve", kt_[0:64, kt * 512:(kt + 1) * 512], pk[0:64, :])
            voff = 0 if par == 0 else 64
            for k8 in range(NKB // 8):
                pv = c.ps[5 + (k8 % 2)]
                for j in range(8):
                    kb = k8 * 8 + j
                    for kc in range(2):
                        K.mm(pv[:, j * 64:(j + 1) * 64], ckvnT[:, kc, kb * 128:(kb + 1) * 128],
                             wukv[:, kc, h * 128 + 64:h * 128 + 128], kc == 0, kc == 1)
                K.copy("dve", vh[:, k8 * 8:(k8 + 1) * 8, voff:voff + 64],
                       V(pv.h[:, :].rearrange("p (j e) -> p j e", j=8), (pv.buf,)))
            srow = 64 if par == 0 else 0
            if amode == 2:
                continue
            for qt in range(NQT):
                po = c.ps[3 + (cnt % 2)]
                cnt += 1
                qv = qh[:, qt * QW:(qt + 1) * QW]
                K.mm(c.ps[0][:, 0:QW], kt_[:, 0:128], qv, True, True)
                for kb in range(NKB):
                    if kb + 1 < NKB:
                        K.mm(c.ps[(kb + 1) % 3][:, 0:QW], kt_[:, (kb + 1) * 128:(kb + 2) * 128], qv, True, True)
                    K.act(pT[kb % 3][:, :], c.ps[kb % 3][:, 0:QW], AF.Exp)
                    K.mm(po[:, 0:QW], vh[:, kb, :], pT[kb % 3][:, :], kb == 0, kb == NKB - 1)
                if amode == 3:
                    continue
                K.recip(rs[srow:srow + 1, :], po[srow:srow + 1, 0:QW])
                pb = c.ps[7]
                K.mm(pb[:, 0:QW], c.rowsel[par][:, :], rs[:, :], True, True)
                K.copy("act", bc[:, :], pb[:, 0:QW])
                K.tt("dve", aoT[voff:voff + 64, h // 2, qt * QW:(qt + 1) * QW], po[voff:voff + 64, 0:QW],
                     bc[voff:voff + 64, :], ALU.mult)
        K.barrier()


def phase_m(K, c, l, d, aoT, goT):
    import contextlib
    nc = c.nc
    with contextlib.ExitStack() as stack:
        wpa = sb(nc, stack, "wpa", [128, 4, D_MODEL], BF16)
        wpb = sb(nc, stack, "wpb", [128, 4, D_MODEL], BF16)
        wo = sb(nc, stack, "wo", [128, 8, D_MODEL], BF16)
        with contextlib.ExitStack() as st2:
            stg = [sb(nc, st2, "stgm%d" % i, [128, D_MODEL], F32) for i in range(3)]
            dsts = [wpa[:, kc, :] for kc in range(4)] + [wpb[:, kc, :] for kc in range(4)] + [wo[:, kc, :] for kc in range(8)]
            srcs = [d["gdn_proj"][l][kc * 128:(kc + 1) * 128, :] for kc in range(4)] + \
                   [d["mla_proj"][l][kc * 128:(kc + 1) * 128, :] for kc in range(4)] + \
                   [d["w_out"][l][kc * 128:(kc + 1) * 128, :] for kc in range(8)]
            load_cast(K, c, stg, dsts, srcs)
            K.barrier()
        xt = [sb(nc, stack, "xm%d" % i, [128, D_MODEL], F32) for i in range(2)]
        sgt = [sb(nc, stack, "sgm%d" % i, [128, 2048], F32) for i in range(2)]
        ya = sb(nc, stack, "ya", [128, D_MODEL], F32)
        yb = sb(nc, stack, "yb", [128, D_MODEL], BF16)
        yT = sb(nc, stack, "yT", [128, 8, 128], BF16)
        nb = T // 128
        K.dma("sp", xt[0][:, :], dv(d["xres"][0:128, :]))
        K.dma("act", sgt[0][:, :], dv(d["sgd"][0:128, :]))
        for blk in range(nb):
            x = xt[blk % 2]
            sg = sgt[blk % 2]
            if blk + 1 < nb:
                K.dma("sp", xt[(blk + 1) % 2][:, :], dv(d["xres"][(blk + 1) * 128:(blk + 2) * 128, :]))
                K.dma("act", sgt[(blk + 1) % 2][:, :], dv(d["sgd"][(blk + 1) * 128:(blk + 2) * 128, :]))
            for hf in range(2):
                pa = c.ps[hf]
                pb = c.ps[2 + hf]
                for kc in range(4):
                    K.mm(pa[:, :], goT[:, kc, blk * 128:(blk + 1) * 128], wpa[:, kc, hf * 512:(hf + 1) * 512], kc == 0, kc == 3)
                for kc in range(4):
                    K.mm(pb[:, :], aoT[:, kc, blk * 128:(blk + 1) * 128], wpb[:, kc, hf * 512:(hf + 1) * 512], kc == 0, kc == 3)
                K.tt("dve", ya[:, hf * 512:(hf + 1) * 512], pa[:, :], sg[:, hf * 512:(hf + 1) * 512], ALU.mult)
                K.tt("dve", sg[:, 1024 + hf * 512:1024 + (hf + 1) * 512], pb[:, :], sg[:, 1024 + hf * 512:1024 + (hf + 1) * 512], ALU.mult)
                K.tt("pool", yb[:, hf * 512:(hf + 1) * 512], ya[:, hf * 512:(hf + 1) * 512],
                     sg[:, 1024 + hf * 512:1024 + (hf + 1) * 512], ALU.add)
            pt = c.psb[4]
            for kc in range(8):
                K.tr(pt[:, kc * 128:(kc + 1) * 128], yb[:, kc * 128:(kc + 1) * 128], c.ident_bf[:, :])
            K.copy("act", yT[:, :, :], V(pt.h[:, :].rearrange("p (k t) -> p k t", k=8), (pt.buf,)))
            for hf in range(2):
                po = c.ps[5 + hf]
                for kc in range(8):
                    K.mm(po[:, :], yT[:, kc, :], wo[:, kc, hf * 512:(hf + 1) * 512], kc == 0, kc == 7)
                K.tt("dve", x[:, hf * 512:(hf + 1) * 512], po[:, :], x[:, hf * 512:(hf + 1) * 512], ALU.add)
            K.dma("pool", dv(d["xres"][blk * 128:(blk + 1) * 128, :]), x[:, :])
        K.barrier()


def phase_g0(K, c, l, d):
    import contextlib
    nc = c.nc
    TG = 512 if T >= 512 else T
    NB = TG // 128
    with contextlib.ExitStack() as stack:
        cw = sb(nc, stack, "cw", [128, 12, 5], F32)
        K.dma("sp", cw[:, :, :], dv(d["gdn_conv"][l].rearrange("(c p) k -> p c k", p=128)))
        xin = [sb(nc, stack, "xin%d" % i, [128, 12, TG + 4], F32) for i in range(2)]
        y = [sb(nc, stack, "ycv%d" % i, [128, TG], F32) for i in range(3)]
        ysil = sb(nc, stack, "ysil", [128, 12, TG], F32)
        ytmp = sb(nc, stack, "ytmp", [128, TG], F32)
        tm = [sb(nc, stack, "tm%d" % i, [128, 1536], F32) for i in range(2)]
        ss8 = sb(nc, stack, "ss8", [128, 8], F32)
        rs8 = sb(nc, stack, "rs8", [128, 8], F32)
        junk = sb(nc, stack, "junkg", [128, 128], F32)
        qkvT_v = d["qkvT"].rearrange("(c p) t -> p c t", p=128)
        ntile = T // TG
        K.dma("sp", xin[0][:, :, :], dv(qkvT_v[:, :, 0:TG + 4]))
        for i in range(ntile):
            xi = xin[i % 2]
            if i + 1 < ntile:
                K.dma("sp", xin[(i + 1) % 2][:, :, :], dv(qkvT_v[:, :, (i + 1) * TG:(i + 1) * TG + TG + 4]))
            for cc in range(12):
                e = "pool" if cc in (5, 11) else "dve"
                yy = y[cc % 2] if e == "dve" else y[2]
                K.ts(e, yy[:, :], xi[:, cc, 0:TG], cw[:, cc, 0:1], ALU.mult)
                for k in range(1, 5):
                    if e == "dve":
                        K.stt(e, yy[:, :], xi[:, cc, k:k + TG], cw[:, cc, k:k + 1], yy[:, :], ALU.mult, ALU.add)
                    else:
                        K.ts(e, ytmp[:, :], xi[:, cc, k:k + TG], cw[:, cc, k:k + 1], ALU.mult)
                        K.tt(e, yy[:, :], yy[:, :], ytmp[:, :], ALU.add)
                K.act(ysil[:, cc, :], yy[:, :], AF.Silu)
            for b in range(NB):
                blk = i * NB + b
                t_ = tm[blk % 2]
                for cc in range(12):
                    K.tr(c.ps[cc // 4][:, (cc % 4) * 128:(cc % 4 + 1) * 128], ysil[:, cc, b * 128:(b + 1) * 128], c.ident_f[:, :])
                for g in range(8):
                    K.act(junk[:, :], c.ps[g // 4][:, (g % 4) * 128:(g % 4 + 1) * 128], AF.Square, accum=ss8[:, g:g + 1])
                K.act(rs8[:, :], ss8[:, :], AF.Sqrt, bias=c.eps_t[:, 0:1], scale=1.0)
                K.recip(rs8[:, :], rs8[:, :])
                K.ts("dve", rs8[:, 0:4], rs8[:, 0:4], 128.0 ** -0.5, ALU.mult)
                for g in range(8):
                    K.ts("dve", t_[:, g * 128:(g + 1) * 128], c.ps[g // 4][:, (g % 4) * 128:(g % 4 + 1) * 128], rs8[:, g:g + 1], ALU.mult)
                K.copy("act", t_[:, 1024:1536], c.ps[2][:, :])
                K.dma("pool", dv(d["qkvn"][blk * 128:(blk + 1) * 128, :]), t_[:, :])
        K.barrier()


def phase_gscan(K, c, l, d, dirn, S, goT=None):
    import contextlib
    nc = c.nc
    nb = T // 128
    tri = c.tri[dirn]
    negm4 = c.negm4[dirn]
    nstr4 = c.nstr4[dirn]
    with contextlib.ExitStack() as stack:
        qkv = [sb(nc, stack, "qkvg%d" % i, [128, 1536], F32) for i in range(2)]
        bgt = [sb(nc, stack, "bgg%d" % i, [128, 16], F32) for i in range(2)]
        kqT = sb(nc, stack, "kqT", [128, 4, 256], F32)
        GT = sb(nc, stack, "GT", [128, 4, 128], F32)
        sm = sb(nc, stack, "smg", [128, 6, 4], F32)
        DT = sb(nc, stack, "DT", [128, 4, 128], F32)
        Xt = sb(nc, stack, "Xt", [128, 4, 128], F32)
        Xs = [sb(nc, stack, "Xs%d" % i, [128, 4, 128], F32) for i in range(2)]
        Ys = [sb(nc, stack, "Ys%d" % i, [128, 4, 128], F32) for i in range(2)]
        Ns = [sb(nc, stack, "Ns%d" % i, [128, 4, 128], F32) for i in range(2)]
        A2 = sb(nc, stack, "A2", [128, 4, 128], F32)
        rhs = sb(nc, stack, "rhsg", [128, 4, 256], F32)
        UW = sb(nc, stack, "UW", [128, 4, 256], F32)
        WT = sb(nc, stack, "WT", [128, 4, 128], F32)
        qg = sb(nc, stack, "qg", [128, 4, 128], F32)
        qgT = sb(nc, stack, "qgT", [128, 4, 128], F32)
        kd = sb(nc, stack, "kd", [128, 4, 128], F32)
        VNs = [sb(nc, stack, "VN%d" % i, [128, 4, 128], F32) for i in range(2)]
        for i in range(2):
            K.memset("pool", VNs[i][:, :, :], 0.0)
        O = [sb(nc, stack, "Og%d" % i, [128, 4, 128], F32) for i in range(2)]
        if dirn == 1:
            o1t = [sb(nc, stack, "o1t%d" % i, [128, 4, 128], F32) for i in range(2)]
            zt = [sb(nc, stack, "ztg%d" % i, [128, 4, 128], F32) for i in range(2)]
            ob = sb(nc, stack, "obg", [128, 4, 128], BF16)
            nrow = sb(nc, stack, "nrowg", [128, 128], F32)
            ss4 = sb(nc, stack, "ss4", [128, 4], F32)
            rs4 = sb(nc, stack, "rs4", [128, 4], F32)
            junk = sb(nc, stack, "junkgs", [128, 128], F32)
            K.dma("sp", nrow[:, :], dv(d["gdn_norm"][l].partition_broadcast(128)))
        ident4 = V(c.ident_f.h[:, :].unsqueeze(1).to_broadcast([128, 4, 128]), (c.ident_f.buf,))
        order = list(range(nb)) if dirn == 0 else list(range(nb - 1, -1, -1))

        def loads(j, blk):
            K.dma("sp", qkv[j % 2][:, :], dv(d["qkvn"][blk * 128:(blk + 1) * 128, :]))
            K.dma("act", bgt[j % 2][:, :], dv(d["bg"][blk * 128:(blk + 1) * 128, :]))
            if dirn == 1:
                K.dma("sp", o1t[j % 2][:, :, :], dv(d["o1"][blk * 128:(blk + 1) * 128, :].rearrange("p (h e) -> p h e", h=4)))
                K.dma("act", zt[j % 2][:, :, :], dv(d["zs"][blk * 128:(blk + 1) * 128, :].rearrange("p (h e) -> p h e", h=4)))

        loads(0, order[0])
        for j, blk in enumerate(order):
            if j + 1 < nb:
                loads(j + 1, order[j + 1])
            qk = qkv[j % 2]
            bg_ = bgt[j % 2]
            beta = lambda h: bg_[:, dirn * 4 + h:dirn * 4 + h + 1]
            g4 = bg_[:, 8 + dirn * 4:8 + dirn * 4 + 4]
            gcol = lambda h: bg_[:, 8 + dirn * 4 + h:8 + dirn * 4 + h + 1]
            qv = lambda h: qk[:, h * 128:(h + 1) * 128]
            kv = lambda h: qk[:, 512 + h * 128:512 + (h + 1) * 128]
            for hh in range(2):
                pt = c.ps[hh]
                for h2 in range(2):
                    h = hh * 2 + h2
                    K.tr(pt[:, h2 * 256:h2 * 256 + 128], kv(h), c.ident_f[:, :])
                    K.tr(pt[:, h2 * 256 + 128:h2 * 256 + 256], qv(h), c.ident_f[:, :])
                K.copy("act" if hh == 0 else "dve", kqT[:, hh * 2:hh * 2 + 2, :],
                       V(pt.h[:, :].rearrange("p (h e) -> p h e", h=2), (pt.buf,)))
            pg = c.ps[2]
            K.mm(pg[:, 0:4], tri[:, :], g4, True, True)
            K.mm(pg[:, 4:8], c.same[:, :], g4, True, True)
            K.mm(pg[:, 8:12], c.ch[0][:, :], g4, True, True)
            K.mm(pg[:, 12:16], c.ch[1][:, :], g4, True, True)
            gc = sm[:, 0, :]
            ngc = sm[:, 1, :]
            egc = sm[:, 2, :]
            ekd = sm[:, 3, :]
            K.copy("dve", gc, pg[:, 0:4])
            K.ts("dve", ngc, pg[:, 0:4], -1.0, ALU.mult)
            K.act(egc, pg[:, 0:4], AF.Exp)
            K.tt("dve", ekd, pg[:, 4:8], gc, ALU.subtract)
            K.act(ekd, ekd, AF.Exp)
            K.act(V(sm.h[:, 4:6, :], (sm.buf,)), V(pg.h[:, 8:16].rearrange("p (a b) -> p a b", a=2), (pg.buf,)), AF.Exp)
            for h in range(4):
                K.ts("dve", GT[:, h, :], tri[:, :], gcol(h), ALU.mult)
            pR = c.ps[3]
            GTf = V(GT.h[:, :, :].rearrange("p h e -> p (h e)"), (GT.buf,))
            K.mm(pR[:, :], c.ones_f[:, :], GTf, True, False)
            K.mm(pR[:, :], c.ident_f[:, :], negm4[:, :], False, True)
            for h in range(4):
                K.act(DT[:, h, :], pR[:, h * 128:(h + 1) * 128], AF.Exp, bias=sm[:, 1, h:h + 1])
            pK = [c.ps[4], c.ps[5]]
            for h in range(4):
                K.mm(pK[h // 2][:, (h % 2) * 256:(h % 2) * 256 + 256], kqT[:, h, 0:128], kqT[:, h, :], True, True)
            for h in range(4):
                K.stt("dve", Xt[:, h, :], pK[h // 2][:, (h % 2) * 256:(h % 2) * 256 + 128], beta(h), DT[:, h, :], ALU.mult, ALU.mult)
                K.tt("dve", A2[:, h, :], pK[h // 2][:, (h % 2) * 256 + 128:(h % 2) * 256 + 256], DT[:, h, :], ALU.mult)
            X, Y, N = Xs[0], Ys[0], Ns[0]
            K.tt("dve", X[:, :, :], Xt[:, :, :], nstr4[:, :, :], ALU.mult)
            pY = c.ps[6]
            for h in range(4):
                K.tr(pY[:, h * 128:(h + 1) * 128], X[:, h, :], c.ident_f[:, :])
            K.copy("act", Y[:, :, :], V(pY.h[:, :].rearrange("p (h e) -> p h e", h=4), (pY.buf,)))
            K.tt("pool", N[:, :, :], X[:, :, :], ident4, ALU.add)
            for lvl in range(5):
                last = lvl == 4
                Xn, Yn, Nn = Xs[(lvl + 1) % 2], Ys[(lvl + 1) % 2], Ns[(lvl + 1) % 2]
                pY2 = c.ps[6 + (lvl % 2)]
                for h in range(4):
                    K.mm(pY2[:, h * 128:(h + 1) * 128], X[:, h, :], Y[:, h, :], True, True)
                if not last:
                    pX2 = c.ps[0 + (lvl % 2)]
                    for h in range(4):
                        K.mm(pX2[:, h * 128:(h + 1) * 128], Y[:, h, :], X[:, h, :], True, True)
                K.copy("act", Yn[:, :, :], V(pY2.h[:, :].rearrange("p (h e) -> p h e", h=4), (pY2.buf,)))
                if not last:
                    K.copy("dve", Xn[:, :, :], V(pX2.h[:, :].rearrange("p (h e) -> p h e", h=4), (pX2.buf,)))
                pN = c.ps[2 + (lvl % 2)]
                for h in range(4):
                    K.mm(pN[:, h * 128:(h + 1) * 128], Yn[:, h, :], N[:, h, :], True, True)
                K.tt("dve", Nn[:, :, :], V(pN.h[:, :].rearrange("p (h e) -> p h e", h=4), (pN.buf,)), N[:, :, :], ALU.add)
                X, Y, N = Xn, Yn, Nn
            K.copy("pool", rhs[:, :, 0:128], V(qk.h[:, 1024:1536].rearrange("p (h e) -> p h e", h=4), (qk.buf,)))
            for h in range(4):
                K.ts("pool", rhs[:, h, 128:256], kv(h), sm[:, 2, h:h + 1], ALU.mult)
            pU = [c.ps[4], c.ps[5]]
            for h in range(4):
                K.mm(pU[h // 2][:, (h % 2) * 256:(h % 2) * 256 + 256], N[:, h, :], rhs[:, h, :], True, True)
            for h in range(4):
                K.ts("dve", UW[:, h, :], pU[h // 2][:, (h % 2) * 256:(h % 2) * 256 + 256], beta(h), ALU.mult)
            pW = c.ps[0]
            for h in range(4):
                K.tr(pW[:, h * 128:(h + 1) * 128], UW[:, h, 128:256], c.ident_f[:, :])
            K.copy("act", WT[:, :, :], V(pW.h[:, :].rearrange("p (h e) -> p h e", h=4), (pW.buf,)))
            for h in range(4):
                K.ts("pool", qg[:, h, :], qv(h), sm[:, 2, h:h + 1], ALU.mult)
                K.ts("pool", kd[:, h, :], kv(h), sm[:, 3, h:h + 1], ALU.mult)
            pQ = c.ps[1]
            for h in range(4):
                K.tr(pQ[:, h * 128:(h + 1) * 128], qg[:, h, :], c.ident_f[:, :])
            K.copy("dve", qgT[:, :, :], V(pQ.h[:, :].rearrange("p (h e) -> p h e", h=4), (pQ.buf,)))
            Ot = O[j % 2]
            for cch in ((0, 1) if dirn == 0 else (1, 0)):
                r0, r1 = cch * 64, cch * 64 + 64
                VN = VNs[cch]
                pV = c.ps[2]
                for h in range(4):
                    K.mm(pV[:, h * 128:(h + 1) * 128], WT[:, h, :], S[:, h, :], True, True)
                K.tt("dve", VN[r0:r1, :, :], UW[r0:r1, :, 0:128],
                     V(pV.h[r0:r1, :].rearrange("p (h e) -> p h e", h=4), (pV.buf,)), ALU.subtract)
                pO = c.ps[3]
                for h in range(4):
                    K.mm(pO[:, h * 128:(h + 1) * 128], qgT[:, h, :], S[:, h, :], True, False)
                    K.mm(pO[:, h * 128:(h + 1) * 128], A2[:, h, :], VN[:, h, :], False, True)
                K.copy("act", Ot[r0:r1, :, :], V(pO.h[r0:r1, :].rearrange("p (h e) -> p h e", h=4), (pO.buf,)))
                pS = c.ps[6]
                for h in range(4):
                    K.mm(pS[:, h * 128:(h + 1) * 128], kd[:, h, :], VN[:, h, :], True, True)
                for h in range(4):
                    K.stt("dve", S[:, h, :], S[:, h, :], sm[:, 4 + cch, h:h + 1], pS[:, h * 128:(h + 1) * 128], ALU.mult, ALU.add)
            if dirn == 0:
                K.dma("pool", dv(d["o1"][blk * 128:(blk + 1) * 128, :].rearrange("p (h e) -> p h e", h=4)), Ot[:, :, :])
            else:
                K.tt("pool", Ot[:, :, :], Ot[:, :, :], o1t[j % 2][:, :, :], ALU.add)
                for h in range(4):
                    K.act(junk[:, :], Ot[:, h, :], AF.Square, accum=ss4[:, h:h + 1])
                K.act(rs4[:, :], ss4[:, :], AF.Sqrt, bias=c.eps_t[:, 0:1], scale=1.0 / 128)
                K.recip(rs4[:, :], rs4[:, :])
                for h in range(4):
                    K.stt("dve", Ot[:, h, :], Ot[:, h, :], rs4[:, h:h + 1], nrow[:, :], ALU.mult, ALU.mult)
                K.tt("pool", ob[:, :, :], Ot[:, :, :], zt[j % 2][:, :, :], ALU.mult)
                pG = c.psb[7]
                for h in range(4):
                    K.tr(pG[:, h * 128:(h + 1) * 128], ob[:, h, :], c.ident_bf[:, :])
                K.copy("act", goT[:, :, blk * 128:(blk + 1) * 128], V(pG.h[:, 0:512].rearrange("p (h e) -> p h e", h=4), (pG.buf,)))
        K.barrier()


def phase_x2(K, c, l, d, S):
    import contextlib
    nc = c.nc
    K.dma("sp", dv(d["st_src"].rearrange("(h p) v -> p h v", p=128)), S[:, :, :])
    collective_gather(K, c, d["st_src_h"], d["st_all_h"])
    with contextlib.ExitStack() as stack:
        sa = sb(nc, stack, "sa", [128, 2, 4, 128], F32)
        K.dma("sp", sa[:, :, :, :], dv(d["st_all"].rearrange("(s h p) v -> p s h v", p=128, s=2)))
        sel_other(K, c, "dve", S[:, :, :], sa[:, 0, :, :], sa[:, 1, :, :])
        K.barrier()


CONST_COLS = 128 * 9 + 512 * 4


def make_consts():
    idx = np.arange(128)
    same = (idx[:, None] // 64) == (idx[None, :] // 64)
    cs = np.zeros((128, CONST_COLS), np.float32)
    cs[:, 0:128] = np.eye(128)
    cs[:, 128:256] = same
    cs[:, 256:384] = 1.0
    for dirn in range(2):
        if dirn == 0:
            tri = same & (idx[:, None] <= idx[None, :])
            allow = same & (idx[None, :] >= idx[:, None])
            strict = same & (idx[None, :] > idx[:, None])
        else:
            tri = same & (idx[:, None] >= idx[None, :])
            allow = same & (idx[None, :] <= idx[:, None])
            strict = same & (idx[None, :] < idx[:, None])
        cs[:, 384 + dirn * 128:384 + (dirn + 1) * 128] = tri
        cs[:, 640 + dirn * 512:640 + (dirn + 1) * 512] = np.tile(np.where(allow, 0.0, -30000.0), (1, 4))
        cs[:, 1664 + dirn * 512:1664 + (dirn + 1) * 512] = np.tile(np.where(strict, -1.0, 0.0), (1, 4))
    cs[0:64, 2688:2816] = 1.0
    cs[64:128, 2816:2944] = 1.0
    cs[64, 2944:3072] = 1.0
    cs[0, 3072:3200] = 1.0
    return cs


WEIGHT_SPECS = [
    ("norm_ffn1", [DEPTH, D_MODEL]), ("ffn1_w_gate", [DEPTH, D_MODEL, D_FF]), ("ffn1_w_up", [DEPTH, D_MODEL, D_FF]),
    ("ffn1_w_down", [DEPTH, D_FF, D_MODEL]), ("norm_mix", [DEPTH, D_MODEL]), ("w_in", [DEPTH, D_MODEL, D_IN]),
    ("gdn_conv", [DEPTH, 1536, 5]), ("gdn_A_log", [DEPTH, 8]), ("gdn_dt_bias", [DEPTH, 8]), ("gdn_norm", [DEPTH, 128]),
    ("gdn_proj", [DEPTH, 512, D_MODEL]), ("mla_q_norm", [DEPTH, 384]), ("mla_w_uq", [DEPTH, 384, 768]),
    ("mla_kv_norm", [DEPTH, 256]), ("mla_w_ukv", [DEPTH, 256, 1024]), ("mla_proj", [DEPTH, 512, D_MODEL]),
    ("w_out", [DEPTH, D_MODEL, D_MODEL]), ("norm_ffn2", [DEPTH, D_MODEL]), ("ffn2_w_gate", [DEPTH, D_MODEL, D_FF]),
    ("ffn2_w_up", [DEPTH, D_MODEL, D_FF]), ("ffn2_w_down", [DEPTH, D_FF, D_MODEL]), ("final_norm", [D_MODEL]),
]


def build(cfg):
    import contextlib
    nc = bass.Bass("TRN2", target_bir_lowering=False)
    K = KB(nc)
    c = Ctx()
    c.nc = nc
    c.K = K
    c.ncores = cfg.get("ncores", 8)
    c.cc_sem = nc.alloc_semaphore("cc_sem")
    c.cc_cnt = 0
    d = {}

    def inp(name, shape, dtype=F32):
        d[name] = nc.dram_tensor(name, shape, dtype, kind="ExternalInput").ap()
        return d[name]

    inp("x", [T, D_MODEL])
    inp("pos_pm", [128, T // 128], I32)
    inp("inv_freq", [16])
    inp("consts", [128, CONST_COLS])
    inp("sel", [2])
    for name, shape in WEIGHT_SPECS:
        inp(name, shape)
    y_out = nc.dram_tensor("y", [T, D_MODEL], F32, kind="ExternalOutput").ap()

    def scratch(name, shape, dtype=F32):
        h = nc.dram_tensor(name, shape, dtype)
        d[name + "_h"] = h
        d[name] = h.ap()

    scratch("xres", [T, D_MODEL])
    scratch("qkvT", [1536, T + 4])
    scratch("zs", [T, 512])
    scratch("sgd", [T, 2048])
    scratch("bg", [T, 16])
    scratch("QT", [8, 128, T], BF16)
    for ch in range(T // min(1024, T)):
        scratch("lat_src%d" % ch, [min(1024, T), 288])
        scratch("lat_all%d" % ch, [2 * min(1024, T), 288])
    scratch("halo_src", [1536, 2])
    scratch("halo_all", [2 * 1536, 2])
    scratch("qkvn", [T, 1536])
    scratch("o1", [T, 512])
    scratch("st_src", [512, 128])
    scratch("st_all", [1024, 128])

    dump = cfg.get("dump", ())
    stop = cfg.get("stop", None)
    dbg = {}

    def dbg_out(name, shape, dtype=F32):
        dbg[name] = nc.dram_tensor("dbg_" + name, shape, dtype, kind="ExternalOutput").ap()
        return dbg[name]

    with contextlib.ExitStack() as stack:
        c.eps_t = sb(nc, stack, "eps_t", [128, 1], F32)
        c.one_t = sb(nc, stack, "one_t", [128, 1], F32)
        K.memset("dve", c.eps_t[:, :], EPS)
        K.memset("dve", c.one_t[:, :], 1.0)
        cst = sb(nc, stack, "cst", [128, CONST_COLS], F32)
        K.dma("sp", cst[:, :], dv(d["consts"]))

        def cview(lo, hi, shape3=None):
            t = TT(cst.h[:, lo:hi] if shape3 is None else cst.h[:, lo:hi].rearrange("p (h e) -> p h e", h=4), "cst")
            t.buf = cst.buf
            return t
        c.ident_f = cview(0, 128)
        c.same = cview(128, 256)
        c.ones_f = cview(256, 384)
        c.tri = [cview(384, 512), cview(512, 640)]
        c.negm4 = [cview(640, 1152), cview(1152, 1664)]
        c.nstr4 = [cview(1664, 2176, True), cview(2176, 2688, True)]
        c.ch = [cview(2688, 2816), cview(2816, 2944)]
        c.rowsel = [cview(2944, 3072), cview(3072, 3200)]
        c.ident_bf = sb(nc, stack, "ident_bf", [128, 128], BF16)
        K.copy("dve", c.ident_bf[:, :], c.ident_f[:, :])
        c.sel = sb(nc, stack, "sel", [128, 2], F32)
        K.dma("sp", c.sel[:, :], dv(d["sel"].partition_broadcast(128)))
        c.ps = []
        c.psb = []
        for i in range(8):
            h = stack.enter_context(nc.psum_tensor("ps%d" % i, [128, 512], F32))
            t = TT(h, "ps%d" % i)
            t.buf.x = True
            c.ps.append(t)
            tb = TT(h[:, :].bitcast(BF16), "psb%d" % i)
            tb.buf = t.buf
            c.psb.append(tb)
        prologue_rope(K, c, stack, d["pos_pm"], d["inv_freq"])

        def run():
            src = d["x"]
            for l in range(DEPTH):
                phase_ffn(K, c, stack, src, d["xres"], d["ffn1_w_gate"][l], d["ffn1_w_up"][l], d["ffn1_w_down"][l],
                          d["norm_ffn1"][l])
                src = d["xres"]
                if stop == ("f1", l):
                    return
                phase_w(K, c, l, d)
                if stop == ("w", l):
                    return
                phase_x1(K, c, l, d)
                if stop == ("x1", l):
                    return
                with contextlib.ExitStack() as lst:
                    aoT = sb(nc, lst, "aoT", [128, 4, T], BF16)
                    if cfg.get("skip_a"):
                        K.memset("dve", aoT[:, :, :], 0.0)
                    else:
                        phase_a(K, c, l, d, aoT)
                        if cfg.get("a_twice"):
                            phase_a(K, c, l, d, aoT)
                    if "aoT" in dump and l == 0:
                        K.dma("sp", dv(dbg_out("aoT", [128, 4, T], BF16)), aoT[:, :, :])
                    if stop == ("a", l):
                        K.barrier()
                        return
                    goT = sb(nc, lst, "goT", [128, 4, T], BF16)
                    S = sb(nc, lst, "Sst", [128, 4, 128], F32)
                    phase_g0(K, c, l, d)
                    if stop == ("g0", l):
                        return
                    K.memset("dve", S[:, :, :], 0.0)
                    phase_gscan(K, c, l, d, 0, S)
                    if "S1" in dump and l == 0:
                        K.dma("sp", dv(dbg_out("S1", [128, 4, 128])), S[:, :, :])
                    if stop == ("g1", l):
                        K.barrier()
                        return
                    phase_x2(K, c, l, d, S)
                    phase_gscan(K, c, l, d, 1, S, goT)
                    if "goT" in dump and l == 0:
                        K.dma("sp", dv(dbg_out("goT", [128, 4, T], BF16)), goT[:, :, :])
                    if stop == ("g2", l):
                        K.barrier()
                        return
                    phase_m(K, c, l, d, aoT, goT)
                if stop == ("m", l):
                    return
                last = (l == DEPTH - 1)
                phase_ffn(K, c, stack, d["xres"], d["xres"], d["ffn2_w_gate"][l], d["ffn2_w_up"][l], d["ffn2_w_down"][l],
                          d["norm_ffn2"][l], final_gain=d["final_norm"] if last else None, y_out=y_out if last else None)
                if stop == ("f2", l):
                    return

        run()
        K.barrier()
        for name in dump:
            if name in ("aoT", "goT", "S1"):
                continue
            src_ap = d[name]
            K.dma("sp", dv(dbg_out(name, list(src_ap.shape), src_ap.dtype)), dv(src_ap))
        K.barrier()
    print("instructions:", K.ninst, dict(K.ecnt))
    return nc


def shard_inputs(inputs, ncores=8):
    maps = []
    consts = make_consts()
    inv_freq = np.power(np.float32(10000.0), -np.arange(0, 32, 2, dtype=np.float32) / np.float32(32)).astype(np.float32)
    w = {k: np.asarray(inputs[k], dtype=np.float32) for k, _ in WEIGHT_SPECS}
    w["gdn_conv"] = np.ascontiguousarray(np.transpose(w["gdn_conv"], (0, 2, 1)))
    w["gdn_A_log"] = w["gdn_A_log"].reshape(DEPTH, 8)
    w["gdn_dt_bias"] = w["gdn_dt_bias"].reshape(DEPTH, 8)
    wr = dict(w)
    wi = w["w_in"].copy()
    for base in (2048, 2056):
        wi[:, :, base:base + 4] = w["w_in"][:, :, base + 4:base + 8]
        wi[:, :, base + 4:base + 8] = w["w_in"][:, :, base:base + 4]
    wr["w_in"] = wi
    wr["gdn_A_log"] = np.ascontiguousarray(w["gdn_A_log"].reshape(DEPTH, 2, 4)[:, ::-1].reshape(DEPTH, 8))
    wr["gdn_dt_bias"] = np.ascontiguousarray(w["gdn_dt_bias"].reshape(DEPTH, 2, 4)[:, ::-1].reshape(DEPTH, 8))
    wr["gdn_conv"] = np.ascontiguousarray(w["gdn_conv"][:, :, ::-1])
    S_ = inputs["x"].shape[1]
    Tl = S_ // 2
    for core in range(ncores):
        b, p = core // 2, core % 2
        xs = inputs["x"][b, p * Tl:(p + 1) * Tl]
        ps = inputs["positions"][b, p * Tl:(p + 1) * Tl]
        if p == 1:
            xs = xs[::-1]
            ps = ps[::-1]
        m = dict(w if p == 0 else wr)
        m["x"] = np.ascontiguousarray(xs, dtype=np.float32)
        m["pos_pm"] = np.ascontiguousarray(np.asarray(ps, dtype=np.int32).reshape(Tl // 128, 128).T)
        m["inv_freq"] = inv_freq
        m["consts"] = consts
        m["sel"] = np.array([0.0, 1.0] if p == 0 else [1.0, 0.0], np.float32)
        maps.append(m)
    return maps


def kernel(**inputs):
    inputs = {k: np.asarray(v) for k, v in inputs.items()}
    nc = build({})
    maps = shard_inputs(inputs)
    res = run_bass_kernel_spmd(nc, maps, core_ids=list(range(8)))
    out = np.empty((BATCH, SEQ, D_MODEL), np.float32)
    for core in range(8):
        b, p = core // 2, core % 2
        y = res.results[core]["y"]
        if p == 1:
            y = y[::-1]
        out[b, p * T:(p + 1) * T] = y
    return out
```

```python
import numpy as np
import concourse.bass as bass
import concourse.mybir as mybir
from concourse.bass_utils import run_bass_kernel_spmd

F32 = mybir.dt.float32
BF16 = mybir.dt.bfloat16
I32 = mybir.dt.int32
AF = mybir.ActivationFunctionType
ALU = mybir.AluOpType
AX = mybir.AxisListType

D_MODEL = 1024
BATCH = 4
SEQ = 8192
DEPTH = 2
T = SEQ // 2
NBLK = T // 128
D_FF = 2816
NFF = D_FF // 128
D_IN = 4784
EPS = 1e-6
NDS = 48
NSW = 12


class Buf:
    __slots__ = ("name", "w", "r", "x")

    def __init__(self, name):
        self.name = name
        self.w = None
        self.r = {}
        self.x = False


class V:
    __slots__ = ("ap", "bufs")

    def __init__(self, ap, bufs):
        self.ap = ap
        self.bufs = bufs


class TT:
    def __init__(self, handle, name):
        self.h = handle
        self.buf = Buf(name)

    def __getitem__(self, idx):
        return V(self.h[idx], (self.buf,))


def dv(ap):
    return V(ap, ())


class KB:
    def __init__(self, nc):
        self.nc = nc
        self.engs = {"pe": nc.tensor, "dve": nc.vector, "act": nc.scalar, "pool": nc.gpsimd, "sp": nc.sync}
        self.esem = {k: nc.alloc_semaphore("e_" + k) for k in self.engs}
        self.ecnt = {k: 0 for k in self.engs}
        self.dsem = [nc.alloc_semaphore("d%d" % i) for i in range(NDS)]
        self.dcnt = [0] * NDS
        self.dnext = 0
        self.dnext_sw = 0
        self.seen = {k: {} for k in self.engs}
        self.ninst = 0

    def _sem(self, key):
        return self.esem[key[1]] if key[0] == "e" else self.dsem[key[1]]

    def _wait(self, e, key, val):
        if self.seen[e].get(key, 0) >= val:
            return
        self.seen[e][key] = val
        self.engs[e].wait_ge(self._sem(key), val)
        self.ninst += 1

    def _deps(self, e, reads, writes):
        need = {}
        for v in reads:
            for b in v.bufs:
                if b.w is not None:
                    k, val = b.w
                    if need.get(k, 0) < val:
                        need[k] = val
                if b.x:
                    for k, val in b.r.items():
                        if k != ("e", e) and need.get(k, 0) < val:
                            need[k] = val
        for v in writes:
            for b in v.bufs:
                if b.w is not None:
                    k, val = b.w
                    if need.get(k, 0) < val:
                        need[k] = val
                for k, val in b.r.items():
                    if need.get(k, 0) < val:
                        need[k] = val
        for k, val in need.items():
            if k == ("e", "pe") and e == "pe":
                continue
            self._wait(e, k, val)

    def _mark(self, tok, reads, writes):
        k, val = tok
        for v in reads:
            for b in v.bufs:
                if b.r.get(k, 0) < val:
                    b.r[k] = val
        for v in writes:
            for b in v.bufs:
                b.w = tok
                b.r = {}

    def op(self, e, fn, reads=(), writes=()):
        self._deps(e, reads, writes)
        ins = fn(self.engs[e])
        self.ecnt[e] += 1
        ins.then_inc(self.esem[e], 1)
        self.ninst += 1
        self._mark((("e", e), self.ecnt[e]), reads, writes)
        return ins

    def dma(self, q, out, in_, **kw):
        if q == "pool":
            i = self.dnext_sw
            self.dnext_sw = (i + 1) % NSW
        else:
            i = NSW + self.dnext
            self.dnext = (self.dnext + 1) % (NDS - NSW)
        if self.dcnt[i] > 0:
            self._wait(q, ("d", i), self.dcnt[i])
        self._deps(q, (in_,), (out,))
        ins = self.engs[q].dma_start(out=out.ap, in_=in_.ap, **kw)
        self.dcnt[i] += 16
        ins.then_inc(self.dsem[i], 16)
        self.ninst += 1
        self._mark((("d", i), self.dcnt[i]), (in_,), (out,))
        return ins

    def barrier(self, engines=None):
        for e in (engines or self.engs):
            for k2 in self.engs:
                if k2 != e and self.ecnt[k2] > 0:
                    self._wait(e, ("e", k2), self.ecnt[k2])
            for i in range(NDS):
                if self.dcnt[i] > 0:
                    self._wait(e, ("d", i), self.dcnt[i])

    def mm(self, out, lhsT, rhs, start, stop):
        return self.op("pe", lambda t: t.matmul(out.ap, lhsT=lhsT.ap, rhs=rhs.ap, start=start, stop=stop),
                       reads=(lhsT, rhs), writes=(out,))

    def tr(self, out, in_, ident):
        return self.op("pe", lambda t: t.transpose(out.ap, in_.ap, ident.ap), reads=(in_, ident), writes=(out,))

    def act(self, out, in_, func, bias=None, scale=None, accum=None, e="act"):
        kw = {}
        reads = [in_]
        writes = [out]
        if bias is not None:
            if isinstance(bias, V):
                kw["bias"] = bias.ap
                reads.append(bias)
            else:
                kw["bias"] = bias
        if scale is not None:
            if isinstance(scale, V):
                kw["scale"] = scale.ap
                reads.append(scale)
            else:
                kw["scale"] = scale
        if accum is not None:
            kw["accum_out"] = accum.ap
            writes.append(accum)
        return self.op(e, lambda a: a.activation(out=out.ap, in_=in_.ap, func=func, **kw), reads=reads, writes=writes)

    def tt(self, e, out, in0, in1, op):
        return self.op(e, lambda g: g.tensor_tensor(out=out.ap, in0=in0.ap, in1=in1.ap, op=op),
                       reads=(in0, in1), writes=(out,))

    def ts(self, e, out, in0, s1, op0, s2=None, op1=None):
        reads = [in0]
        a1 = s1
        a2 = s2
        if isinstance(s1, V):
            reads.append(s1)
            a1 = s1.ap
        if isinstance(s2, V):
            reads.append(s2)
            a2 = s2.ap
        kw = {}
        if op1 is not None:
            kw["op1"] = op1
        return self.op(e, lambda g: g.tensor_scalar(out=out.ap, in0=in0.ap, scalar1=a1, scalar2=a2, op0=op0, **kw),
                       reads=reads, writes=(out,))

    def stt(self, e, out, in0, scalar, in1, op0, op1):
        reads = [in0, in1]
        a = scalar
        if isinstance(scalar, V):
            reads.append(scalar)
            a = scalar.ap
        return self.op(e, lambda g: g.scalar_tensor_tensor(out=out.ap, in0=in0.ap, scalar=a, in1=in1.ap, op0=op0, op1=op1),
                       reads=reads, writes=(out,))

    def copy(self, e, out, in_):
        if e == "act":
            return self.op(e, lambda g: g.copy(out=out.ap, in_=in_.ap), reads=(in_,), writes=(out,))
        return self.op(e, lambda g: g.tensor_copy(out=out.ap, in_=in_.ap), reads=(in_,), writes=(out,))

    def memset(self, e, out, val):
        return self.op(e, lambda g: g.memset(out.ap, val), reads=(), writes=(out,))

    def recip(self, out, in_):
        return self.op("dve", lambda g: g.reciprocal(out=out.ap, in_=in_.ap), reads=(in_,), writes=(out,))


class Ctx:
    pass


_uid = [0]


def sb(nc, stack, name, shape, dtype):
    _uid[0] += 1
    name = "%s_%d" % (name, _uid[0])
    h = stack.enter_context(nc.sbuf_tensor(name, shape, dtype))
    return TT(h, name)


def load_cast(K, c, stg, dst_views, src_aps, cast_engs=("dve", "act")):
    qs = ("sp", "act", "pool")
    for i, (d, s) in enumerate(zip(dst_views, src_aps)):
        st = stg[i % len(stg)]
        shp = list(s.shape)
        sv = st[:shp[0], :shp[1]] if len(shp) == 2 else st[:shp[0], :shp[1], :shp[2]]
        K.dma(qs[i % 3], sv, dv(s))
        K.copy(cast_engs[i % len(cast_engs)], d, sv)


def rms_rstd(K, c, x_v, ss, rstd, junk, n, e_sq="act"):
    K.act(junk, x_v, AF.Square, accum=ss[:, 0:1])
    K.act(rstd[:, 0:1], ss[:, 0:1], AF.Sqrt, bias=c.eps_t[:, 0:1], scale=1.0 / n)
    K.recip(rstd[:, 0:1], rstd[:, 0:1])


def phase_ffn(K, c, stack_outer, x_src, x_dst, w_gate, w_up, w_down, gain, final_gain=None, y_out=None):
    import contextlib
    nc = c.nc
    TTK = 256
    NB = TTK // 128
    with contextlib.ExitStack() as stack:
        wg = sb(nc, stack, "wg", [128, 8, D_FF], BF16)
        wu = sb(nc, stack, "wu", [128, 8, D_FF], BF16)
        wd = sb(nc, stack, "wd", [128, NFF, D_MODEL], BF16)
        grow = sb(nc, stack, "grow", [128, D_MODEL], F32)
        K.dma("sp", grow[:, :], dv(gain.partition_broadcast(128)))
        if final_gain is not None:
            fgrow = sb(nc, stack, "fgrow", [128, D_MODEL], F32)
            K.dma("sp", fgrow[:, :], dv(final_gain.partition_broadcast(128)))
        with contextlib.ExitStack() as st2:
            stg = [sb(nc, st2, "stg%d" % i, [128, D_FF], F32) for i in range(3)]
            dsts, srcs = [], []
            for kc in range(8):
                dsts.append(wg[:, kc, :]); srcs.append(w_gate[kc * 128:(kc + 1) * 128, :])
                dsts.append(wu[:, kc, :]); srcs.append(w_up[kc * 128:(kc + 1) * 128, :])
            for j in range(NFF):
                dsts.append(wd[:, j, :]); srcs.append(w_down[j * 128:(j + 1) * 128, :])
            load_cast(K, c, stg, dsts, srcs)
            K.barrier()
        xt = [sb(nc, stack, "xt%d" % i, [128, NB, D_MODEL], F32) for i in range(2)]
        xn = [sb(nc, stack, "xn%d" % i, [128, D_MODEL], BF16) for i in range(2)]
        junk = sb(nc, stack, "junk", [128, D_MODEL], F32)
        hT = sb(nc, stack, "hT", [128, 8, TTK], BF16)
        actT = sb(nc, stack, "actT", [128, NFF, TTK], BF16)
        sg = [sb(nc, stack, "sg%d" % i, [128, TTK], F32) for i in range(2)]
        ss = [sb(nc, stack, "ss%d" % i, [128, 1], F32) for i in range(2)]
        rstd = [sb(nc, stack, "rstd%d" % i, [128, 1], F32) for i in range(2)]
        ntile = T // TTK
        xs_t = x_src.rearrange("(n b p) d -> n p b d", p=128, b=NB)
        xd_t = x_dst.rearrange("(n b p) d -> n p b d", p=128, b=NB)
        if y_out is not None:
            yo_t = y_out.rearrange("(n b p) d -> n p b d", p=128, b=NB)
        K.dma("sp", xt[0][:, :, :], dv(xs_t[0]))
        for i in range(ntile):
            x = xt[i % 2]
            if i + 1 < ntile:
                K.dma("sp", xt[(i + 1) % 2][:, :, :], dv(xs_t[i + 1]))
            for b in range(NB):
                rms_rstd(K, c, x[:, b, :], ss[b], rstd[b], junk[:, :], D_MODEL)
                K.stt("dve", xn[b][:, :], x[:, b, :], rstd[b][:, 0:1], grow[:, :], ALU.mult, ALU.mult)
                pt = c.psb[b]
                for kc in range(8):
                    K.tr(pt[:, kc * 128:(kc + 1) * 128], xn[b][:, kc * 128:(kc + 1) * 128], c.ident_bf[:, :])
                K.copy("act" if b == 0 else "dve", hT[:, :, b * 128:(b + 1) * 128],
                       V(pt.h[:, :].rearrange("p (k t) -> p k t", k=8), (pt.buf,)))
            for j in range(NFF):
                pg = c.ps[2 + (j % 3)]
                for kc in range(8):
                    K.mm(pg[:, 0:TTK], wg[:, kc, j * 128:(j + 1) * 128], hT[:, kc, :], kc == 0, kc == 7)
                for kc in range(8):
                    K.mm(pg[:, TTK:2 * TTK], wu[:, kc, j * 128:(j + 1) * 128], hT[:, kc, :], kc == 0, kc == 7)
                s = sg[j % 2]
                K.act(s[:, :], pg[:, 0:TTK], AF.Silu)
                K.tt("dve", actT[:, j, :], s[:, :], pg[:, TTK:2 * TTK], ALU.mult)
            for b in range(NB):
                for h in range(2):
                    po = c.ps[5 + ((b * 2 + h) % 3)]
                    for j in range(NFF):
                        K.mm(po[:, :], actT[:, j, b * 128:(b + 1) * 128], wd[:, j, h * 512:(h + 1) * 512], j == 0, j == NFF - 1)
                    K.stt("dve", x[:, b, h * 512:(h + 1) * 512], po[:, :], 0.5, x[:, b, h * 512:(h + 1) * 512], ALU.mult, ALU.add)
            if final_gain is None:
                K.dma("pool", dv(xd_t[i]), x[:, :, :])
            else:
                for b in range(NB):
                    rms_rstd(K, c, x[:, b, :], ss[b], rstd[b], junk[:, :], D_MODEL)
                    K.stt("dve", x[:, b, :], x[:, b, :], rstd[b][:, 0:1], fgrow[:, :], ALU.mult, ALU.mult)
                    K.dma("pool", dv(yo_t[i][:, b, :]), x[:, b, :])
        K.barrier()


MLA_SCALE = 96.0 ** -0.5
TWO_PI = 6.283185307179586
CW1 = 6.28125
CW2 = TWO_PI - CW1
MAGIC = 12582912.0
PI_LIM = 3.1415925


def prologue_rope(K, c, stack, pos_pm, invf):
    import contextlib
    nc = c.nc
    nb = T // 128
    c.cos = sb(nc, stack, "cos", [128, nb, 16], F32)
    c.sin = sb(nc, stack, "sin", [128, nb, 16], F32)
    c.cos_s = sb(nc, stack, "cos_s", [128, nb, 16], F32)
    c.sin_s = sb(nc, stack, "sin_s", [128, nb, 16], F32)
    with contextlib.ExitStack() as st:
        pos_i = sb(nc, st, "pos_i", [128, nb], I32)
        posf = sb(nc, st, "posf", [128, nb], F32)
        ivf = sb(nc, st, "ivf", [128, 16], F32)
        ang = sb(nc, st, "ang", [128, nb, 16], F32)
        u = sb(nc, st, "u", [128, nb, 16], F32)
        n = sb(nc, st, "n", [128, nb, 16], F32)
        r = sb(nc, st, "r", [128, nb, 16], F32)
        K.dma("sp", pos_i[:, :], dv(pos_pm))
        K.dma("sp", ivf[:, :], dv(invf.partition_broadcast(128)))
        K.copy("dve", posf[:, :], pos_i[:, :])
        pf_b = V(posf.h[:, :].unsqueeze(2).to_broadcast([128, nb, 16]), (posf.buf,))
        iv_b = V(ivf.h[:, :].unsqueeze(1).to_broadcast([128, nb, 16]), (ivf.buf,))
        K.tt("dve", ang[:, :, :], pf_b, iv_b, ALU.mult)
        for which, dst, dst_s in (("sin", c.sin, c.sin_s), ("cos", c.cos, c.cos_s)):
            off = 0.0 if which == "sin" else 0.25
            K.ts("dve", u[:, :, :], ang[:, :, :], 1.0 / TWO_PI, ALU.mult, off, ALU.add)
            K.ts("dve", n[:, :, :], u[:, :, :], MAGIC, ALU.add)
            K.ts("dve", n[:, :, :], n[:, :, :], MAGIC, ALU.subtract)
            K.stt("dve", r[:, :, :], n[:, :, :], -CW1, ang[:, :, :], ALU.mult, ALU.add)
            K.stt("dve", r[:, :, :], n[:, :, :], -CW2, r[:, :, :], ALU.mult, ALU.add)
            if which == "cos":
                K.ts("dve", r[:, :, :], r[:, :, :], TWO_PI / 4, ALU.add)
            K.ts("dve", r[:, :, :], r[:, :, :], PI_LIM, ALU.min, -PI_LIM, ALU.max)
            K.act(dst[:, :, :], r[:, :, :], AF.Sin)
            K.ts("dve", dst_s[:, :, :], dst[:, :, :], MLA_SCALE, ALU.mult)
        K.barrier()


def rope_tm(K, c, e, out, xin, cs, sn, nh, tmp):
    def bc(v):
        return V(v.ap.unsqueeze(1).to_broadcast([128, nh, 16]), v.bufs)
    x1 = V(xin.ap[:, :, 0:16], xin.bufs)
    x2 = V(xin.ap[:, :, 16:32], xin.bufs)
    o1 = V(out.ap[:, :, 0:16], out.bufs)
    o2 = V(out.ap[:, :, 16:32], out.bufs)
    t = [V(tmp.h[:, i, 0:nh, :], (tmp.buf,)) for i in range(4)]
    K.tt(e, t[0], x1, bc(cs), ALU.mult)
    K.tt(e, t[1], x2, bc(sn), ALU.mult)
    K.tt(e, t[2], x2, bc(cs), ALU.mult)
    K.tt(e, t[3], x1, bc(sn), ALU.mult)
    K.tt(e, o1, t[0], t[1], ALU.subtract)
    K.tt(e, o2, t[2], t[3], ALU.add)


def phase_w(K, c, l, d):
    import contextlib
    nc = c.nc
    TW = 256
    NB = TW // 128
    with contextlib.ExitStack() as stack:
        win = sb(nc, stack, "win", [128, 8, D_IN], BF16)
        wuq = sb(nc, stack, "wuq", [128, 3, 768], BF16)
        grow = sb(nc, stack, "grow_w", [128, D_MODEL], F32)
        qnrow = sb(nc, stack, "qnrow", [128, 384], F32)
        kvnrow = sb(nc, stack, "kvnrow", [128, 256], F32)
        dtbrow = sb(nc, stack, "dtbrow", [128, 8], F32)
        negA = sb(nc, stack, "negA", [128, 8], F32)
        K.dma("sp", grow[:, :], dv(d["norm_mix"][l].partition_broadcast(128)))
        K.dma("sp", qnrow[:, :], dv(d["mla_q_norm"][l].partition_broadcast(128)))
        K.dma("sp", kvnrow[:, :], dv(d["mla_kv_norm"][l].partition_broadcast(128)))
        K.dma("sp", dtbrow[:, :], dv(d["gdn_dt_bias"][l].partition_broadcast(128)))
        K.dma("sp", negA[:, :], dv(d["gdn_A_log"][l].partition_broadcast(128)))
        K.act(negA[:, :], negA[:, :], AF.Exp)
        K.ts("dve", negA[:, :], negA[:, :], -1.0, ALU.mult)
        with contextlib.ExitStack() as st2:
            stg = [sb(nc, st2, "stgw%d" % i, [128, D_IN], F32) for i in range(2)]
            dsts = [win[:, kc, :] for kc in range(8)] + [wuq[:, kc, :] for kc in range(3)]
            srcs = [d["w_in"][l][kc * 128:(kc + 1) * 128, :] for kc in range(8)] + \
                   [d["mla_w_uq"][l][kc * 128:(kc + 1) * 128, :] for kc in range(3)]
            load_cast(K, c, stg, dsts, srcs)
            K.barrier()
        xt = [sb(nc, stack, "xw%d" % i, [128, NB, D_MODEL], F32) for i in range(2)]
        xn = [sb(nc, stack, "xnw%d" % i, [128, D_MODEL], BF16) for i in range(2)]
        junk = sb(nc, stack, "junkw", [128, D_MODEL], F32)
        hT = sb(nc, stack, "hTw", [128, 8, TW], BF16)
        ss = [sb(nc, stack, "ssw%d" % i, [128, 1], F32) for i in range(2)]
        rstd = [sb(nc, stack, "rstdw%d" % i, [128, 1], F32) for i in range(2)]
        qkvst = sb(nc, stack, "qkvst", [128, 12, TW], F32)
        zs_t = [sb(nc, stack, "zs_t%d" % i, [128, 512], F32) for i in range(2)]
        sg_t = [sb(nc, stack, "sg_t%d" % i, [128, 2048], F32) for i in range(2)]
        bg_t = [sb(nc, stack, "bg_t%d" % i, [128, 16], F32) for i in range(2)]
        sp_t = sb(nc, stack, "sp_t", [128, 4, 8], F32)
        cqn = sb(nc, stack, "cqn", [128, 384], BF16)
        cqnT = sb(nc, stack, "cqnT", [128, 3, 128], BF16)
        qs = sb(nc, stack, "qs", [128, 8, 128], BF16)
        K.memset("pool", qs[:, :, :], 0.0)
        qT_t = [sb(nc, stack, "qT_t%d" % i, [128, 8, 128], BF16) for i in range(2)]
        lat_t = [sb(nc, stack, "lat_t%d" % i, [128, 288], F32) for i in range(2)]
        rtmp = sb(nc, stack, "rtmp", [128, 4, 8, 16], F32)
        ss2 = sb(nc, stack, "ss2", [128, 2], F32)
        rs2 = sb(nc, stack, "rs2", [128, 2], F32)
        ntile = T // TW
        xs_t = d["xres"].rearrange("(n b p) d -> n p b d", p=128, b=NB)
        qkvT_v = d["qkvT"].rearrange("(c p) t -> p c t", p=128)
        QT_v = d["QT"].rearrange("h d t -> d h t")
        K.dma("sp", xt[0][:, :, :], dv(xs_t[0]))
        for i in range(ntile):
            x = xt[i % 2]
            if i + 1 < ntile:
                K.dma("sp", xt[(i + 1) % 2][:, :, :], dv(xs_t[i + 1]))
            for b in range(NB):
                rms_rstd(K, c, x[:, b, :], ss[b % 2], rstd[b % 2], junk[:, :], D_MODEL)
                K.stt("dve", xn[b % 2][:, :], x[:, b, :], rstd[b % 2][:, 0:1], grow[:, :], ALU.mult, ALU.mult)
                pt = c.psb[b % 2]
                for kc in range(8):
                    K.tr(pt[:, kc * 128:(kc + 1) * 128], xn[b % 2][:, kc * 128:(kc + 1) * 128], c.ident_bf[:, :])
                K.copy("act" if b % 2 == 0 else "dve", hT[:, :, b * 128:(b + 1) * 128],
                       V(pt.h[:, :].rearrange("p (k t) -> p k t", k=8), (pt.buf,)))
            for j in range(12):
                pf = c.ps[2 + (j % 2)]
                for kc in range(8):
                    K.mm(pf[:, 0:TW], win[:, kc, j * 128:(j + 1) * 128], hT[:, kc, :], kc == 0, kc == 7)
                K.copy("act" if j % 2 == 0 else "dve", qkvst[:, j, :], pf[:, 0:TW])
            K.dma("pool", dv(qkvT_v[:, :, 2 + i * TW:2 + (i + 1) * TW]), qkvst[:, :, :])
            for b in range(NB):
                blk = i * NB + b
                hb = lambda kc: hT[:, kc, b * 128:(b + 1) * 128]
                pz = c.ps[4]
                for kc in range(8):
                    K.mm(pz[:, :], hb(kc), win[:, kc, 1536:2048], kc == 0, kc == 7)
                z = zs_t[blk % 2]
                K.act(z[:, :], pz[:, :], AF.Silu)
                K.dma("pool", dv(d["zs"][blk * 128:(blk + 1) * 128, :]), z[:, :])
                sgt = sg_t[blk % 2]
                for g4 in range(4):
                    pgt = c.ps[5 + (g4 % 2)]
                    for kc in range(8):
                        K.mm(pgt[:, :], hb(kc), win[:, kc, 2736 + g4 * 512:2736 + (g4 + 1) * 512], kc == 0, kc == 7)
                    K.act(sgt[:, g4 * 512:(g4 + 1) * 512], pgt[:, :], AF.Sigmoid)
                K.dma("pool", dv(d["sgd"][blk * 128:(blk + 1) * 128, :]), sgt[:, :])
                p2 = c.ps[7]
                for kc in range(8):
                    K.mm(p2[:, 0:400], hb(kc), win[:, kc, 2048:2448], kc == 0, kc == 7)
                p3 = c.ps[4]
                bgt = bg_t[blk % 2]
                K.act(bgt[:, 0:8], p2[:, 0:8], AF.Sigmoid)
                tt_ = sp_t[:, 0, :]
                K.tt("dve", tt_, p2[:, 8:16], dtbrow[:, :], ALU.add)
                ab = sp_t[:, 1, :]
                K.act(ab, tt_, AF.Abs)
                ee = sp_t[:, 2, :]
                K.act(ee, ab, AF.Exp, scale=-1.0)
                K.act(ee, ee, AF.Ln, bias=c.one_t[:, 0:1])
                sp_ = sp_t[:, 3, :]
                K.stt("dve", sp_, tt_, 0.0, ee, ALU.max, ALU.add)
                K.tt("dve", bgt[:, 8:16], sp_, negA[:, :], ALU.mult)
                K.dma("pool", dv(d["bg"][blk * 128:(blk + 1) * 128, :]), bgt[:, :])
                K.act(junk[:, 0:384], p2[:, 16:400], AF.Square, accum=ss2[:, 0:1])
                K.act(rs2[:, 0:1], ss2[:, 0:1], AF.Sqrt, bias=c.eps_t[:, 0:1], scale=1.0 / 384)
                K.recip(rs2[:, 0:1], rs2[:, 0:1])
                K.stt("dve", cqn[:, :], p2[:, 16:400], rs2[:, 0:1], qnrow[:, :], ALU.mult, ALU.mult)
                ptq = c.psb[0]
                for kc in range(3):
                    K.tr(ptq[:, kc * 128:(kc + 1) * 128], cqn[:, kc * 128:(kc + 1) * 128], c.ident_bf[:, :])
                K.copy("act", cqnT[:, :, :], V(ptq.h[:, 0:384].rearrange("p (k t) -> p k t", k=3), (ptq.buf,)))
                for kc in range(8):
                    K.mm(p3[:, 0:288], hb(kc), win[:, kc, 2448:2736], kc == 0, kc == 7)
                for hh in range(2):
                    pq = c.ps[5 + hh]
                    for kc in range(3):
                        K.mm(pq[:, 0:384], cqnT[:, kc, :], wuq[:, kc, hh * 384:(hh + 1) * 384], kc == 0, kc == 2)
                    pqv = V(pq.h[:, 0:384].rearrange("p (h e) -> p h e", h=4), (pq.buf,))
                    K.ts("dve", qs[:, hh * 4:(hh + 1) * 4, 0:64], V(pqv.ap[:, :, 0:64], pqv.bufs), MLA_SCALE, ALU.mult)
                    rope_tm(K, c, "dve", qs[:, hh * 4:(hh + 1) * 4, 64:96], V(pqv.ap[:, :, 64:96], pqv.bufs),
                            c.cos_s[:, blk, :], c.sin_s[:, blk, :], 4, rtmp)
                pqt = c.psb[1]
                for h in range(8):
                    K.tr(pqt[:, h * 128:(h + 1) * 128], qs[:, h, :], c.ident_bf[:, :])
                qTt = qT_t[blk % 2]
                K.copy("act", qTt[:, :, :], V(pqt.h[:, :].rearrange("p (h t) -> p h t", h=8), (pqt.buf,)))
                K.dma("pool", dv(QT_v[:, :, blk * 128:(blk + 1) * 128]), qTt[:, :, :])
                lt = lat_t[blk % 2]
                K.act(junk[:, 0:256], p3[:, 0:256], AF.Square, accum=ss2[:, 1:2])
                K.act(rs2[:, 1:2], ss2[:, 1:2], AF.Sqrt, bias=c.eps_t[:, 0:1], scale=1.0 / 256)
                K.recip(rs2[:, 1:2], rs2[:, 1:2])
                K.stt("dve", lt[:, 0:256], p3[:, 0:256], rs2[:, 1:2], kvnrow[:, :], ALU.mult, ALU.mult)
                rope_tm(K, c, "dve", V(lt.h[:, 256:288].rearrange("p (h e) -> p h e", h=1), (lt.buf,)),
                        V(p3.h[:, 256:288].rearrange("p (h e) -> p h e", h=1), (p3.buf,)),
                        c.cos[:, blk, :], c.sin[:, blk, :], 1, rtmp)
                CR = min(1024, T)
                K.dma("pool", dv(d["lat_src%d" % (blk * 128 // CR)][(blk * 128) % CR:(blk * 128) % CR + 128, :]), lt[:, :])
        K.barrier()
        hl = sb(nc, stack, "hl", [128, 12, 2], F32)
        K.dma("sp", hl[:, :, :], dv(qkvT_v[:, :, T:T + 2]))
        K.dma("sp", dv(d["halo_src"].rearrange("(c p) t -> p c t", p=128)), hl[:, :, :])
        K.barrier()


def collective_gather(K, c, src_h, dst_h):
    nc = c.nc
    K.barrier()
    groups = [[2 * i, 2 * i + 1] for i in range(c.ncores // 2)]
    ins = nc.gpsimd.collective_compute("AllGather", ALU.bypass, replica_groups=groups,
                                       ins=[src_h.ap().opt()], outs=[dst_h.ap().opt()])
    c.cc_cnt += 1
    ins.then_inc(c.cc_sem)
    K.ninst += 1
    for e in K.engs:
        K.engs[e].wait_ge(c.cc_sem, c.cc_cnt)


def sel_other(K, c, e, out, slot0, slot1):
    K.ts(e, out, slot0, c.sel[:, 0:1], ALU.mult)
    K.stt(e, out, slot1, c.sel[:, 1:2], out, ALU.mult, ALU.add)


def phase_x1(K, c, l, d):
    import contextlib
    nc = c.nc
    for ch in range(T // min(1024, T)):
        collective_gather(K, c, d["lat_src%d_h" % ch], d["lat_all%d_h" % ch])
    collective_gather(K, c, d["halo_src_h"], d["halo_all_h"])
    with contextlib.ExitStack() as stack:
        ha = sb(nc, stack, "ha", [128, 2, 12, 2], F32)
        ho = sb(nc, stack, "ho", [128, 12, 2], F32)
        hz = sb(nc, stack, "hz", [128, 12, 2], F32)
        K.dma("sp", ha[:, :, :, :], dv(d["halo_all"].rearrange("(s c p) t -> p s c t", p=128, s=2)))
        sel_other(K, c, "dve", ho[:, :, :], ha[:, 0, :, :], ha[:, 1, :, :])
        qkvT_v = d["qkvT"].rearrange("(c p) t -> p c t", p=128)
        K.dma("sp", dv(qkvT_v[:, :, T + 2:T + 3]), ho[:, :, 1:2], allow_slow_non_contiguous=True)
        K.dma("sp", dv(qkvT_v[:, :, T + 3:T + 4]), ho[:, :, 0:1], allow_slow_non_contiguous=True)
        K.memset("dve", hz[:, :, :], 0.0)
        K.dma("sp", dv(qkvT_v[:, :, 0:2]), hz[:, :, :])
        K.barrier()


def phase_a(K, c, l, d, aoT):
    import contextlib
    nc = c.nc
    NK = 2 * T
    NKB = NK // 128
    NKT = NK // 512
    QW = 512 if T >= 512 else T
    NQT = T // QW
    with contextlib.ExitStack() as stack:
        wukv = sb(nc, stack, "wukv", [128, 2, 1024], BF16)
        ckvnT = sb(nc, stack, "ckvnT", [128, 2, NK], BF16)
        KT = [sb(nc, stack, "KT%d" % i, [128, NK], BF16) for i in range(2)]
        Vh = [sb(nc, stack, "Vh%d" % i, [128, NKB, 128], BF16) for i in range(2)]
        QTh = [sb(nc, stack, "QTh%d" % i, [128, T], BF16) for i in range(2)]
        pT = [sb(nc, stack, "pT%d" % i, [128, QW], BF16) for i in range(3)]
        rs = sb(nc, stack, "rs_a", [128, QW], F32)
        bc = sb(nc, stack, "bc_a", [128, QW], F32)
        latf = [sb(nc, stack, "latf%d" % i, [128, 288], F32) for i in range(3)]
        latb = [sb(nc, stack, "latb%d" % i, [128, 320], BF16) for i in range(2)]
        for i in range(2):
            K.memset("pool", latb[i][:, 288:320], 0.0)
        K.memset("pool", rs[:, :], 0.0)
        with contextlib.ExitStack() as st2:
            stg = [sb(nc, st2, "stga%d" % i, [128, 1024], F32) for i in range(2)]
            load_cast(K, c, stg, [wukv[:, kc, :] for kc in range(2)],
                      [d["mla_w_ukv"][l][kc * 128:(kc + 1) * 128, :] for kc in range(2)])
            K.barrier()
        import os
        amode = int(os.environ.get("AMODE", "0"))
        if amode == 4:
            K.barrier()
            return
        for p in range(2):
            K.memset("pool", Vh[p][:, :, :], 0.0)
        K.memset("pool", Vh[0][:, :, 64:65], 1.0)
        K.memset("pool", Vh[1][:, :, 64:65], 1.0)
        rc = sb(nc, stack, "rc_a", [128, 4], F32)
        if amode == 5:
            K.barrier()
            return
        for kb in range(NKB):
            lf = latf[kb % 3]
            lb = latb[kb % 2]
            CR2 = 2 * min(1024, T)
            K.dma("sp" if kb % 2 == 0 else "act", lf[:, :],
                  dv(d["lat_all%d" % (kb * 128 // CR2)][(kb * 128) % CR2:(kb * 128) % CR2 + 128, :]))
            K.copy("pool", lb[:, 0:288], lf[:, :])
            pt = c.psb[5 + (kb % 2)]
            for kc in range(2):
                K.tr(pt[:, kc * 128:(kc + 1) * 128], lb[:, kc * 128:(kc + 1) * 128], c.ident_bf[:, :])
            pk_ = c.psb[3 + (kb % 2)]
            K.tr(pk_[:, 0:128], lb[:, 192:320], c.ident_bf[:, :])
            K.copy("dve", ckvnT[:, :, kb * 128:(kb + 1) * 128],
                   V(pt.h[:, 0:256].rearrange("p (k t) -> p k t", k=2), (pt.buf,)))
            K.copy("act", KT[0][64:128, kb * 128:(kb + 1) * 128], pk_[64:128, 0:128])
            K.copy("dve", KT[1][64:128, kb * 128:(kb + 1) * 128], pk_[64:128, 0:128])
        QT_d = d["QT"]
        if amode == 1:
            K.barrier()
            return
        K.dma("sp", QTh[0][:, :], dv(QT_d[0]))
        cnt = 0
        for h in range(8):
            par = h % 2
            kt_ = KT[par]
            vh = Vh[par]
            if h + 1 < 8:
                K.dma("sp", QTh[(h + 1) % 2][:, :], dv(QT_d[h + 1]))
            qh = QTh[h % 2]
            for kt in range(NKT):
                pk = c.ps[5 + (kt % 2)]
                for kc in range(2):
                    K.mm(pk[:, :], wukv[:, kc, h * 128:h * 128 + 128], ckvnT[:, kc, kt * 512:(kt + 1) * 512], kc == 0, kc == 1)
                K.copy("dve", kt_[0:64, kt * 512:(kt + 1) * 512], pk[0:64, :])
            voff = 0
            for k8 in range(NKB // 8):
                pv = c.ps[5 + (k8 % 2)]
                for j in range(8):
                    kb = k8 * 8 + j
                    for kc in range(2):
                        K.mm(pv[:, j * 64:(j + 1) * 64], ckvnT[:, kc, kb * 128:(kb + 1) * 128],
                             wukv[:, kc, h * 128 + 64:h * 128 + 128], kc == 0, kc == 1)
                K.copy("dve", vh[:, k8 * 8:(k8 + 1) * 8, voff:voff + 64],
                       V(pv.h[:, :].rearrange("p (j e) -> p j e", j=8), (pv.buf,)))
            if amode == 2:
                continue
            for qt in range(NQT):
                po = c.ps[3 + (cnt % 2)]
                cnt += 1
                qv = qh[:, qt * QW:(qt + 1) * QW]
                nsub = QW // 128
                for kk in range(min(2, NKB)):
                    K.mm(c.ps[kk % 3][:, 0:QW], kt_[:, kk * 128:(kk + 1) * 128], qv, True, True)
                for kb in range(NKB):
                    if kb + 2 < NKB:
                        K.mm(c.ps[(kb + 2) % 3][:, 0:QW], kt_[:, (kb + 2) * 128:(kb + 3) * 128], qv, True, True)
                    K.act(pT[kb % 3][:, :], c.ps[kb % 3][:, 0:QW], AF.Exp)
                    for j in range(nsub):
                        K.mm(po[:, j * 128:j * 128 + 65], pT[kb % 3][:, j * 128:(j + 1) * 128], vh[:, kb, 0:65],
                             kb == 0 and j == 0, kb == NKB - 1 and j == nsub - 1)
                pov = V(po.h[:, 0:nsub * 128].rearrange("p (j e) -> p j e", j=nsub), (po.buf,))
                rcv = V(rc.h[:, 0:nsub].unsqueeze(2), (rc.buf,))
                K.recip(rcv, V(pov.ap[:, :, 64:65], pov.bufs))
                K.tt("dve", aoT[:, qt * nsub:(qt + 1) * nsub, h * 64:(h + 1) * 64], V(pov.ap[:, :, 0:64], pov.bufs),
                     V(rc.h[:, 0:nsub].unsqueeze(2).to_broadcast([128, nsub, 64]), (rc.buf,)), ALU.mult)
        K.barrier()


def phase_m(K, c, l, d, aoT, goT):
    import contextlib
    nc = c.nc
    with contextlib.ExitStack() as stack:
        wpa = sb(nc, stack, "wpa", [128, 4, D_MODEL], BF16)
        wpb = sb(nc, stack, "wpb", [128, 4, D_MODEL], BF16)
        wo = sb(nc, stack, "wo", [128, 8, D_MODEL], BF16)
        with contextlib.ExitStack() as st2:
            stg = [sb(nc, st2, "stgm%d" % i, [128, D_MODEL], F32) for i in range(3)]
            dsts = [wpa[:, kc, :] for kc in range(4)] + [wpb[:, kc, :] for kc in range(4)] + [wo[:, kc, :] for kc in range(8)]
            srcs = [d["gdn_proj"][l][kc * 128:(kc + 1) * 128, :] for kc in range(4)] + \
                   [d["mla_proj"][l][kc * 128:(kc + 1) * 128, :] for kc in range(4)] + \
                   [d["w_out"][l][kc * 128:(kc + 1) * 128, :] for kc in range(8)]
            load_cast(K, c, stg, dsts, srcs)
            K.barrier()
        xt = [sb(nc, stack, "xm%d" % i, [128, D_MODEL], F32) for i in range(2)]
        sgt = [sb(nc, stack, "sgm%d" % i, [128, 2048], F32) for i in range(2)]
        ya = sb(nc, stack, "ya", [128, D_MODEL], F32)
        yb = sb(nc, stack, "yb", [128, D_MODEL], BF16)
        yT = sb(nc, stack, "yT", [128, 8, 128], BF16)
        aoTb = sb(nc, stack, "aoTb", [128, 4, 128], BF16)
        nb = T // 128
        K.dma("sp", xt[0][:, :], dv(d["xres"][0:128, :]))
        K.dma("act", sgt[0][:, :], dv(d["sgd"][0:128, :]))
        for blk in range(nb):
            x = xt[blk % 2]
            sg = sgt[blk % 2]
            if blk + 1 < nb:
                K.dma("sp", xt[(blk + 1) % 2][:, :], dv(d["xres"][(blk + 1) * 128:(blk + 2) * 128, :]))
                K.dma("act", sgt[(blk + 1) % 2][:, :], dv(d["sgd"][(blk + 1) * 128:(blk + 2) * 128, :]))
            ptA = c.psb[7]
            for kc in range(4):
                K.tr(ptA[:, kc * 128:(kc + 1) * 128], aoT[:, blk, kc * 128:(kc + 1) * 128], c.ident_bf[:, :])
            K.copy("act", aoTb[:, :, :], V(ptA.h[:, 0:512].rearrange("p (k t) -> p k t", k=4), (ptA.buf,)))
            for hf in range(2):
                pa = c.ps[hf]
                pb = c.ps[2 + hf]
                for kc in range(4):
                    K.mm(pa[:, :], goT[:, kc, blk * 128:(blk + 1) * 128], wpa[:, kc, hf * 512:(hf + 1) * 512], kc == 0, kc == 3)
                for kc in range(4):
                    K.mm(pb[:, :], aoTb[:, kc, :], wpb[:, kc, hf * 512:(hf + 1) * 512], kc == 0, kc == 3)
                K.tt("dve", ya[:, hf * 512:(hf + 1) * 512], pa[:, :], sg[:, hf * 512:(hf + 1) * 512], ALU.mult)
                K.tt("dve", sg[:, 1024 + hf * 512:1024 + (hf + 1) * 512], pb[:, :], sg[:, 1024 + hf * 512:1024 + (hf + 1) * 512], ALU.mult)
                K.tt("pool", yb[:, hf * 512:(hf + 1) * 512], ya[:, hf * 512:(hf + 1) * 512],
                     sg[:, 1024 + hf * 512:1024 + (hf + 1) * 512], ALU.add)
            pt = c.psb[4]
            for kc in range(8):
                K.tr(pt[:, kc * 128:(kc + 1) * 128], yb[:, kc * 128:(kc + 1) * 128], c.ident_bf[:, :])
            K.copy("act", yT[:, :, :], V(pt.h[:, :].rearrange("p (k t) -> p k t", k=8), (pt.buf,)))
            for hf in range(2):
                po = c.ps[5 + hf]
                for kc in range(8):
                    K.mm(po[:, :], yT[:, kc, :], wo[:, kc, hf * 512:(hf + 1) * 512], kc == 0, kc == 7)
                K.tt("dve", x[:, hf * 512:(hf + 1) * 512], po[:, :], x[:, hf * 512:(hf + 1) * 512], ALU.add)
            K.dma("pool", dv(d["xres"][blk * 128:(blk + 1) * 128, :]), x[:, :])
        K.barrier()


def phase_g0(K, c, l, d):
    import contextlib
    nc = c.nc
    TG = 512 if T >= 512 else T
    NB = TG // 128
    with contextlib.ExitStack() as stack:
        cw = sb(nc, stack, "cw", [128, 12, 5], F32)
        K.dma("sp", cw[:, :, :], dv(d["gdn_conv"][l].rearrange("(c p) k -> p c k", p=128)))
        xin = [sb(nc, stack, "xin%d" % i, [128, 12, TG + 4], F32) for i in range(2)]
        y = [sb(nc, stack, "ycv%d" % i, [128, TG], F32) for i in range(3)]
        ysil = sb(nc, stack, "ysil", [128, 12, TG], F32)
        ytmp = sb(nc, stack, "ytmp", [128, TG], F32)
        tm = [sb(nc, stack, "tm%d" % i, [128, 1536], F32) for i in range(2)]
        ss8 = sb(nc, stack, "ss8", [128, 8], F32)
        rs8 = sb(nc, stack, "rs8", [128, 8], F32)
        junk = sb(nc, stack, "junkg", [128, 128], F32)
        qkvT_v = d["qkvT"].rearrange("(c p) t -> p c t", p=128)
        ntile = T // TG
        K.dma("sp", xin[0][:, :, :], dv(qkvT_v[:, :, 0:TG + 4]))
        for i in range(ntile):
            xi = xin[i % 2]
            if i + 1 < ntile:
                K.dma("sp", xin[(i + 1) % 2][:, :, :], dv(qkvT_v[:, :, (i + 1) * TG:(i + 1) * TG + TG + 4]))
            for cc in range(12):
                e = "pool" if cc in (5, 11) else "dve"
                yy = y[cc % 2] if e == "dve" else y[2]
                K.ts(e, yy[:, :], xi[:, cc, 0:TG], cw[:, cc, 0:1], ALU.mult)
                for k in range(1, 5):
                    if e == "dve":
                        K.stt(e, yy[:, :], xi[:, cc, k:k + TG], cw[:, cc, k:k + 1], yy[:, :], ALU.mult, ALU.add)
                    else:
                        K.ts(e, ytmp[:, :], xi[:, cc, k:k + TG], cw[:, cc, k:k + 1], ALU.mult)
                        K.tt(e, yy[:, :], yy[:, :], ytmp[:, :], ALU.add)
                K.act(ysil[:, cc, :], yy[:, :], AF.Silu)
            for b in range(NB):
                blk = i * NB + b
                t_ = tm[blk % 2]
                for cc in range(12):
                    K.tr(c.ps[cc // 4][:, (cc % 4) * 128:(cc % 4 + 1) * 128], ysil[:, cc, b * 128:(b + 1) * 128], c.ident_f[:, :])
                for g in range(8):
                    K.act(junk[:, :], c.ps[g // 4][:, (g % 4) * 128:(g % 4 + 1) * 128], AF.Square, accum=ss8[:, g:g + 1])
                K.act(rs8[:, :], ss8[:, :], AF.Sqrt, bias=c.eps_t[:, 0:1], scale=1.0)
                K.recip(rs8[:, :], rs8[:, :])
                K.ts("dve", rs8[:, 0:4], rs8[:, 0:4], 128.0 ** -0.5, ALU.mult)
                for g in range(8):
                    K.ts("dve", t_[:, g * 128:(g + 1) * 128], c.ps[g // 4][:, (g % 4) * 128:(g % 4 + 1) * 128], rs8[:, g:g + 1], ALU.mult)
                K.copy("act", t_[:, 1024:1536], c.ps[2][:, :])
                K.dma("pool", dv(d["qkvn"][blk * 128:(blk + 1) * 128, :]), t_[:, :])
        K.barrier()


def phase_gscan(K, c, l, d, dirn, S, goT=None):
    import contextlib
    nc = c.nc
    nb = T // 128
    tri = c.tri[dirn]
    negm4 = c.negm4[dirn]
    nstr4 = c.nstr4[dirn]
    with contextlib.ExitStack() as stack:
        qkv = [sb(nc, stack, "qkvg%d" % i, [128, 1536], F32) for i in range(2)]
        bgt = [sb(nc, stack, "bgg%d" % i, [128, 16], F32) for i in range(2)]
        kqT = sb(nc, stack, "kqT", [128, 4, 256], F32)
        GT = sb(nc, stack, "GT", [128, 4, 128], F32)
        sm = sb(nc, stack, "smg", [128, 6, 4], F32)
        DT = sb(nc, stack, "DT", [128, 4, 128], F32)
        Xt = sb(nc, stack, "Xt", [128, 4, 128], F32)
        Xs = [sb(nc, stack, "Xs%d" % i, [128, 4, 128], F32) for i in range(2)]
        Ys = [sb(nc, stack, "Ys%d" % i, [128, 4, 128], F32) for i in range(2)]
        Ns = [sb(nc, stack, "Ns%d" % i, [128, 4, 128], F32) for i in range(2)]
        A2 = sb(nc, stack, "A2", [128, 4, 128], F32)
        rhs = sb(nc, stack, "rhsg", [128, 4, 256], F32)
        UW = sb(nc, stack, "UW", [128, 4, 256], F32)
        WT = sb(nc, stack, "WT", [128, 4, 128], F32)
        qg = sb(nc, stack, "qg", [128, 4, 128], F32)
        qgT = sb(nc, stack, "qgT", [128, 4, 128], F32)
        kd = sb(nc, stack, "kd", [128, 4, 128], F32)
        VNs = [sb(nc, stack, "VN%d" % i, [128, 4, 128], F32) for i in range(2)]
        for i in range(2):
            K.memset("pool", VNs[i][:, :, :], 0.0)
        O = [sb(nc, stack, "Og%d" % i, [128, 4, 128], F32) for i in range(2)]
        if dirn == 1:
            o1t = [sb(nc, stack, "o1t%d" % i, [128, 4, 128], F32) for i in range(2)]
            zt = [sb(nc, stack, "ztg%d" % i, [128, 4, 128], F32) for i in range(2)]
            ob = sb(nc, stack, "obg", [128, 4, 128], BF16)
            nrow = sb(nc, stack, "nrowg", [128, 128], F32)
            ss4 = sb(nc, stack, "ss4", [128, 4], F32)
            rs4 = sb(nc, stack, "rs4", [128, 4], F32)
            junk = sb(nc, stack, "junkgs", [128, 128], F32)
            K.dma("sp", nrow[:, :], dv(d["gdn_norm"][l].partition_broadcast(128)))
        ident4 = V(c.ident_f.h[:, :].unsqueeze(1).to_broadcast([128, 4, 128]), (c.ident_f.buf,))
        order = list(range(nb)) if dirn == 0 else list(range(nb - 1, -1, -1))

        def loads(j, blk):
            K.dma("sp", qkv[j % 2][:, :], dv(d["qkvn"][blk * 128:(blk + 1) * 128, :]))
            K.dma("act", bgt[j % 2][:, :], dv(d["bg"][blk * 128:(blk + 1) * 128, :]))
            if dirn == 1:
                K.dma("sp", o1t[j % 2][:, :, :], dv(d["o1"][blk * 128:(blk + 1) * 128, :].rearrange("p (h e) -> p h e", h=4)))
                K.dma("act", zt[j % 2][:, :, :], dv(d["zs"][blk * 128:(blk + 1) * 128, :].rearrange("p (h e) -> p h e", h=4)))

        loads(0, order[0])
        for j, blk in enumerate(order):
            if j + 1 < nb:
                loads(j + 1, order[j + 1])
            qk = qkv[j % 2]
            bg_ = bgt[j % 2]
            beta = lambda h: bg_[:, dirn * 4 + h:dirn * 4 + h + 1]
            g4 = bg_[:, 8 + dirn * 4:8 + dirn * 4 + 4]
            gcol = lambda h: bg_[:, 8 + dirn * 4 + h:8 + dirn * 4 + h + 1]
            qv = lambda h: qk[:, h * 128:(h + 1) * 128]
            kv = lambda h: qk[:, 512 + h * 128:512 + (h + 1) * 128]
            for hh in range(2):
                pt = c.ps[hh]
                for h2 in range(2):
                    h = hh * 2 + h2
                    K.tr(pt[:, h2 * 256:h2 * 256 + 128], kv(h), c.ident_f[:, :])
                    K.tr(pt[:, h2 * 256 + 128:h2 * 256 + 256], qv(h), c.ident_f[:, :])
                K.copy("act" if hh == 0 else "dve", kqT[:, hh * 2:hh * 2 + 2, :],
                       V(pt.h[:, :].rearrange("p (h e) -> p h e", h=2), (pt.buf,)))
            pg = c.ps[2]
            K.mm(pg[:, 0:4], tri[:, :], g4, True, True)
            K.mm(pg[:, 4:8], c.same[:, :], g4, True, True)
            K.mm(pg[:, 8:12], c.ch[0][:, :], g4, True, True)
            K.mm(pg[:, 12:16], c.ch[1][:, :], g4, True, True)
            gc = sm[:, 0, :]
            ngc = sm[:, 1, :]
            egc = sm[:, 2, :]
            ekd = sm[:, 3, :]
            K.copy("dve", gc, pg[:, 0:4])
            K.ts("dve", ngc, pg[:, 0:4], -1.0, ALU.mult)
            K.act(egc, pg[:, 0:4], AF.Exp)
            K.tt("dve", ekd, pg[:, 4:8], gc, ALU.subtract)
            K.act(ekd, ekd, AF.Exp)
            K.act(V(sm.h[:, 4:6, :], (sm.buf,)), V(pg.h[:, 8:16].rearrange("p (a b) -> p a b", a=2), (pg.buf,)), AF.Exp)
            for h in range(4):
                K.ts("dve", GT[:, h, :], tri[:, :], gcol(h), ALU.mult)
            pR = c.ps[3]
            GTf = V(GT.h[:, :, :].rearrange("p h e -> p (h e)"), (GT.buf,))
            K.mm(pR[:, :], c.ones_f[:, :], GTf, True, False)
            K.mm(pR[:, :], c.ident_f[:, :], negm4[:, :], False, True)
            for h in range(4):
                K.act(DT[:, h, :], pR[:, h * 128:(h + 1) * 128], AF.Exp, bias=sm[:, 1, h:h + 1])
            pK = [c.ps[4], c.ps[5]]
            for h in range(4):
                K.mm(pK[h // 2][:, (h % 2) * 256:(h % 2) * 256 + 256], kqT[:, h, 0:128], kqT[:, h, :], True, True)
            for h in range(4):
                K.stt("dve", Xt[:, h, :], pK[h // 2][:, (h % 2) * 256:(h % 2) * 256 + 128], beta(h), DT[:, h, :], ALU.mult, ALU.mult)
                K.tt("dve", A2[:, h, :], pK[h // 2][:, (h % 2) * 256 + 128:(h % 2) * 256 + 256], DT[:, h, :], ALU.mult)
            X, Y, N = Xs[0], Ys[0], Ns[0]
            K.tt("dve", X[:, :, :], Xt[:, :, :], nstr4[:, :, :], ALU.mult)
            pY = c.ps[6]
            for h in range(4):
                K.tr(pY[:, h * 128:(h + 1) * 128], X[:, h, :], c.ident_f[:, :])
            K.copy("act", Y[:, :, :], V(pY.h[:, :].rearrange("p (h e) -> p h e", h=4), (pY.buf,)))
            K.tt("pool", N[:, :, :], X[:, :, :], ident4, ALU.add)
            for lvl in range(5):
                last = lvl == 4
                Xn, Yn, Nn = Xs[(lvl + 1) % 2], Ys[(lvl + 1) % 2], Ns[(lvl + 1) % 2]
                pY2 = c.ps[6 + (lvl % 2)]
                for h in range(4):
                    K.mm(pY2[:, h * 128:(h + 1) * 128], X[:, h, :], Y[:, h, :], True, True)
                if not last:
                    pX2 = c.ps[0 + (lvl % 2)]
                    for h in range(4):
                        K.mm(pX2[:, h * 128:(h + 1) * 128], Y[:, h, :], X[:, h, :], True, True)
                K.copy("act", Yn[:, :, :], V(pY2.h[:, :].rearrange("p (h e) -> p h e", h=4), (pY2.buf,)))
                if not last:
                    K.copy("dve", Xn[:, :, :], V(pX2.h[:, :].rearrange("p (h e) -> p h e", h=4), (pX2.buf,)))
                pN = c.ps[2 + (lvl % 2)]
                for h in range(4):
                    K.mm(pN[:, h * 128:(h + 1) * 128], Yn[:, h, :], N[:, h, :], True, True)
                K.tt("dve", Nn[:, :, :], V(pN.h[:, :].rearrange("p (h e) -> p h e", h=4), (pN.buf,)), N[:, :, :], ALU.add)
                X, Y, N = Xn, Yn, Nn
            K.copy("pool", rhs[:, :, 0:128], V(qk.h[:, 1024:1536].rearrange("p (h e) -> p h e", h=4), (qk.buf,)))
            for h in range(4):
                K.ts("pool", rhs[:, h, 128:256], kv(h), sm[:, 2, h:h + 1], ALU.mult)
            pU = [c.ps[4], c.ps[5]]
            for h in range(4):
                K.mm(pU[h // 2][:, (h % 2) * 256:(h % 2) * 256 + 256], N[:, h, :], rhs[:, h, :], True, True)
            for h in range(4):
                K.ts("dve", UW[:, h, :], pU[h // 2][:, (h % 2) * 256:(h % 2) * 256 + 256], beta(h), ALU.mult)
            pW = c.ps[0]
            for h in range(4):
                K.tr(pW[:, h * 128:(h + 1) * 128], UW[:, h, 128:256], c.ident_f[:, :])
            K.copy("act", WT[:, :, :], V(pW.h[:, :].rearrange("p (h e) -> p h e", h=4), (pW.buf,)))
            for h in range(4):
                K.ts("pool", qg[:, h, :], qv(h), sm[:, 2, h:h + 1], ALU.mult)
                K.ts("pool", kd[:, h, :], kv(h), sm[:, 3, h:h + 1], ALU.mult)
            pQ = c.ps[1]
            for h in range(4):
                K.tr(pQ[:, h * 128:(h + 1) * 128], qg[:, h, :], c.ident_f[:, :])
            K.copy("dve", qgT[:, :, :], V(pQ.h[:, :].rearrange("p (h e) -> p h e", h=4), (pQ.buf,)))
            Ot = O[j % 2]
            for cch in ((0, 1) if dirn == 0 else (1, 0)):
                r0, r1 = cch * 64, cch * 64 + 64
                VN = VNs[cch]
                pV = c.ps[2]
                for h in range(4):
                    K.mm(pV[:, h * 128:(h + 1) * 128], WT[:, h, :], S[:, h, :], True, True)
                K.tt("dve", VN[r0:r1, :, :], UW[r0:r1, :, 0:128],
                     V(pV.h[r0:r1, :].rearrange("p (h e) -> p h e", h=4), (pV.buf,)), ALU.subtract)
                pO = c.ps[3]
                for h in range(4):
                    K.mm(pO[:, h * 128:(h + 1) * 128], qgT[:, h, :], S[:, h, :], True, False)
                    K.mm(pO[:, h * 128:(h + 1) * 128], A2[:, h, :], VN[:, h, :], False, True)
                K.copy("act", Ot[r0:r1, :, :], V(pO.h[r0:r1, :].rearrange("p (h e) -> p h e", h=4), (pO.buf,)))
                pS = c.ps[6]
                for h in range(4):
                    K.mm(pS[:, h * 128:(h + 1) * 128], kd[:, h, :], VN[:, h, :], True, True)
                for h in range(4):
                    K.stt("dve", S[:, h, :], S[:, h, :], sm[:, 4 + cch, h:h + 1], pS[:, h * 128:(h + 1) * 128], ALU.mult, ALU.add)
            if dirn == 0:
                K.dma("pool", dv(d["o1"][blk * 128:(blk + 1) * 128, :].rearrange("p (h e) -> p h e", h=4)), Ot[:, :, :])
            else:
                K.tt("pool", Ot[:, :, :], Ot[:, :, :], o1t[j % 2][:, :, :], ALU.add)
                for h in range(4):
                    K.act(junk[:, :], Ot[:, h, :], AF.Square, accum=ss4[:, h:h + 1])
                K.act(rs4[:, :], ss4[:, :], AF.Sqrt, bias=c.eps_t[:, 0:1], scale=1.0 / 128)
                K.recip(rs4[:, :], rs4[:, :])
                for h in range(4):
                    K.stt("dve", Ot[:, h, :], Ot[:, h, :], rs4[:, h:h + 1], nrow[:, :], ALU.mult, ALU.mult)
                K.tt("pool", ob[:, :, :], Ot[:, :, :], zt[j % 2][:, :, :], ALU.mult)
                pG = c.psb[7]
                for h in range(4):
                    K.tr(pG[:, h * 128:(h + 1) * 128], ob[:, h, :], c.ident_bf[:, :])
                K.copy("act", goT[:, :, blk * 128:(blk + 1) * 128], V(pG.h[:, 0:512].rearrange("p (h e) -> p h e", h=4), (pG.buf,)))
        K.barrier()


def phase_x2(K, c, l, d, S):
    import contextlib
    nc = c.nc
    K.dma("sp", dv(d["st_src"].rearrange("(h p) v -> p h v", p=128)), S[:, :, :])
    collective_gather(K, c, d["st_src_h"], d["st_all_h"])
    with contextlib.ExitStack() as stack:
        sa = sb(nc, stack, "sa", [128, 2, 4, 128], F32)
        K.dma("sp", sa[:, :, :, :], dv(d["st_all"].rearrange("(s h p) v -> p s h v", p=128, s=2)))
        sel_other(K, c, "dve", S[:, :, :], sa[:, 0, :, :], sa[:, 1, :, :])
        K.barrier()


CONST_COLS = 128 * 9 + 512 * 4


def make_consts():
    idx = np.arange(128)
    same = (idx[:, None] // 64) == (idx[None, :] // 64)
    cs = np.zeros((128, CONST_COLS), np.float32)
    cs[:, 0:128] = np.eye(128)
    cs[:, 128:256] = same
    cs[:, 256:384] = 1.0
    for dirn in range(2):
        if dirn == 0:
            tri = same & (idx[:, None] <= idx[None, :])
            allow = same & (idx[None, :] >= idx[:, None])
            strict = same & (idx[None, :] > idx[:, None])
        else:
            tri = same & (idx[:, None] >= idx[None, :])
            allow = same & (idx[None, :] <= idx[:, None])
            strict = same & (idx[None, :] < idx[:, None])
        cs[:, 384 + dirn * 128:384 + (dirn + 1) * 128] = tri
        cs[:, 640 + dirn * 512:640 + (dirn + 1) * 512] = np.tile(np.where(allow, 0.0, -30000.0), (1, 4))
        cs[:, 1664 + dirn * 512:1664 + (dirn + 1) * 512] = np.tile(np.where(strict, -1.0, 0.0), (1, 4))
    cs[0:64, 2688:2816] = 1.0
    cs[64:128, 2816:2944] = 1.0
    cs[64, 2944:3072] = 1.0
    cs[0, 3072:3200] = 1.0
    return cs


WEIGHT_SPECS = [
    ("norm_ffn1", [DEPTH, D_MODEL]), ("ffn1_w_gate", [DEPTH, D_MODEL, D_FF]), ("ffn1_w_up", [DEPTH, D_MODEL, D_FF]),
    ("ffn1_w_down", [DEPTH, D_FF, D_MODEL]), ("norm_mix", [DEPTH, D_MODEL]), ("w_in", [DEPTH, D_MODEL, D_IN]),
    ("gdn_conv", [DEPTH, 1536, 5]), ("gdn_A_log", [DEPTH, 8]), ("gdn_dt_bias", [DEPTH, 8]), ("gdn_norm", [DEPTH, 128]),
    ("gdn_proj", [DEPTH, 512, D_MODEL]), ("mla_q_norm", [DEPTH, 384]), ("mla_w_uq", [DEPTH, 384, 768]),
    ("mla_kv_norm", [DEPTH, 256]), ("mla_w_ukv", [DEPTH, 256, 1024]), ("mla_proj", [DEPTH, 512, D_MODEL]),
    ("w_out", [DEPTH, D_MODEL, D_MODEL]), ("norm_ffn2", [DEPTH, D_MODEL]), ("ffn2_w_gate", [DEPTH, D_MODEL, D_FF]),
    ("ffn2_w_up", [DEPTH, D_MODEL, D_FF]), ("ffn2_w_down", [DEPTH, D_FF, D_MODEL]), ("final_norm", [D_MODEL]),
]


def build(cfg):
    import contextlib
    nc = bass.Bass("TRN2", target_bir_lowering=False)
    K = KB(nc)
    c = Ctx()
    c.nc = nc
    c.K = K
    c.ncores = cfg.get("ncores", 8)
    c.cc_sem = nc.alloc_semaphore("cc_sem")
    c.cc_cnt = 0
    d = {}

    def inp(name, shape, dtype=F32):
        d[name] = nc.dram_tensor(name, shape, dtype, kind="ExternalInput").ap()
        return d[name]

    inp("x", [T, D_MODEL])
    inp("pos_pm", [128, T // 128], I32)
    inp("inv_freq", [16])
    inp("consts", [128, CONST_COLS])
    inp("sel", [2])
    for name, shape in WEIGHT_SPECS:
        inp(name, shape)
    y_out = nc.dram_tensor("y", [T, D_MODEL], F32, kind="ExternalOutput").ap()

    def scratch(name, shape, dtype=F32):
        h = nc.dram_tensor(name, shape, dtype)
        d[name + "_h"] = h
        d[name] = h.ap()

    scratch("xres", [T, D_MODEL])
    scratch("qkvT", [1536, T + 4])
    scratch("zs", [T, 512])
    scratch("sgd", [T, 2048])
    scratch("bg", [T, 16])
    scratch("QT", [8, 128, T], BF16)
    for ch in range(T // min(1024, T)):
        scratch("lat_src%d" % ch, [min(1024, T), 288])
        scratch("lat_all%d" % ch, [2 * min(1024, T), 288])
    scratch("halo_src", [1536, 2])
    scratch("halo_all", [2 * 1536, 2])
    scratch("qkvn", [T, 1536])
    scratch("o1", [T, 512])
    scratch("st_src", [512, 128])
    scratch("st_all", [1024, 128])

    dump = cfg.get("dump", ())
    stop = cfg.get("stop", None)
    dbg = {}

    def dbg_out(name, shape, dtype=F32):
        dbg[name] = nc.dram_tensor("dbg_" + name, shape, dtype, kind="ExternalOutput").ap()
        return dbg[name]

    with contextlib.ExitStack() as stack:
        c.eps_t = sb(nc, stack, "eps_t", [128, 1], F32)
        c.one_t = sb(nc, stack, "one_t", [128, 1], F32)
        K.memset("dve", c.eps_t[:, :], EPS)
        K.memset("dve", c.one_t[:, :], 1.0)
        cst = sb(nc, stack, "cst", [128, CONST_COLS], F32)
        K.dma("sp", cst[:, :], dv(d["consts"]))

        def cview(lo, hi, shape3=None):
            t = TT(cst.h[:, lo:hi] if shape3 is None else cst.h[:, lo:hi].rearrange("p (h e) -> p h e", h=4), "cst")
            t.buf = cst.buf
            return t
        c.ident_f = cview(0, 128)
        c.same = cview(128, 256)
        c.ones_f = cview(256, 384)
        c.tri = [cview(384, 512), cview(512, 640)]
        c.negm4 = [cview(640, 1152), cview(1152, 1664)]
        c.nstr4 = [cview(1664, 2176, True), cview(2176, 2688, True)]
        c.ch = [cview(2688, 2816), cview(2816, 2944)]
        c.rowsel = [cview(2944, 3072), cview(3072, 3200)]
        c.ident_bf = sb(nc, stack, "ident_bf", [128, 128], BF16)
        K.copy("dve", c.ident_bf[:, :], c.ident_f[:, :])
        c.sel = sb(nc, stack, "sel", [128, 2], F32)
        K.dma("sp", c.sel[:, :], dv(d["sel"].partition_broadcast(128)))
        c.ps = []
        c.psb = []
        for i in range(8):
            h = stack.enter_context(nc.psum_tensor("ps%d" % i, [128, 512], F32))
            t = TT(h, "ps%d" % i)
            t.buf.x = True
            c.ps.append(t)
            tb = TT(h[:, :].bitcast(BF16), "psb%d" % i)
            tb.buf = t.buf
            c.psb.append(tb)
        prologue_rope(K, c, stack, d["pos_pm"], d["inv_freq"])

        def run():
            src = d["x"]
            for l in range(DEPTH):
                phase_ffn(K, c, stack, src, d["xres"], d["ffn1_w_gate"][l], d["ffn1_w_up"][l], d["ffn1_w_down"][l],
                          d["norm_ffn1"][l])
                src = d["xres"]
                if stop == ("f1", l):
                    return
                phase_w(K, c, l, d)
                if stop == ("w", l):
                    return
                phase_x1(K, c, l, d)
                if stop == ("x1", l):
                    return
                with contextlib.ExitStack() as lst:
                    aoT = sb(nc, lst, "aoT", [128, T // 128, 512], BF16)
                    if cfg.get("skip_a"):
                        K.memset("dve", aoT[:, :, :], 0.0)
                    else:
                        phase_a(K, c, l, d, aoT)
                        if cfg.get("a_twice"):
                            phase_a(K, c, l, d, aoT)
                    if "aoT" in dump and l == 0:
                        K.dma("sp", dv(dbg_out("aoT", [128, T // 128, 512], BF16)), aoT[:, :, :])
                    if stop == ("a", l):
                        K.barrier()
                        return
                    goT = sb(nc, lst, "goT", [128, 4, T], BF16)
                    S = sb(nc, lst, "Sst", [128, 4, 128], F32)
                    phase_g0(K, c, l, d)
                    if stop == ("g0", l):
                        return
                    K.memset("dve", S[:, :, :], 0.0)
                    phase_gscan(K, c, l, d, 0, S)
                    if "S1" in dump and l == 0:
                        K.dma("sp", dv(dbg_out("S1", [128, 4, 128])), S[:, :, :])
                    if stop == ("g1", l):
                        K.barrier()
                        return
                    phase_x2(K, c, l, d, S)
                    phase_gscan(K, c, l, d, 1, S, goT)
                    if "goT" in dump and l == 0:
                        K.dma("sp", dv(dbg_out("goT", [128, 4, T], BF16)), goT[:, :, :])
                    if stop == ("g2", l):
                        K.barrier()
                        return
                    phase_m(K, c, l, d, aoT, goT)
                if stop == ("m", l):
                    return
                last = (l == DEPTH - 1)
                phase_ffn(K, c, stack, d["xres"], d["xres"], d["ffn2_w_gate"][l], d["ffn2_w_up"][l], d["ffn2_w_down"][l],
                          d["norm_ffn2"][l], final_gain=d["final_norm"] if last else None, y_out=y_out if last else None)
                if stop == ("f2", l):
                    return

        run()
        K.barrier()
        for name in dump:
            if name in ("aoT", "goT", "S1"):
                continue
            src_ap = d[name]
            K.dma("sp", dv(dbg_out(name, list(src_ap.shape), src_ap.dtype)), dv(src_ap))
        K.barrier()
    print("instructions:", K.ninst, dict(K.ecnt))
    return nc


def shard_inputs(inputs, ncores=8):
    maps = []
    consts = make_consts()
    inv_freq = np.power(np.float32(10000.0), -np.arange(0, 32, 2, dtype=np.float32) / np.float32(32)).astype(np.float32)
    w = {k: np.asarray(inputs[k], dtype=np.float32) for k, _ in WEIGHT_SPECS}
    w["gdn_conv"] = np.ascontiguousarray(np.transpose(w["gdn_conv"], (0, 2, 1)))
    w["gdn_A_log"] = w["gdn_A_log"].reshape(DEPTH, 8)
    w["gdn_dt_bias"] = w["gdn_dt_bias"].reshape(DEPTH, 8)
    wr = dict(w)
    wi = w["w_in"].copy()
    for base in (2048, 2056):
        wi[:, :, base:base + 4] = w["w_in"][:, :, base + 4:base + 8]
        wi[:, :, base + 4:base + 8] = w["w_in"][:, :, base:base + 4]
    wr["w_in"] = wi
    wr["gdn_A_log"] = np.ascontiguousarray(w["gdn_A_log"].reshape(DEPTH, 2, 4)[:, ::-1].reshape(DEPTH, 8))
    wr["gdn_dt_bias"] = np.ascontiguousarray(w["gdn_dt_bias"].reshape(DEPTH, 2, 4)[:, ::-1].reshape(DEPTH, 8))
    wr["gdn_conv"] = np.ascontiguousarray(w["gdn_conv"][:, :, ::-1])
    S_ = inputs["x"].shape[1]
    Tl = S_ // 2
    for core in range(ncores):
        b, p = core // 2, core % 2
        xs = inputs["x"][b, p * Tl:(p + 1) * Tl]
        ps = inputs["positions"][b, p * Tl:(p + 1) * Tl]
        if p == 1:
            xs = xs[::-1]
            ps = ps[::-1]
        m = dict(w if p == 0 else wr)
        m["x"] = np.ascontiguousarray(xs, dtype=np.float32)
        m["pos_pm"] = np.ascontiguousarray(np.asarray(ps, dtype=np.int32).reshape(Tl // 128, 128).T)
        m["inv_freq"] = inv_freq
        m["consts"] = consts
        m["sel"] = np.array([0.0, 1.0] if p == 0 else [1.0, 0.0], np.float32)
        maps.append(m)
    return maps


def kernel(**inputs):
    inputs = {k: np.asarray(v) for k, v in inputs.items()}
    nc = build({})
    maps = shard_inputs(inputs)
    res = run_bass_kernel_spmd(nc, maps, core_ids=list(range(8)))
    out = np.empty((BATCH, SEQ, D_MODEL), np.float32)
    for core in range(8):
        b, p = core // 2, core % 2
        y = res.results[core]["y"]
        if p == 1:
            y = y[::-1]
        out[b, p * T:(p + 1) * T] = y
    return out
```

```python
import numpy as np
import concourse.bass as bass
import concourse.mybir as mybir
from concourse.bass_utils import run_bass_kernel_spmd

F32 = mybir.dt.float32
BF16 = mybir.dt.bfloat16
I32 = mybir.dt.int32
AF = mybir.ActivationFunctionType
ALU = mybir.AluOpType
AX = mybir.AxisListType

D_MODEL = 1024
BATCH = 4
SEQ = 8192
DEPTH = 2
T = SEQ // 2
NBLK = T // 128
D_FF = 2816
NFF = D_FF // 128
D_IN = 4784
EPS = 1e-6
NDS = 48
NSW = 12


class Buf:
    __slots__ = ("name", "w", "r", "x")

    def __init__(self, name):
        self.name = name
        self.w = None
        self.r = {}
        self.x = False


class V:
    __slots__ = ("ap", "bufs")

    def __init__(self, ap, bufs):
        self.ap = ap
        self.bufs = bufs


class TT:
    def __init__(self, handle, name):
        self.h = handle
        self.buf = Buf(name)

    def __getitem__(self, idx):
        return V(self.h[idx], (self.buf,))


def dv(ap):
    return V(ap, ())


class KB:
    def __init__(self, nc):
        self.nc = nc
        self.engs = {"pe": nc.tensor, "dve": nc.vector, "act": nc.scalar, "pool": nc.gpsimd, "sp": nc.sync}
        self.esem = {k: nc.alloc_semaphore("e_" + k) for k in self.engs}
        self.ecnt = {k: 0 for k in self.engs}
        self.dsem = [nc.alloc_semaphore("d%d" % i) for i in range(NDS)]
        self.dcnt = [0] * NDS
        self.dnext = 0
        self.dnext_sw = 0
        self.seen = {k: {} for k in self.engs}
        self.ninst = 0

    def _sem(self, key):
        return self.esem[key[1]] if key[0] == "e" else self.dsem[key[1]]

    def _wait(self, e, key, val):
        if self.seen[e].get(key, 0) >= val:
            return
        self.seen[e][key] = val
        self.engs[e].wait_ge(self._sem(key), val)
        self.ninst += 1

    def _deps(self, e, reads, writes):
        need = {}
        for v in reads:
            for b in v.bufs:
                if b.w is not None:
                    k, val = b.w
                    if need.get(k, 0) < val:
                        need[k] = val
                if b.x:
                    for k, val in b.r.items():
                        if k != ("e", e) and need.get(k, 0) < val:
                            need[k] = val
        for v in writes:
            for b in v.bufs:
                if b.w is not None:
                    k, val = b.w
                    if need.get(k, 0) < val:
                        need[k] = val
                for k, val in b.r.items():
                    if need.get(k, 0) < val:
                        need[k] = val
        for k, val in need.items():
            if k == ("e", "pe") and e == "pe":
                continue
            self._wait(e, k, val)

    def _mark(self, tok, reads, writes):
        k, val = tok
        for v in reads:
            for b in v.bufs:
                if b.r.get(k, 0) < val:
                    b.r[k] = val
        for v in writes:
            for b in v.bufs:
                b.w = tok
                b.r = {}

    def op(self, e, fn, reads=(), writes=()):
        self._deps(e, reads, writes)
        ins = fn(self.engs[e])
        self.ecnt[e] += 1
        ins.then_inc(self.esem[e], 1)
        self.ninst += 1
        self._mark((("e", e), self.ecnt[e]), reads, writes)
        return ins

    def dma(self, q, out, in_, **kw):
        if q == "pool":
            i = self.dnext_sw
            self.dnext_sw = (i + 1) % NSW
        else:
            i = NSW + self.dnext
            self.dnext = (self.dnext + 1) % (NDS - NSW)
        if self.dcnt[i] > 0:
            self._wait(q, ("d", i), self.dcnt[i])
        self._deps(q, (in_,), (out,))
        ins = self.engs[q].dma_start(out=out.ap, in_=in_.ap, **kw)
        self.dcnt[i] += 16
        ins.then_inc(self.dsem[i], 16)
        self.ninst += 1
        self._mark((("d", i), self.dcnt[i]), (in_,), (out,))
        return ins

    def barrier(self, engines=None):
        for e in (engines or self.engs):
            for k2 in self.engs:
                if k2 != e and self.ecnt[k2] > 0:
                    self._wait(e, ("e", k2), self.ecnt[k2])
            for i in range(NDS):
                if self.dcnt[i] > 0:
                    self._wait(e, ("d", i), self.dcnt[i])

    def mm(self, out, lhsT, rhs, start, stop):
        return self.op("pe", lambda t: t.matmul(out.ap, lhsT=lhsT.ap, rhs=rhs.ap, start=start, stop=stop),
                       reads=(lhsT, rhs), writes=(out,))

    def tr(self, out, in_, ident):
        return self.op("pe", lambda t: t.transpose(out.ap, in_.ap, ident.ap), reads=(in_, ident), writes=(out,))

    def act(self, out, in_, func, bias=None, scale=None, accum=None, e="act"):
        kw = {}
        reads = [in_]
        writes = [out]
        if bias is not None:
            if isinstance(bias, V):
                kw["bias"] = bias.ap
                reads.append(bias)
            else:
                kw["bias"] = bias
        if scale is not None:
            if isinstance(scale, V):
                kw["scale"] = scale.ap
                reads.append(scale)
            else:
                kw["scale"] = scale
        if accum is not None:
            kw["accum_out"] = accum.ap
            writes.append(accum)
        return self.op(e, lambda a: a.activation(out=out.ap, in_=in_.ap, func=func, **kw), reads=reads, writes=writes)

    def tt(self, e, out, in0, in1, op):
        return self.op(e, lambda g: g.tensor_tensor(out=out.ap, in0=in0.ap, in1=in1.ap, op=op),
                       reads=(in0, in1), writes=(out,))

    def ts(self, e, out, in0, s1, op0, s2=None, op1=None):
        reads = [in0]
        a1 = s1
        a2 = s2
        if isinstance(s1, V):
            reads.append(s1)
            a1 = s1.ap
        if isinstance(s2, V):
            reads.append(s2)
            a2 = s2.ap
        kw = {}
        if op1 is not None:
            kw["op1"] = op1
        return self.op(e, lambda g: g.tensor_scalar(out=out.ap, in0=in0.ap, scalar1=a1, scalar2=a2, op0=op0, **kw),
                       reads=reads, writes=(out,))

    def stt(self, e, out, in0, scalar, in1, op0, op1):
        reads = [in0, in1]
        a = scalar
        if isinstance(scalar, V):
            reads.append(scalar)
            a = scalar.ap
        return self.op(e, lambda g: g.scalar_tensor_tensor(out=out.ap, in0=in0.ap, scalar=a, in1=in1.ap, op0=op0, op1=op1),
                       reads=reads, writes=(out,))

    def copy(self, e, out, in_):
        if e == "act":
            return self.op(e, lambda g: g.copy(out=out.ap, in_=in_.ap), reads=(in_,), writes=(out,))
        return self.op(e, lambda g: g.tensor_copy(out=out.ap, in_=in_.ap), reads=(in_,), writes=(out,))

    def memset(self, e, out, val):
        return self.op(e, lambda g: g.memset(out.ap, val), reads=(), writes=(out,))

    def recip(self, out, in_):
        return self.op("dve", lambda g: g.reciprocal(out=out.ap, in_=in_.ap), reads=(in_,), writes=(out,))


class Ctx:
    pass


_uid = [0]


def sb(nc, stack, name, shape, dtype):
    _uid[0] += 1
    name = "%s_%d" % (name, _uid[0])
    h = stack.enter_context(nc.sbuf_tensor(name, shape, dtype))
    return TT(h, name)


def load_cast(K, c, stg, dst_views, src_aps, cast_engs=("dve", "act")):
    qs = ("sp", "act", "pool")
    for i, (d, s) in enumerate(zip(dst_views, src_aps)):
        st = stg[i % len(stg)]
        shp = list(s.shape)
        sv = st[:shp[0], :shp[1]] if len(shp) == 2 else st[:shp[0], :shp[1], :shp[2]]
        K.dma(qs[i % 3], sv, dv(s))
        K.copy(cast_engs[i % len(cast_engs)], d, sv)


def rms_rstd(K, c, x_v, ss, rstd, junk, n, e_sq="act"):
    K.act(junk, x_v, AF.Square, accum=ss[:, 0:1])
    K.act(rstd[:, 0:1], ss[:, 0:1], AF.Sqrt, bias=c.eps_t[:, 0:1], scale=1.0 / n)
    K.recip(rstd[:, 0:1], rstd[:, 0:1])


def phase_ffn(K, c, stack_outer, x_src, x_dst, w_gate, w_up, w_down, gain, final_gain=None, y_out=None):
    import contextlib
    nc = c.nc
    TTK = 256
    NB = TTK // 128
    with contextlib.ExitStack() as stack:
        wg = sb(nc, stack, "wg", [128, 8, D_FF], BF16)
        wu = sb(nc, stack, "wu", [128, 8, D_FF], BF16)
        wd = sb(nc, stack, "wd", [128, NFF, D_MODEL], BF16)
        grow = sb(nc, stack, "grow", [128, D_MODEL], F32)
        K.dma("sp", grow[:, :], dv(gain.partition_broadcast(128)))
        if final_gain is not None:
            fgrow = sb(nc, stack, "fgrow", [128, D_MODEL], F32)
            K.dma("sp", fgrow[:, :], dv(final_gain.partition_broadcast(128)))
        with contextlib.ExitStack() as st2:
            stg = [sb(nc, st2, "stg%d" % i, [128, D_FF], F32) for i in range(4)]
            dsts, srcs = [], []
            for kc in range(8):
                dsts.append(wg[:, kc, :]); srcs.append(w_gate[kc * 128:(kc + 1) * 128, :])
                dsts.append(wu[:, kc, :]); srcs.append(w_up[kc * 128:(kc + 1) * 128, :])
            for j in range(NFF):
                dsts.append(wd[:, j, :]); srcs.append(w_down[j * 128:(j + 1) * 128, :])
            load_cast(K, c, stg, dsts, srcs)
            K.barrier()
        xt = [sb(nc, stack, "xt%d" % i, [128, NB, D_MODEL], F32) for i in range(2)]
        xn = [sb(nc, stack, "xn%d" % i, [128, D_MODEL], BF16) for i in range(2)]
        junk = sb(nc, stack, "junk", [128, D_MODEL], F32)
        hTs = [sb(nc, stack, "hT%d" % i, [128, 8, TTK], BF16) for i in range(2)]
        actT = sb(nc, stack, "actT", [128, NFF, TTK], BF16)
        sg = [sb(nc, stack, "sg%d" % i, [128, TTK], F32) for i in range(2)]
        ss = [sb(nc, stack, "ss%d" % i, [128, 1], F32) for i in range(2)]
        rstd = [sb(nc, stack, "rstd%d" % i, [128, 1], F32) for i in range(2)]
        ntile = T // TTK
        xs_t = x_src.rearrange("(n b p) d -> n p b d", p=128, b=NB)
        xd_t = x_dst.rearrange("(n b p) d -> n p b d", p=128, b=NB)
        if y_out is not None:
            yo_t = y_out.rearrange("(n b p) d -> n p b d", p=128, b=NB)
        K.dma("sp", xt[0][:, :, :], dv(xs_t[0]))

        def norm_T(i):
            x = xt[i % 2]
            hT = hTs[i % 2]
            for b in range(NB):
                rms_rstd(K, c, x[:, b, :], ss[b], rstd[b], junk[:, :], D_MODEL)
                K.stt("dve", xn[b][:, :], x[:, b, :], rstd[b][:, 0:1], grow[:, :], ALU.mult, ALU.mult)
                pt = c.psb[b]
                for kc in range(8):
                    K.tr(pt[:, kc * 128:(kc + 1) * 128], xn[b][:, kc * 128:(kc + 1) * 128], c.ident_bf[:, :])
                K.copy("act" if b == 0 else "dve", hT[:, :, b * 128:(b + 1) * 128],
                       V(pt.h[:, :].rearrange("p (k t) -> p k t", k=8), (pt.buf,)))

        norm_T(0)
        for i in range(ntile):
            x = xt[i % 2]
            hT = hTs[i % 2]
            if i + 1 < ntile:
                K.dma("sp", xt[(i + 1) % 2][:, :, :], dv(xs_t[i + 1]))
            for j in range(NFF):
                pg = c.ps[2 + (j % 3)]
                for kc in range(8):
                    K.mm(pg[:, 0:TTK], wg[:, kc, j * 128:(j + 1) * 128], hT[:, kc, :], kc == 0, kc == 7)
                for kc in range(8):
                    K.mm(pg[:, TTK:2 * TTK], wu[:, kc, j * 128:(j + 1) * 128], hT[:, kc, :], kc == 0, kc == 7)
                s = sg[j % 2]
                K.act(s[:, :], pg[:, 0:TTK], AF.Silu)
                K.tt("dve", actT[:, j, :], s[:, :], pg[:, TTK:2 * TTK], ALU.mult)
            if i + 1 < ntile:
                norm_T(i + 1)
            for b in range(NB):
                for h in range(2):
                    po = c.ps[5 + ((b * 2 + h) % 3)]
                    for j in range(NFF):
                        K.mm(po[:, :], actT[:, j, b * 128:(b + 1) * 128], wd[:, j, h * 512:(h + 1) * 512], j == 0, j == NFF - 1)
                    K.stt("dve", x[:, b, h * 512:(h + 1) * 512], po[:, :], 0.5, x[:, b, h * 512:(h + 1) * 512], ALU.mult, ALU.add)
            if final_gain is None:
                K.dma("pool", dv(xd_t[i]), x[:, :, :])
            else:
                for b in range(NB):
                    rms_rstd(K, c, x[:, b, :], ss[b], rstd[b], junk[:, :], D_MODEL)
                    K.stt("dve", x[:, b, :], x[:, b, :], rstd[b][:, 0:1], fgrow[:, :], ALU.mult, ALU.mult)
                    K.dma("pool", dv(yo_t[i][:, b, :]), x[:, b, :])
        K.barrier()


MLA_SCALE = 96.0 ** -0.5
TWO_PI = 6.283185307179586
CW1 = 6.28125
CW2 = TWO_PI - CW1
MAGIC = 12582912.0
PI_LIM = 3.1415925


def prologue_rope(K, c, stack, pos_pm, invf):
    import contextlib
    nc = c.nc
    nb = T // 128
    c.cos = sb(nc, stack, "cos", [128, nb, 16], F32)
    c.sin = sb(nc, stack, "sin", [128, nb, 16], F32)
    c.cos_s = sb(nc, stack, "cos_s", [128, nb, 16], F32)
    c.sin_s = sb(nc, stack, "sin_s", [128, nb, 16], F32)
    with contextlib.ExitStack() as st:
        pos_i = sb(nc, st, "pos_i", [128, nb], I32)
        posf = sb(nc, st, "posf", [128, nb], F32)
        ivf = sb(nc, st, "ivf", [128, 16], F32)
        ang = sb(nc, st, "ang", [128, nb, 16], F32)
        u = sb(nc, st, "u", [128, nb, 16], F32)
        n = sb(nc, st, "n", [128, nb, 16], F32)
        r = sb(nc, st, "r", [128, nb, 16], F32)
        K.dma("sp", pos_i[:, :], dv(pos_pm))
        K.dma("sp", ivf[:, :], dv(invf.partition_broadcast(128)))
        K.copy("dve", posf[:, :], pos_i[:, :])
        pf_b = V(posf.h[:, :].unsqueeze(2).to_broadcast([128, nb, 16]), (posf.buf,))
        iv_b = V(ivf.h[:, :].unsqueeze(1).to_broadcast([128, nb, 16]), (ivf.buf,))
        K.tt("dve", ang[:, :, :], pf_b, iv_b, ALU.mult)
        for which, dst, dst_s in (("sin", c.sin, c.sin_s), ("cos", c.cos, c.cos_s)):
            off = 0.0 if which == "sin" else 0.25
            K.ts("dve", u[:, :, :], ang[:, :, :], 1.0 / TWO_PI, ALU.mult, off, ALU.add)
            K.ts("dve", n[:, :, :], u[:, :, :], MAGIC, ALU.add)
            K.ts("dve", n[:, :, :], n[:, :, :], MAGIC, ALU.subtract)
            K.stt("dve", r[:, :, :], n[:, :, :], -CW1, ang[:, :, :], ALU.mult, ALU.add)
            K.stt("dve", r[:, :, :], n[:, :, :], -CW2, r[:, :, :], ALU.mult, ALU.add)
            if which == "cos":
                K.ts("dve", r[:, :, :], r[:, :, :], TWO_PI / 4, ALU.add)
            K.ts("dve", r[:, :, :], r[:, :, :], PI_LIM, ALU.min, -PI_LIM, ALU.max)
            K.act(dst[:, :, :], r[:, :, :], AF.Sin)
            K.ts("dve", dst_s[:, :, :], dst[:, :, :], MLA_SCALE, ALU.mult)
        K.barrier()


def rope_tm(K, c, e, out, xin, cs, sn, nh, tmp):
    def bc(v):
        return V(v.ap.unsqueeze(1).to_broadcast([128, nh, 16]), v.bufs)
    x1 = V(xin.ap[:, :, 0:16], xin.bufs)
    x2 = V(xin.ap[:, :, 16:32], xin.bufs)
    o1 = V(out.ap[:, :, 0:16], out.bufs)
    o2 = V(out.ap[:, :, 16:32], out.bufs)
    t = [V(tmp.h[:, i, 0:nh, :], (tmp.buf,)) for i in range(4)]
    K.tt(e, t[0], x1, bc(cs), ALU.mult)
    K.tt(e, t[1], x2, bc(sn), ALU.mult)
    K.tt(e, t[2], x2, bc(cs), ALU.mult)
    K.tt(e, t[3], x1, bc(sn), ALU.mult)
    K.tt(e, o1, t[0], t[1], ALU.subtract)
    K.tt(e, o2, t[2], t[3], ALU.add)


def phase_w(K, c, l, d):
    import contextlib
    nc = c.nc
    TW = 256
    NB = TW // 128
    with contextlib.ExitStack() as stack:
        win = sb(nc, stack, "win", [128, 8, D_IN], BF16)
        wuq = sb(nc, stack, "wuq", [128, 3, 768], BF16)
        grow = sb(nc, stack, "grow_w", [128, D_MODEL], F32)
        qnrow = sb(nc, stack, "qnrow", [128, 384], F32)
        kvnrow = sb(nc, stack, "kvnrow", [128, 256], F32)
        dtbrow = sb(nc, stack, "dtbrow", [128, 8], F32)
        negA = sb(nc, stack, "negA", [128, 8], F32)
        K.dma("sp", grow[:, :], dv(d["norm_mix"][l].partition_broadcast(128)))
        K.dma("sp", qnrow[:, :], dv(d["mla_q_norm"][l].partition_broadcast(128)))
        K.dma("sp", kvnrow[:, :], dv(d["mla_kv_norm"][l].partition_broadcast(128)))
        K.dma("sp", dtbrow[:, :], dv(d["gdn_dt_bias"][l].partition_broadcast(128)))
        K.dma("sp", negA[:, :], dv(d["gdn_A_log"][l].partition_broadcast(128)))
        K.act(negA[:, :], negA[:, :], AF.Exp)
        K.ts("dve", negA[:, :], negA[:, :], -1.0, ALU.mult)
        with contextlib.ExitStack() as st2:
            stg = [sb(nc, st2, "stgw%d" % i, [128, D_IN], F32) for i in range(2)]
            dsts = [win[:, kc, :] for kc in range(8)] + [wuq[:, kc, :] for kc in range(3)]
            srcs = [d["w_in"][l][kc * 128:(kc + 1) * 128, :] for kc in range(8)] + \
                   [d["mla_w_uq"][l][kc * 128:(kc + 1) * 128, :] for kc in range(3)]
            load_cast(K, c, stg, dsts, srcs)
            K.barrier()
        xt = [sb(nc, stack, "xw%d" % i, [128, NB, D_MODEL], F32) for i in range(2)]
        xn = [sb(nc, stack, "xnw%d" % i, [128, D_MODEL], BF16) for i in range(2)]
        junk = sb(nc, stack, "junkw", [128, D_MODEL], F32)
        hT = sb(nc, stack, "hTw", [128, 8, TW], BF16)
        ss = [sb(nc, stack, "ssw%d" % i, [128, 1], F32) for i in range(2)]
        rstd = [sb(nc, stack, "rstdw%d" % i, [128, 1], F32) for i in range(2)]
        qkvst = sb(nc, stack, "qkvst", [128, 12, TW], F32)
        zs_t = [sb(nc, stack, "zs_t%d" % i, [128, 512], F32) for i in range(2)]
        sg_t = [sb(nc, stack, "sg_t%d" % i, [128, 2048], F32) for i in range(2)]
        bg_t = [sb(nc, stack, "bg_t%d" % i, [128, 16], F32) for i in range(2)]
        sp_t = sb(nc, stack, "sp_t", [128, 4, 8], F32)
        cqn = sb(nc, stack, "cqn", [128, 384], BF16)
        cqnT = sb(nc, stack, "cqnT", [128, 3, 128], BF16)
        qs = sb(nc, stack, "qs", [128, 8, 128], BF16)
        K.memset("pool", qs[:, :, :], 0.0)
        qT_t = [sb(nc, stack, "qT_t%d" % i, [128, 8, 128], BF16) for i in range(2)]
        lat_t = [sb(nc, stack, "lat_t%d" % i, [128, 288], F32) for i in range(2)]
        rtmp = sb(nc, stack, "rtmp", [128, 4, 8, 16], F32)
        ss2 = sb(nc, stack, "ss2", [128, 2], F32)
        rs2 = sb(nc, stack, "rs2", [128, 2], F32)
        ntile = T // TW
        xs_t = d["xres"].rearrange("(n b p) d -> n p b d", p=128, b=NB)
        qkvT_v = d["qkvT"].rearrange("(c p) t -> p c t", p=128)
        QT_v = d["QT"].rearrange("h d t -> d h t")
        K.dma("sp", xt[0][:, :, :], dv(xs_t[0]))
        for i in range(ntile):
            x = xt[i % 2]
            if i + 1 < ntile:
                K.dma("sp", xt[(i + 1) % 2][:, :, :], dv(xs_t[i + 1]))
            for b in range(NB):
                rms_rstd(K, c, x[:, b, :], ss[b % 2], rstd[b % 2], junk[:, :], D_MODEL)
                K.stt("dve", xn[b % 2][:, :], x[:, b, :], rstd[b % 2][:, 0:1], grow[:, :], ALU.mult, ALU.mult)
                pt = c.psb[b % 2]
                for kc in range(8):
                    K.tr(pt[:, kc * 128:(kc + 1) * 128], xn[b % 2][:, kc * 128:(kc + 1) * 128], c.ident_bf[:, :])
                K.copy("act" if b % 2 == 0 else "dve", hT[:, :, b * 128:(b + 1) * 128],
                       V(pt.h[:, :].rearrange("p (k t) -> p k t", k=8), (pt.buf,)))
            for j in range(12):
                pf = c.ps[2 + (j % 2)]
                for kc in range(8):
                    K.mm(pf[:, 0:TW], win[:, kc, j * 128:(j + 1) * 128], hT[:, kc, :], kc == 0, kc == 7)
                K.copy("act" if j % 2 == 0 else "dve", qkvst[:, j, :], pf[:, 0:TW])
            K.dma("pool", dv(qkvT_v[:, :, 2 + i * TW:2 + (i + 1) * TW]), qkvst[:, :, :])
            for b in range(NB):
                blk = i * NB + b
                hb = lambda kc: hT[:, kc, b * 128:(b + 1) * 128]
                pz = c.ps[4]
                for kc in range(8):
                    K.mm(pz[:, :], hb(kc), win[:, kc, 1536:2048], kc == 0, kc == 7)
                z = zs_t[blk % 2]
                K.act(z[:, :], pz[:, :], AF.Silu)
                K.dma("pool", dv(d["zs"][blk * 128:(blk + 1) * 128, :]), z[:, :])
                sgt = sg_t[blk % 2]
                for g4 in range(4):
                    pgt = c.ps[5 + (g4 % 2)]
                    for kc in range(8):
                        K.mm(pgt[:, :], hb(kc), win[:, kc, 2736 + g4 * 512:2736 + (g4 + 1) * 512], kc == 0, kc == 7)
                    K.act(sgt[:, g4 * 512:(g4 + 1) * 512], pgt[:, :], AF.Sigmoid)
                K.dma("pool", dv(d["sgd"][blk * 128:(blk + 1) * 128, :]), sgt[:, :])
                p2 = c.ps[7]
                for kc in range(8):
                    K.mm(p2[:, 0:400], hb(kc), win[:, kc, 2048:2448], kc == 0, kc == 7)
                p3 = c.ps[4]
                bgt = bg_t[blk % 2]
                K.act(bgt[:, 0:8], p2[:, 0:8], AF.Sigmoid)
                tt_ = sp_t[:, 0, :]
                K.tt("dve", tt_, p2[:, 8:16], dtbrow[:, :], ALU.add)
                ab = sp_t[:, 1, :]
                K.act(ab, tt_, AF.Abs)
                ee = sp_t[:, 2, :]
                K.act(ee, ab, AF.Exp, scale=-1.0)
                K.act(ee, ee, AF.Ln, bias=c.one_t[:, 0:1])
                sp_ = sp_t[:, 3, :]
                K.stt("dve", sp_, tt_, 0.0, ee, ALU.max, ALU.add)
                K.tt("dve", bgt[:, 8:16], sp_, negA[:, :], ALU.mult)
                K.dma("pool", dv(d["bg"][blk * 128:(blk + 1) * 128, :]), bgt[:, :])
                K.act(junk[:, 0:384], p2[:, 16:400], AF.Square, accum=ss2[:, 0:1])
                K.act(rs2[:, 0:1], ss2[:, 0:1], AF.Sqrt, bias=c.eps_t[:, 0:1], scale=1.0 / 384)
                K.recip(rs2[:, 0:1], rs2[:, 0:1])
                K.stt("dve", cqn[:, :], p2[:, 16:400], rs2[:, 0:1], qnrow[:, :], ALU.mult, ALU.mult)
                ptq = c.psb[0]
                for kc in range(3):
                    K.tr(ptq[:, kc * 128:(kc + 1) * 128], cqn[:, kc * 128:(kc + 1) * 128], c.ident_bf[:, :])
                K.copy("act", cqnT[:, :, :], V(ptq.h[:, 0:384].rearrange("p (k t) -> p k t", k=3), (ptq.buf,)))
                for kc in range(8):
                    K.mm(p3[:, 0:288], hb(kc), win[:, kc, 2448:2736], kc == 0, kc == 7)
                for hh in range(2):
                    pq = c.ps[5 + hh]
                    for kc in range(3):
                        K.mm(pq[:, 0:384], cqnT[:, kc, :], wuq[:, kc, hh * 384:(hh + 1) * 384], kc == 0, kc == 2)
                    pqv = V(pq.h[:, 0:384].rearrange("p (h e) -> p h e", h=4), (pq.buf,))
                    K.ts("dve", qs[:, hh * 4:(hh + 1) * 4, 0:64], V(pqv.ap[:, :, 0:64], pqv.bufs), MLA_SCALE, ALU.mult)
                    rope_tm(K, c, "dve", qs[:, hh * 4:(hh + 1) * 4, 64:96], V(pqv.ap[:, :, 64:96], pqv.bufs),
                            c.cos_s[:, blk, :], c.sin_s[:, blk, :], 4, rtmp)
                pqt = c.psb[1]
                for h in range(8):
                    K.tr(pqt[:, h * 128:(h + 1) * 128], qs[:, h, :], c.ident_bf[:, :])
                qTt = qT_t[blk % 2]
                K.copy("act", qTt[:, :, :], V(pqt.h[:, :].rearrange("p (h t) -> p h t", h=8), (pqt.buf,)))
                K.dma("pool", dv(QT_v[:, :, blk * 128:(blk + 1) * 128]), qTt[:, :, :])
                lt = lat_t[blk % 2]
                K.act(junk[:, 0:256], p3[:, 0:256], AF.Square, accum=ss2[:, 1:2])
                K.act(rs2[:, 1:2], ss2[:, 1:2], AF.Sqrt, bias=c.eps_t[:, 0:1], scale=1.0 / 256)
                K.recip(rs2[:, 1:2], rs2[:, 1:2])
                K.stt("dve", lt[:, 0:256], p3[:, 0:256], rs2[:, 1:2], kvnrow[:, :], ALU.mult, ALU.mult)
                rope_tm(K, c, "dve", V(lt.h[:, 256:288].rearrange("p (h e) -> p h e", h=1), (lt.buf,)),
                        V(p3.h[:, 256:288].rearrange("p (h e) -> p h e", h=1), (p3.buf,)),
                        c.cos[:, blk, :], c.sin[:, blk, :], 1, rtmp)
                CR = min(1024, T)
                K.dma("pool", dv(d["lat_src%d" % (blk * 128 // CR)][(blk * 128) % CR:(blk * 128) % CR + 128, :]), lt[:, :])
        K.barrier()
        hl = sb(nc, stack, "hl", [128, 12, 2], F32)
        K.dma("sp", hl[:, :, :], dv(qkvT_v[:, :, T:T + 2]))
        K.dma("sp", dv(d["halo_src"].rearrange("(c p) t -> p c t", p=128)), hl[:, :, :])
        K.barrier()


def collective_gather(K, c, src_h, dst_h):
    nc = c.nc
    K.barrier()
    groups = [[2 * i, 2 * i + 1] for i in range(c.ncores // 2)]
    ins = nc.gpsimd.collective_compute("AllGather", ALU.bypass, replica_groups=groups,
                                       ins=[src_h.ap().opt()], outs=[dst_h.ap().opt()])
    c.cc_cnt += 1
    ins.then_inc(c.cc_sem)
    K.ninst += 1
    for e in K.engs:
        K.engs[e].wait_ge(c.cc_sem, c.cc_cnt)


def sel_other(K, c, e, out, slot0, slot1):
    K.ts(e, out, slot0, c.sel[:, 0:1], ALU.mult)
    K.stt(e, out, slot1, c.sel[:, 1:2], out, ALU.mult, ALU.add)


def phase_x1(K, c, l, d):
    import contextlib
    nc = c.nc
    for ch in range(T // min(1024, T)):
        collective_gather(K, c, d["lat_src%d_h" % ch], d["lat_all%d_h" % ch])
    collective_gather(K, c, d["halo_src_h"], d["halo_all_h"])
    with contextlib.ExitStack() as stack:
        ha = sb(nc, stack, "ha", [128, 2, 12, 2], F32)
        ho = sb(nc, stack, "ho", [128, 12, 2], F32)
        hz = sb(nc, stack, "hz", [128, 12, 2], F32)
        K.dma("sp", ha[:, :, :, :], dv(d["halo_all"].rearrange("(s c p) t -> p s c t", p=128, s=2)))
        sel_other(K, c, "dve", ho[:, :, :], ha[:, 0, :, :], ha[:, 1, :, :])
        qkvT_v = d["qkvT"].rearrange("(c p) t -> p c t", p=128)
        K.dma("sp", dv(qkvT_v[:, :, T + 2:T + 3]), ho[:, :, 1:2], allow_slow_non_contiguous=True)
        K.dma("sp", dv(qkvT_v[:, :, T + 3:T + 4]), ho[:, :, 0:1], allow_slow_non_contiguous=True)
        K.memset("dve", hz[:, :, :], 0.0)
        K.dma("sp", dv(qkvT_v[:, :, 0:2]), hz[:, :, :])
        K.barrier()


def phase_a(K, c, l, d, aoT):
    import contextlib
    nc = c.nc
    NK = 2 * T
    NKB = NK // 128
    NKT = NK // 512
    QW = 512 if T >= 512 else T
    NQT = T // QW
    with contextlib.ExitStack() as stack:
        wukv = sb(nc, stack, "wukv", [128, 2, 1024], BF16)
        ckvnT = sb(nc, stack, "ckvnT", [128, 2, NK], BF16)
        KT = [sb(nc, stack, "KT%d" % i, [128, NK], BF16) for i in range(2)]
        Vh = [sb(nc, stack, "Vh%d" % i, [128, NKB, 128], BF16) for i in range(2)]
        QTh = [sb(nc, stack, "QTh%d" % i, [128, T], BF16) for i in range(2)]
        pT = [sb(nc, stack, "pT%d" % i, [128, QW], BF16) for i in range(3)]
        rs = sb(nc, stack, "rs_a", [128, QW], F32)
        bc = sb(nc, stack, "bc_a", [128, QW], F32)
        latf = [sb(nc, stack, "latf%d" % i, [128, 288], F32) for i in range(3)]
        latb = [sb(nc, stack, "latb%d" % i, [128, 320], BF16) for i in range(2)]
        for i in range(2):
            K.memset("pool", latb[i][:, 288:320], 0.0)
        K.memset("pool", rs[:, :], 0.0)
        with contextlib.ExitStack() as st2:
            stg = [sb(nc, st2, "stga%d" % i, [128, 1024], F32) for i in range(2)]
            load_cast(K, c, stg, [wukv[:, kc, :] for kc in range(2)],
                      [d["mla_w_ukv"][l][kc * 128:(kc + 1) * 128, :] for kc in range(2)])
            K.barrier()
        import os
        amode = int(os.environ.get("AMODE", "0"))
        if amode == 4:
            K.barrier()
            return
        for p in range(2):
            K.memset("pool", Vh[p][:, :, :], 0.0)
        K.memset("pool", Vh[0][:, :, 64:65], 1.0)
        K.memset("pool", Vh[1][:, :, 64:65], 1.0)
        rc = sb(nc, stack, "rc_a", [128, 4], F32)
        if amode == 5:
            K.barrier()
            return
        for kb in range(NKB):
            lf = latf[kb % 3]
            lb = latb[kb % 2]
            CR2 = 2 * min(1024, T)
            K.dma("sp" if kb % 2 == 0 else "act", lf[:, :],
                  dv(d["lat_all%d" % (kb * 128 // CR2)][(kb * 128) % CR2:(kb * 128) % CR2 + 128, :]))
            K.copy("pool", lb[:, 0:288], lf[:, :])
            pt = c.psb[5 + (kb % 2)]
            for kc in range(2):
                K.tr(pt[:, kc * 128:(kc + 1) * 128], lb[:, kc * 128:(kc + 1) * 128], c.ident_bf[:, :])
            pk_ = c.psb[3 + (kb % 2)]
            K.tr(pk_[:, 0:128], lb[:, 192:320], c.ident_bf[:, :])
            K.copy("dve", ckvnT[:, :, kb * 128:(kb + 1) * 128],
                   V(pt.h[:, 0:256].rearrange("p (k t) -> p k t", k=2), (pt.buf,)))
            K.copy("act", KT[0][64:128, kb * 128:(kb + 1) * 128], pk_[64:128, 0:128])
            K.copy("dve", KT[1][64:128, kb * 128:(kb + 1) * 128], pk_[64:128, 0:128])
        QT_d = d["QT"]
        if amode == 1:
            K.barrier()
            return
        K.dma("sp", QTh[0][:, :], dv(QT_d[0]))
        cnt = 0
        for h in range(8):
            par = h % 2
            kt_ = KT[par]
            vh = Vh[par]
            if h + 1 < 8:
                K.dma("sp", QTh[(h + 1) % 2][:, :], dv(QT_d[h + 1]))
            qh = QTh[h % 2]
            for kt in range(NKT):
                pk = c.ps[5 + (kt % 2)]
                for kc in range(2):
                    K.mm(pk[:, :], wukv[:, kc, h * 128:h * 128 + 128], ckvnT[:, kc, kt * 512:(kt + 1) * 512], kc == 0, kc == 1)
                K.copy("dve", kt_[0:64, kt * 512:(kt + 1) * 512], pk[0:64, :])
            voff = 0
            for k8 in range(NKB // 8):
                pv = c.ps[5 + (k8 % 2)]
                for j in range(8):
                    kb = k8 * 8 + j
                    for kc in range(2):
                        K.mm(pv[:, j * 64:(j + 1) * 64], ckvnT[:, kc, kb * 128:(kb + 1) * 128],
                             wukv[:, kc, h * 128 + 64:h * 128 + 128], kc == 0, kc == 1)
                K.copy("dve", vh[:, k8 * 8:(k8 + 1) * 8, voff:voff + 64],
                       V(pv.h[:, :].rearrange("p (j e) -> p j e", j=8), (pv.buf,)))
            if amode == 2:
                continue
            for qt in range(NQT):
                po = c.ps[3 + (cnt % 2)]
                cnt += 1
                qv = qh[:, qt * QW:(qt + 1) * QW]
                nsub = QW // 128
                for kk in range(min(2, NKB)):
                    K.mm(c.ps[kk % 3][:, 0:QW], kt_[:, kk * 128:(kk + 1) * 128], qv, True, True)
                for kb in range(NKB):
                    if kb + 2 < NKB:
                        K.mm(c.ps[(kb + 2) % 3][:, 0:QW], kt_[:, (kb + 2) * 128:(kb + 3) * 128], qv, True, True)
                    K.act(pT[kb % 3][:, :], c.ps[kb % 3][:, 0:QW], AF.Exp)
                    for j in range(nsub):
                        K.mm(po[:, j * 128:j * 128 + 65], pT[kb % 3][:, j * 128:(j + 1) * 128], vh[:, kb, 0:65],
                             kb == 0 and j == 0, kb == NKB - 1 and j == nsub - 1)
                pov = V(po.h[:, 0:nsub * 128].rearrange("p (j e) -> p j e", j=nsub), (po.buf,))
                rcv = V(rc.h[:, 0:nsub].unsqueeze(2), (rc.buf,))
                K.recip(rcv, V(pov.ap[:, :, 64:65], pov.bufs))
                K.tt("dve", aoT[:, qt * nsub:(qt + 1) * nsub, h * 64:(h + 1) * 64], V(pov.ap[:, :, 0:64], pov.bufs),
                     V(rc.h[:, 0:nsub].unsqueeze(2).to_broadcast([128, nsub, 64]), (rc.buf,)), ALU.mult)
        K.barrier()


def phase_m(K, c, l, d, aoT, goT):
    import contextlib
    nc = c.nc
    with contextlib.ExitStack() as stack:
        wpa = sb(nc, stack, "wpa", [128, 4, D_MODEL], BF16)
        wpb = sb(nc, stack, "wpb", [128, 4, D_MODEL], BF16)
        wo = sb(nc, stack, "wo", [128, 8, D_MODEL], BF16)
        with contextlib.ExitStack() as st2:
            stg = [sb(nc, st2, "stgm%d" % i, [128, D_MODEL], F32) for i in range(3)]
            dsts = [wpa[:, kc, :] for kc in range(4)] + [wpb[:, kc, :] for kc in range(4)] + [wo[:, kc, :] for kc in range(8)]
            srcs = [d["gdn_proj"][l][kc * 128:(kc + 1) * 128, :] for kc in range(4)] + \
                   [d["mla_proj"][l][kc * 128:(kc + 1) * 128, :] for kc in range(4)] + \
                   [d["w_out"][l][kc * 128:(kc + 1) * 128, :] for kc in range(8)]
            load_cast(K, c, stg, dsts, srcs)
            K.barrier()
        xt = [sb(nc, stack, "xm%d" % i, [128, D_MODEL], F32) for i in range(3)]
        sgt = [sb(nc, stack, "sgm%d" % i, [128, 2048], F32) for i in range(3)]
        yas = [sb(nc, stack, "ya%d" % i, [128, D_MODEL], F32) for i in range(2)]
        ybs = [sb(nc, stack, "yb%d" % i, [128, D_MODEL], BF16) for i in range(2)]
        yTs = [sb(nc, stack, "yT%d" % i, [128, 8, 128], BF16) for i in range(2)]
        aoTbs = [sb(nc, stack, "aoTb%d" % i, [128, 4, 128], BF16) for i in range(2)]
        nb = T // 128
        for pb_ in range(min(2, nb)):
            K.dma("sp", xt[pb_][:, :], dv(d["xres"][pb_ * 128:(pb_ + 1) * 128, :]))
            K.dma("sp", sgt[pb_][:, :], dv(d["sgd"][pb_ * 128:(pb_ + 1) * 128, :]))
        def stage1(blk):
            sg = sgt[blk % 3]
            ya, yb, aoTb = yas[blk % 2], ybs[blk % 2], aoTbs[blk % 2]
            ptA = c.psb[7]
            for kc in range(4):
                K.tr(ptA[:, kc * 128:(kc + 1) * 128], aoT[:, blk, kc * 128:(kc + 1) * 128], c.ident_bf[:, :])
            K.copy("act", aoTb[:, :, :], V(ptA.h[:, 0:512].rearrange("p (k t) -> p k t", k=4), (ptA.buf,)))
            for hf in range(2):
                pa = c.ps[hf]
                pb = c.ps[2 + hf]
                for kc in range(4):
                    K.mm(pa[:, :], goT[:, kc, blk * 128:(blk + 1) * 128], wpa[:, kc, hf * 512:(hf + 1) * 512], kc == 0, kc == 3)
                for kc in range(4):
                    K.mm(pb[:, :], aoTb[:, kc, :], wpb[:, kc, hf * 512:(hf + 1) * 512], kc == 0, kc == 3)
                K.tt("dve", ya[:, hf * 512:(hf + 1) * 512], pa[:, :], sg[:, hf * 512:(hf + 1) * 512], ALU.mult)
                K.tt("dve", sg[:, 1024 + hf * 512:1024 + (hf + 1) * 512], pb[:, :], sg[:, 1024 + hf * 512:1024 + (hf + 1) * 512], ALU.mult)
                K.tt("pool", yb[:, hf * 512:(hf + 1) * 512], ya[:, hf * 512:(hf + 1) * 512],
                     sg[:, 1024 + hf * 512:1024 + (hf + 1) * 512], ALU.add)

        def stage2(blk):
            x = xt[blk % 3]
            yb, yT = ybs[blk % 2], yTs[blk % 2]
            pt = c.psb[4]
            for kc in range(8):
                K.tr(pt[:, kc * 128:(kc + 1) * 128], yb[:, kc * 128:(kc + 1) * 128], c.ident_bf[:, :])
            K.copy("act", yT[:, :, :], V(pt.h[:, :].rearrange("p (k t) -> p k t", k=8), (pt.buf,)))
            for hf in range(2):
                po = c.ps[5 + hf]
                for kc in range(8):
                    K.mm(po[:, :], yT[:, kc, :], wo[:, kc, hf * 512:(hf + 1) * 512], kc == 0, kc == 7)
                K.tt("dve", x[:, hf * 512:(hf + 1) * 512], po[:, :], x[:, hf * 512:(hf + 1) * 512], ALU.add)
            K.dma("pool", dv(d["xres"][blk * 128:(blk + 1) * 128, :]), x[:, :])

        for blk in range(nb):
            stage1(blk)
            if blk >= 1:
                stage2(blk - 1)
            if blk + 2 < nb:
                K.dma("sp", xt[(blk + 2) % 3][:, :], dv(d["xres"][(blk + 2) * 128:(blk + 3) * 128, :]))
                K.dma("sp", sgt[(blk + 2) % 3][:, :], dv(d["sgd"][(blk + 2) * 128:(blk + 3) * 128, :]))
        stage2(nb - 1)
        K.barrier()


def phase_g0(K, c, l, d):
    import contextlib
    nc = c.nc
    TG = 512 if T >= 512 else T
    NB = TG // 128
    with contextlib.ExitStack() as stack:
        cw = sb(nc, stack, "cw", [128, 12, 5], F32)
        K.dma("sp", cw[:, :, :], dv(d["gdn_conv"][l].rearrange("(c p) k -> p c k", p=128)))
        xin = [sb(nc, stack, "xin%d" % i, [128, 12, TG + 4], F32) for i in range(2)]
        y = [sb(nc, stack, "ycv%d" % i, [128, TG], F32) for i in range(3)]
        ysil = sb(nc, stack, "ysil", [128, 12, TG], F32)
        ytmp = sb(nc, stack, "ytmp", [128, TG], F32)
        tm = [sb(nc, stack, "tm%d" % i, [128, 1536], F32) for i in range(2)]
        ss8 = sb(nc, stack, "ss8", [128, 8], F32)
        rs8 = sb(nc, stack, "rs8", [128, 8], F32)
        junk = sb(nc, stack, "junkg", [128, 128], F32)
        qkvT_v = d["qkvT"].rearrange("(c p) t -> p c t", p=128)
        ntile = T // TG
        PE_CH = (0, 2, 4, 6, 8, 10)
        dg = sb(nc, stack, "dgcv", [128, len(PE_CH), 5, 128], F32)
        for ci, cc in enumerate(PE_CH):
            for k in range(5):
                K.ts("dve", dg[:, ci, k, :], c.ident_f[:, :], cw[:, cc, k:k + 1], ALU.mult)
        K.dma("sp", xin[0][:, :, :], dv(qkvT_v[:, :, 0:TG + 4]))
        for i in range(ntile):
            xi = xin[i % 2]
            if i + 1 < ntile:
                K.dma("sp", xin[(i + 1) % 2][:, :, :], dv(qkvT_v[:, :, (i + 1) * TG:(i + 1) * TG + TG + 4]))
            for cc in range(12):
                if cc in PE_CH:
                    ci = PE_CH.index(cc)
                    pc = c.ps[4 + (ci % 4)]
                    for k in range(5):
                        K.mm(pc[:, 0:TG], dg[:, ci, k, :], xi[:, cc, k:k + TG], k == 0, k == 4)
                    K.act(ysil[:, cc, :], pc[:, 0:TG], AF.Silu)
                    continue
                e = "pool" if cc == 11 else "dve"
                yy = y[cc % 2] if e == "dve" else y[2]
                K.ts(e, yy[:, :], xi[:, cc, 0:TG], cw[:, cc, 0:1], ALU.mult)
                for k in range(1, 5):
                    if e == "dve":
                        K.stt(e, yy[:, :], xi[:, cc, k:k + TG], cw[:, cc, k:k + 1], yy[:, :], ALU.mult, ALU.add)
                    else:
                        K.ts(e, ytmp[:, :], xi[:, cc, k:k + TG], cw[:, cc, k:k + 1], ALU.mult)
                        K.tt(e, yy[:, :], yy[:, :], ytmp[:, :], ALU.add)
                K.act(ysil[:, cc, :], yy[:, :], AF.Silu)
            for b in range(NB):
                blk = i * NB + b
                t_ = tm[blk % 2]
                for cc in range(12):
                    K.tr(c.ps[cc // 4][:, (cc % 4) * 128:(cc % 4 + 1) * 128], ysil[:, cc, b * 128:(b + 1) * 128], c.ident_f[:, :])
                for g in range(8):
                    K.act(junk[:, :], c.ps[g // 4][:, (g % 4) * 128:(g % 4 + 1) * 128], AF.Square, accum=ss8[:, g:g + 1])
                K.act(rs8[:, :], ss8[:, :], AF.Sqrt, bias=c.eps_t[:, 0:1], scale=1.0)
                K.recip(rs8[:, :], rs8[:, :])
                K.ts("dve", rs8[:, 0:4], rs8[:, 0:4], 128.0 ** -0.5, ALU.mult)
                for g in range(8):
                    K.ts("dve", t_[:, g * 128:(g + 1) * 128], c.ps[g // 4][:, (g % 4) * 128:(g % 4 + 1) * 128], rs8[:, g:g + 1], ALU.mult)
                K.copy("act", t_[:, 1024:1536], c.ps[2][:, :])
                K.dma("pool", dv(d["qkvn"][blk * 128:(blk + 1) * 128, :]), t_[:, :])
        K.barrier()


def phase_gscan(K, c, l, d, dirn, S, goT=None):
    import contextlib
    nc = c.nc
    nb = T // 128
    tri = c.tri[dirn]
    negm4 = c.negm4[dirn]
    nstr4 = c.nstr4[dirn]
    with contextlib.ExitStack() as stack:
        qkv = [sb(nc, stack, "qkvg%d" % i, [128, 1536], F32) for i in range(2)]
        bgt = [sb(nc, stack, "bgg%d" % i, [128, 16], F32) for i in range(2)]
        kqT = sb(nc, stack, "kqT", [128, 4, 256], F32)
        GT = sb(nc, stack, "GT", [128, 4, 128], F32)
        sm = sb(nc, stack, "smg", [128, 6, 4], F32)
        DT = sb(nc, stack, "DT", [128, 4, 128], F32)
        Xt = sb(nc, stack, "Xt", [128, 4, 128], F32)
        Xs = [sb(nc, stack, "Xs%d" % i, [128, 4, 128], F32) for i in range(2)]
        Ys = [sb(nc, stack, "Ys%d" % i, [128, 4, 128], F32) for i in range(2)]
        Ns = [sb(nc, stack, "Ns%d" % i, [128, 4, 128], F32) for i in range(2)]
        A2 = sb(nc, stack, "A2", [128, 4, 128], F32)
        rhs = sb(nc, stack, "rhsg", [128, 4, 256], F32)
        UW = sb(nc, stack, "UW", [128, 4, 256], F32)
        WT = sb(nc, stack, "WT", [128, 4, 128], F32)
        qg = sb(nc, stack, "qg", [128, 4, 128], F32)
        qgT = sb(nc, stack, "qgT", [128, 4, 128], F32)
        kd = sb(nc, stack, "kd", [128, 4, 128], F32)
        VNs = [sb(nc, stack, "VN%d" % i, [128, 4, 128], F32) for i in range(2)]
        for i in range(2):
            K.memset("pool", VNs[i][:, :, :], 0.0)
        O = [sb(nc, stack, "Og%d" % i, [128, 4, 128], F32) for i in range(2)]
        if dirn == 1:
            o1t = [sb(nc, stack, "o1t%d" % i, [128, 4, 128], F32) for i in range(2)]
            zt = [sb(nc, stack, "ztg%d" % i, [128, 4, 128], F32) for i in range(2)]
            ob = sb(nc, stack, "obg", [128, 4, 128], BF16)
            nrow = sb(nc, stack, "nrowg", [128, 128], F32)
            ss4 = sb(nc, stack, "ss4", [128, 4], F32)
            rs4 = sb(nc, stack, "rs4", [128, 4], F32)
            junk = sb(nc, stack, "junkgs", [128, 128], F32)
            K.dma("sp", nrow[:, :], dv(d["gdn_norm"][l].partition_broadcast(128)))
        ident4 = V(c.ident_f.h[:, :].unsqueeze(1).to_broadcast([128, 4, 128]), (c.ident_f.buf,))
        order = list(range(nb)) if dirn == 0 else list(range(nb - 1, -1, -1))

        def loads(j, blk):
            K.dma("sp", qkv[j % 2][:, :], dv(d["qkvn"][blk * 128:(blk + 1) * 128, :]))
            K.dma("act", bgt[j % 2][:, :], dv(d["bg"][blk * 128:(blk + 1) * 128, :]))
            if dirn == 1:
                K.dma("sp", o1t[j % 2][:, :, :], dv(d["o1"][blk * 128:(blk + 1) * 128, :].rearrange("p (h e) -> p h e", h=4)))
                K.dma("act", zt[j % 2][:, :, :], dv(d["zs"][blk * 128:(blk + 1) * 128, :].rearrange("p (h e) -> p h e", h=4)))

        loads(0, order[0])
        for j, blk in enumerate(order):
            if j + 1 < nb:
                loads(j + 1, order[j + 1])
            qk = qkv[j % 2]
            bg_ = bgt[j % 2]
            beta = lambda h: bg_[:, dirn * 4 + h:dirn * 4 + h + 1]
            g4 = bg_[:, 8 + dirn * 4:8 + dirn * 4 + 4]
            gcol = lambda h: bg_[:, 8 + dirn * 4 + h:8 + dirn * 4 + h + 1]
            qv = lambda h: qk[:, h * 128:(h + 1) * 128]
            kv = lambda h: qk[:, 512 + h * 128:512 + (h + 1) * 128]
            for hh in range(2):
                pt = c.ps[hh]
                for h2 in range(2):
                    h = hh * 2 + h2
                    K.tr(pt[:, h2 * 256:h2 * 256 + 128], kv(h), c.ident_f[:, :])
                    K.tr(pt[:, h2 * 256 + 128:h2 * 256 + 256], qv(h), c.ident_f[:, :])
                K.copy("act" if hh == 0 else "dve", kqT[:, hh * 2:hh * 2 + 2, :],
                       V(pt.h[:, :].rearrange("p (h e) -> p h e", h=2), (pt.buf,)))
            pg = c.ps[2]
            K.mm(pg[:, 0:4], tri[:, :], g4, True, True)
            K.mm(pg[:, 4:8], c.same[:, :], g4, True, True)
            K.mm(pg[:, 8:12], c.ch[0][:, :], g4, True, True)
            K.mm(pg[:, 12:16], c.ch[1][:, :], g4, True, True)
            gc = sm[:, 0, :]
            ngc = sm[:, 1, :]
            egc = sm[:, 2, :]
            ekd = sm[:, 3, :]
            K.copy("dve", gc, pg[:, 0:4])
            K.ts("dve", ngc, pg[:, 0:4], -1.0, ALU.mult)
            K.act(egc, pg[:, 0:4], AF.Exp)
            K.tt("dve", ekd, pg[:, 4:8], gc, ALU.subtract)
            K.act(ekd, ekd, AF.Exp)
            K.act(V(sm.h[:, 4:6, :], (sm.buf,)), V(pg.h[:, 8:16].rearrange("p (a b) -> p a b", a=2), (pg.buf,)), AF.Exp)
            for h in range(4):
                K.ts("dve", GT[:, h, :], tri[:, :], gcol(h), ALU.mult)
            pR = c.ps[3]
            GTf = V(GT.h[:, :, :].rearrange("p h e -> p (h e)"), (GT.buf,))
            K.mm(pR[:, :], c.ones_f[:, :], GTf, True, False)
            K.mm(pR[:, :], c.ident_f[:, :], negm4[:, :], False, True)
            for h in range(4):
                K.act(DT[:, h, :], pR[:, h * 128:(h + 1) * 128], AF.Exp, bias=sm[:, 1, h:h + 1])
            pK = [c.ps[4], c.ps[5]]
            for h in range(4):
                K.mm(pK[h // 2][:, (h % 2) * 256:(h % 2) * 256 + 256], kqT[:, h, 0:128], kqT[:, h, :], True, True)
            for h in range(4):
                K.stt("dve", Xt[:, h, :], pK[h // 2][:, (h % 2) * 256:(h % 2) * 256 + 128], beta(h), DT[:, h, :], ALU.mult, ALU.mult)
                K.tt("dve", A2[:, h, :], pK[h // 2][:, (h % 2) * 256 + 128:(h % 2) * 256 + 256], DT[:, h, :], ALU.mult)
            X, Y, N = Xs[0], Ys[0], Ns[0]
            K.tt("dve", X[:, :, :], Xt[:, :, :], nstr4[:, :, :], ALU.mult)
            pY = c.ps[6]
            for h in range(4):
                K.tr(pY[:, h * 128:(h + 1) * 128], X[:, h, :], c.ident_f[:, :])
            K.copy("act", Y[:, :, :], V(pY.h[:, :].rearrange("p (h e) -> p h e", h=4), (pY.buf,)))
            K.tt("pool", N[:, :, :], X[:, :, :], ident4, ALU.add)
            for lvl in range(5):
                last = lvl == 4
                Xn, Yn, Nn = Xs[(lvl + 1) % 2], Ys[(lvl + 1) % 2], Ns[(lvl + 1) % 2]
                pY2 = c.ps[6 + (lvl % 2)]
                for h in range(4):
                    K.mm(pY2[:, h * 128:(h + 1) * 128], X[:, h, :], Y[:, h, :], True, True)
                if not last:
                    pX2 = c.ps[0 + (lvl % 2)]
                    for h in range(4):
                        K.mm(pX2[:, h * 128:(h + 1) * 128], Y[:, h, :], X[:, h, :], True, True)
                K.copy("act", Yn[:, :, :], V(pY2.h[:, :].rearrange("p (h e) -> p h e", h=4), (pY2.buf,)))
                if not last:
                    K.copy("dve", Xn[:, :, :], V(pX2.h[:, :].rearrange("p (h e) -> p h e", h=4), (pX2.buf,)))
                pN = c.ps[2 + (lvl % 2)]
                for h in range(4):
                    K.mm(pN[:, h * 128:(h + 1) * 128], Yn[:, h, :], N[:, h, :], True, True)
                K.tt("dve", Nn[:, :, :], V(pN.h[:, :].rearrange("p (h e) -> p h e", h=4), (pN.buf,)), N[:, :, :], ALU.add)
                X, Y, N = Xn, Yn, Nn
            K.copy("act", rhs[:, :, 0:128], V(qk.h[:, 1024:1536].rearrange("p (h e) -> p h e", h=4), (qk.buf,)))
            for h in range(4):
                K.ts("dve", rhs[:, h, 128:256], kv(h), sm[:, 2, h:h + 1], ALU.mult)
            pU = [c.ps[4], c.ps[5]]
            for h in range(4):
                K.mm(pU[h // 2][:, (h % 2) * 256:(h % 2) * 256 + 256], N[:, h, :], rhs[:, h, :], True, True)
            for h in range(4):
                K.ts("dve", UW[:, h, :], pU[h // 2][:, (h % 2) * 256:(h % 2) * 256 + 256], beta(h), ALU.mult)
            pW = c.ps[0]
            for h in range(4):
                K.tr(pW[:, h * 128:(h + 1) * 128], UW[:, h, 128:256], c.ident_f[:, :])
            K.copy("act", WT[:, :, :], V(pW.h[:, :].rearrange("p (h e) -> p h e", h=4), (pW.buf,)))
            for h in range(4):
                K.ts("dve", qg[:, h, :], qv(h), sm[:, 2, h:h + 1], ALU.mult)
                K.ts("pool", kd[:, h, :], kv(h), sm[:, 3, h:h + 1], ALU.mult)
            pQ = c.ps[1]
            for h in range(4):
                K.tr(pQ[:, h * 128:(h + 1) * 128], qg[:, h, :], c.ident_f[:, :])
            K.copy("dve", qgT[:, :, :], V(pQ.h[:, :].rearrange("p (h e) -> p h e", h=4), (pQ.buf,)))
            Ot = O[j % 2]
            for cch in ((0, 1) if dirn == 0 else (1, 0)):
                r0, r1 = cch * 64, cch * 64 + 64
                VN = VNs[cch]
                pV = c.ps[2]
                for h in range(4):
                    K.mm(pV[:, h * 128:(h + 1) * 128], WT[:, h, :], S[:, h, :], True, True)
                K.tt("dve", VN[r0:r1, :, :], UW[r0:r1, :, 0:128],
                     V(pV.h[r0:r1, :].rearrange("p (h e) -> p h e", h=4), (pV.buf,)), ALU.subtract)
                pO = c.ps[3]
                for h in range(4):
                    K.mm(pO[:, h * 128:(h + 1) * 128], qgT[:, h, :], S[:, h, :], True, False)
                    K.mm(pO[:, h * 128:(h + 1) * 128], A2[:, h, :], VN[:, h, :], False, True)
                K.copy("act", Ot[r0:r1, :, :], V(pO.h[r0:r1, :].rearrange("p (h e) -> p h e", h=4), (pO.buf,)))
                pS = c.ps[6]
                for h in range(4):
                    K.mm(pS[:, h * 128:(h + 1) * 128], kd[:, h, :], VN[:, h, :], True, True)
                for h in range(4):
                    K.stt("dve", S[:, h, :], S[:, h, :], sm[:, 4 + cch, h:h + 1], pS[:, h * 128:(h + 1) * 128], ALU.mult, ALU.add)
            if dirn == 0:
                K.dma("pool", dv(d["o1"][blk * 128:(blk + 1) * 128, :].rearrange("p (h e) -> p h e", h=4)), Ot[:, :, :])
            else:
                K.tt("pool", Ot[:, :, :], Ot[:, :, :], o1t[j % 2][:, :, :], ALU.add)
                for h in range(4):
                    K.act(junk[:, :], Ot[:, h, :], AF.Square, accum=ss4[:, h:h + 1])
                K.act(rs4[:, :], ss4[:, :], AF.Sqrt, bias=c.eps_t[:, 0:1], scale=1.0 / 128)
                K.recip(rs4[:, :], rs4[:, :])
                for h in range(4):
                    K.stt("dve", Ot[:, h, :], Ot[:, h, :], rs4[:, h:h + 1], nrow[:, :], ALU.mult, ALU.mult)
                K.tt("pool", ob[:, :, :], Ot[:, :, :], zt[j % 2][:, :, :], ALU.mult)
                pG = c.psb[7]
                for h in range(4):
                    K.tr(pG[:, h * 128:(h + 1) * 128], ob[:, h, :], c.ident_bf[:, :])
                K.copy("act", goT[:, :, blk * 128:(blk + 1) * 128], V(pG.h[:, 0:512].rearrange("p (h e) -> p h e", h=4), (pG.buf,)))
        K.barrier()


def phase_x2(K, c, l, d, S):
    import contextlib
    nc = c.nc
    K.dma("sp", dv(d["st_src"].rearrange("(h p) v -> p h v", p=128)), S[:, :, :])
    collective_gather(K, c, d["st_src_h"], d["st_all_h"])
    with contextlib.ExitStack() as stack:
        sa = sb(nc, stack, "sa", [128, 2, 4, 128], F32)
        K.dma("sp", sa[:, :, :, :], dv(d["st_all"].rearrange("(s h p) v -> p s h v", p=128, s=2)))
        sel_other(K, c, "dve", S[:, :, :], sa[:, 0, :, :], sa[:, 1, :, :])
        K.barrier()


CONST_COLS = 128 * 9 + 512 * 4


def make_consts():
    idx = np.arange(128)
    same = (idx[:, None] // 64) == (idx[None, :] // 64)
    cs = np.zeros((128, CONST_COLS), np.float32)
    cs[:, 0:128] = np.eye(128)
    cs[:, 128:256] = same
    cs[:, 256:384] = 1.0
    for dirn in range(2):
        if dirn == 0:
            tri = same & (idx[:, None] <= idx[None, :])
            allow = same & (idx[None, :] >= idx[:, None])
            strict = same & (idx[None, :] > idx[:, None])
        else:
            tri = same & (idx[:, None] >= idx[None, :])
            allow = same & (idx[None, :] <= idx[:, None])
            strict = same & (idx[None, :] < idx[:, None])
        cs[:, 384 + dirn * 128:384 + (dirn + 1) * 128] = tri
        cs[:, 640 + dirn * 512:640 + (dirn + 1) * 512] = np.tile(np.where(allow, 0.0, -30000.0), (1, 4))
        cs[:, 1664 + dirn * 512:1664 + (dirn + 1) * 512] = np.tile(np.where(strict, -1.0, 0.0), (1, 4))
    cs[0:64, 2688:2816] = 1.0
    cs[64:128, 2816:2944] = 1.0
    cs[64, 2944:3072] = 1.0
    cs[0, 3072:3200] = 1.0
    return cs


WEIGHT_SPECS = [
    ("norm_ffn1", [DEPTH, D_MODEL]), ("ffn1_w_gate", [DEPTH, D_MODEL, D_FF]), ("ffn1_w_up", [DEPTH, D_MODEL, D_FF]),
    ("ffn1_w_down", [DEPTH, D_FF, D_MODEL]), ("norm_mix", [DEPTH, D_MODEL]), ("w_in", [DEPTH, D_MODEL, D_IN]),
    ("gdn_conv", [DEPTH, 1536, 5]), ("gdn_A_log", [DEPTH, 8]), ("gdn_dt_bias", [DEPTH, 8]), ("gdn_norm", [DEPTH, 128]),
    ("gdn_proj", [DEPTH, 512, D_MODEL]), ("mla_q_norm", [DEPTH, 384]), ("mla_w_uq", [DEPTH, 384, 768]),
    ("mla_kv_norm", [DEPTH, 256]), ("mla_w_ukv", [DEPTH, 256, 1024]), ("mla_proj", [DEPTH, 512, D_MODEL]),
    ("w_out", [DEPTH, D_MODEL, D_MODEL]), ("norm_ffn2", [DEPTH, D_MODEL]), ("ffn2_w_gate", [DEPTH, D_MODEL, D_FF]),
    ("ffn2_w_up", [DEPTH, D_MODEL, D_FF]), ("ffn2_w_down", [DEPTH, D_FF, D_MODEL]), ("final_norm", [D_MODEL]),
]


def build(cfg):
    import contextlib
    nc = bass.Bass("TRN2", target_bir_lowering=False)
    K = KB(nc)
    c = Ctx()
    c.nc = nc
    c.K = K
    c.ncores = cfg.get("ncores", 8)
    c.cc_sem = nc.alloc_semaphore("cc_sem")
    c.cc_cnt = 0
    d = {}

    def inp(name, shape, dtype=F32):
        d[name] = nc.dram_tensor(name, shape, dtype, kind="ExternalInput").ap()
        return d[name]

    inp("x", [T, D_MODEL])
    inp("pos_pm", [128, T // 128], I32)
    inp("inv_freq", [16])
    inp("consts", [128, CONST_COLS])
    inp("sel", [2])
    for name, shape in WEIGHT_SPECS:
        inp(name, shape)
    y_out = nc.dram_tensor("y", [T, D_MODEL], F32, kind="ExternalOutput").ap()

    def scratch(name, shape, dtype=F32):
        h = nc.dram_tensor(name, shape, dtype)
        d[name + "_h"] = h
        d[name] = h.ap()

    scratch("xres", [T, D_MODEL])
    scratch("qkvT", [1536, T + 4])
    scratch("zs", [T, 512])
    scratch("sgd", [T, 2048])
    scratch("bg", [T, 16])
    scratch("QT", [8, 128, T], BF16)
    for ch in range(T // min(1024, T)):
        scratch("lat_src%d" % ch, [min(1024, T), 288])
        scratch("lat_all%d" % ch, [2 * min(1024, T), 288])
    scratch("halo_src", [1536, 2])
    scratch("halo_all", [2 * 1536, 2])
    scratch("qkvn", [T, 1536])
    scratch("o1", [T, 512])
    scratch("st_src", [512, 128])
    scratch("st_all", [1024, 128])

    dump = cfg.get("dump", ())
    stop = cfg.get("stop", None)
    dbg = {}

    def dbg_out(name, shape, dtype=F32):
        dbg[name] = nc.dram_tensor("dbg_" + name, shape, dtype, kind="ExternalOutput").ap()
        return dbg[name]

    with contextlib.ExitStack() as stack:
        c.eps_t = sb(nc, stack, "eps_t", [128, 1], F32)
        c.one_t = sb(nc, stack, "one_t", [128, 1], F32)
        K.memset("dve", c.eps_t[:, :], EPS)
        K.memset("dve", c.one_t[:, :], 1.0)
        cst = sb(nc, stack, "cst", [128, CONST_COLS], F32)
        K.dma("sp", cst[:, :], dv(d["consts"]))

        def cview(lo, hi, shape3=None):
            t = TT(cst.h[:, lo:hi] if shape3 is None else cst.h[:, lo:hi].rearrange("p (h e) -> p h e", h=4), "cst")
            t.buf = cst.buf
            return t
        c.ident_f = cview(0, 128)
        c.same = cview(128, 256)
        c.ones_f = cview(256, 384)
        c.tri = [cview(384, 512), cview(512, 640)]
        c.negm4 = [cview(640, 1152), cview(1152, 1664)]
        c.nstr4 = [cview(1664, 2176, True), cview(2176, 2688, True)]
        c.ch = [cview(2688, 2816), cview(2816, 2944)]
        c.rowsel = [cview(2944, 3072), cview(3072, 3200)]
        c.ident_bf = sb(nc, stack, "ident_bf", [128, 128], BF16)
        K.copy("dve", c.ident_bf[:, :], c.ident_f[:, :])
        c.sel = sb(nc, stack, "sel", [128, 2], F32)
        K.dma("sp", c.sel[:, :], dv(d["sel"].partition_broadcast(128)))
        c.ps = []
        c.psb = []
        for i in range(8):
            h = stack.enter_context(nc.psum_tensor("ps%d" % i, [128, 512], F32))
            t = TT(h, "ps%d" % i)
            t.buf.x = True
            c.ps.append(t)
            tb = TT(h[:, :].bitcast(BF16), "psb%d" % i)
            tb.buf = t.buf
            c.psb.append(tb)
        prologue_rope(K, c, stack, d["pos_pm"], d["inv_freq"])

        def run():
            src = d["x"]
            for l in range(DEPTH):
                phase_ffn(K, c, stack, src, d["xres"], d["ffn1_w_gate"][l], d["ffn1_w_up"][l], d["ffn1_w_down"][l],
                          d["norm_ffn1"][l])
                src = d["xres"]
                if stop == ("f1", l):
                    return
                phase_w(K, c, l, d)
                if stop == ("w", l):
                    return
                phase_x1(K, c, l, d)
                if stop == ("x1", l):
                    return
                with contextlib.ExitStack() as lst:
                    aoT = sb(nc, lst, "aoT", [128, T // 128, 512], BF16)
                    if cfg.get("skip_a"):
                        K.memset("dve", aoT[:, :, :], 0.0)
                    else:
                        phase_a(K, c, l, d, aoT)
                        if cfg.get("a_twice"):
                            phase_a(K, c, l, d, aoT)
                    if "aoT" in dump and l == 0:
                        K.dma("sp", dv(dbg_out("aoT", [128, T // 128, 512], BF16)), aoT[:, :, :])
                    if stop == ("a", l):
                        K.barrier()
                        return
                    goT = sb(nc, lst, "goT", [128, 4, T], BF16)
                    S = sb(nc, lst, "Sst", [128, 4, 128], F32)
                    phase_g0(K, c, l, d)
                    if stop == ("g0", l):
                        return
                    K.memset("dve", S[:, :, :], 0.0)
                    phase_gscan(K, c, l, d, 0, S)
                    if "S1" in dump and l == 0:
                        K.dma("sp", dv(dbg_out("S1", [128, 4, 128])), S[:, :, :])
                    if stop == ("g1", l):
                        K.barrier()
                        return
                    phase_x2(K, c, l, d, S)
                    phase_gscan(K, c, l, d, 1, S, goT)
                    if "goT" in dump and l == 0:
                        K.dma("sp", dv(dbg_out("goT", [128, 4, T], BF16)), goT[:, :, :])
                    if stop == ("g2", l):
                        K.barrier()
                        return
                    phase_m(K, c, l, d, aoT, goT)
                if stop == ("m", l):
                    return
                last = (l == DEPTH - 1)
                phase_ffn(K, c, stack, d["xres"], d["xres"], d["ffn2_w_gate"][l], d["ffn2_w_up"][l], d["ffn2_w_down"][l],
                          d["norm_ffn2"][l], final_gain=d["final_norm"] if last else None, y_out=y_out if last else None)
                if stop == ("f2", l):
                    return

        run()
        K.barrier()
        for name in dump:
            if name in ("aoT", "goT", "S1"):
                continue
            src_ap = d[name]
            K.dma("sp", dv(dbg_out(name, list(src_ap.shape), src_ap.dtype)), dv(src_ap))
        K.barrier()
    print("instructions:", K.ninst, dict(K.ecnt))
    return nc


def shard_inputs(inputs, ncores=8):
    maps = []
    consts = make_consts()
    inv_freq = np.power(np.float32(10000.0), -np.arange(0, 32, 2, dtype=np.float32) / np.float32(32)).astype(np.float32)
    w = {k: np.asarray(inputs[k], dtype=np.float32) for k, _ in WEIGHT_SPECS}
    w["gdn_conv"] = np.ascontiguousarray(np.transpose(w["gdn_conv"], (0, 2, 1)))
    w["gdn_A_log"] = w["gdn_A_log"].reshape(DEPTH, 8)
    w["gdn_dt_bias"] = w["gdn_dt_bias"].reshape(DEPTH, 8)
    wr = dict(w)
    wi = w["w_in"].copy()
    for base in (2048, 2056):
        wi[:, :, base:base + 4] = w["w_in"][:, :, base + 4:base + 8]
        wi[:, :, base + 4:base + 8] = w["w_in"][:, :, base:base + 4]
    wr["w_in"] = wi
    wr["gdn_A_log"] = np.ascontiguousarray(w["gdn_A_log"].reshape(DEPTH, 2, 4)[:, ::-1].reshape(DEPTH, 8))
    wr["gdn_dt_bias"] = np.ascontiguousarray(w["gdn_dt_bias"].reshape(DEPTH, 2, 4)[:, ::-1].reshape(DEPTH, 8))
    wr["gdn_conv"] = np.ascontiguousarray(w["gdn_conv"][:, :, ::-1])
    S_ = inputs["x"].shape[1]
    Tl = S_ // 2
    for core in range(ncores):
        b, p = core // 2, core % 2
        xs = inputs["x"][b, p * Tl:(p + 1) * Tl]
        ps = inputs["positions"][b, p * Tl:(p + 1) * Tl]
        if p == 1:
            xs = xs[::-1]
            ps = ps[::-1]
        m = dict(w if p == 0 else wr)
        m["x"] = np.ascontiguousarray(xs, dtype=np.float32)
        m["pos_pm"] = np.ascontiguousarray(np.asarray(ps, dtype=np.int32).reshape(Tl // 128, 128).T)
        m["inv_freq"] = inv_freq
        m["consts"] = consts
        m["sel"] = np.array([0.0, 1.0] if p == 0 else [1.0, 0.0], np.float32)
        maps.append(m)
    return maps


def kernel(**inputs):
    inputs = {k: np.asarray(v) for k, v in inputs.items()}
    nc = build({})
    maps = shard_inputs(inputs)
    res = run_bass_kernel_spmd(nc, maps, core_ids=list(range(8)))
    out = np.empty((BATCH, SEQ, D_MODEL), np.float32)
    for core in range(8):
        b, p = core // 2, core % 2
        y = res.results[core]["y"]
        if p == 1:
            y = y[::-1]
        out[b, p * T:(p + 1) * T] = y
    return out
```
